# Optimizing a Trainium2 kernel written in Bass

```python
import math
import jax, jax.numpy as jnp
from jax import lax
import numpy as np

D_MODEL = 1024
BATCH = 8
SEQ = 4096
DEPTH = 4

GRID_W = 64
CTX_LEN = 256
HEAD_DIM = 64
NORM_EPS = 1e-6
ATT_HEADS = 4
ATT_KV_HEADS = 2
ATT_GROUP = ATT_HEADS // ATT_KV_HEADS
ATT_WIDTH = ATT_HEADS * HEAD_DIM
WINDOW = 128
ATT_BLOCK = 128
ROPE_BASE = 10000.0
SSD_HEADS = 6
SSD_HEAD_DIM = 64
SSD_GROUPS = 2
SSD_STATE = 128
SSD_CHUNK = 128
SSD_CONV_W = 5
SSD_D_INNER = SSD_HEADS * SSD_HEAD_DIM
SSD_CONV_CH = SSD_D_INNER + 2 * SSD_GROUPS * SSD_STATE
DN_HEADS = 6
DN_HEAD_K = 64
DN_HEAD_V = 64
DN_CHUNK = 64
DN_CONV_W = 5
DN_D_K = DN_HEADS * DN_HEAD_K
DN_D_V = DN_HEADS * DN_HEAD_V
DN_CONV_CH = 2 * DN_D_K + DN_D_V
D_MIX = ATT_WIDTH + SSD_D_INNER + DN_D_V
D_FF = 4 * D_MODEL
IN_SIZES = (ATT_WIDTH, ATT_KV_HEADS * HEAD_DIM, ATT_KV_HEADS * HEAD_DIM,
            SSD_CONV_CH, SSD_D_INNER, 2 * SSD_HEADS,
            DN_CONV_CH, DN_D_V, 2 * DN_HEADS, 2 * DN_HEADS)
D_IN = sum(IN_SIZES)

kernel_name = 'hymba_ssd_swa_gdn_prefix_dit'


def _normalize(x):
    xf = x.astype(jnp.float32)
    return xf * lax.rsqrt(jnp.mean(xf * xf, axis=-1, keepdims=True) + NORM_EPS)


def rms_norm(x, g):
    return (_normalize(x) * g).astype(x.dtype)


def l2norm(t):
    tf = t.astype(jnp.float32)
    return tf * lax.rsqrt(jnp.sum(tf * tf, axis=-1, keepdims=True) + NORM_EPS)


def modulate(h, shift, scale):
    return h * (1.0 + scale) + shift


def split_cols(t, sizes):
    idx = np.cumsum(np.array(sizes))[:-1].tolist()
    return jnp.split(t, idx, axis=-1)


def depthwise_conv(x, w):
    width = w.shape[1]
    kern = jnp.transpose(w).reshape(width, 1, w.shape[0]).astype(x.dtype)
    return lax.conv_general_dilated(x, kern, window_strides=(1,),
                                    padding=((width // 2, width // 2),),
                                    dimension_numbers=('NWC', 'WIO', 'NWC'),
                                    feature_group_count=x.shape[-1])


def to_chunks(t, size):
    B_, L = t.shape[:2]
    t = t.reshape((B_, L // size, size) + t.shape[2:])
    return jnp.moveaxis(jnp.moveaxis(t, 1, 0), 3, 2)


def from_chunks(t):
    t = jnp.moveaxis(jnp.moveaxis(t, 2, 3), 0, 1)
    return t.reshape((t.shape[0], t.shape[1] * t.shape[2]) + t.shape[3:])


def rope_2d(t):
    L = t.shape[1]
    rows = L // GRID_W
    row_id = jnp.repeat(jnp.arange(rows), GRID_W).astype(jnp.float32)
    col_id = jnp.tile(jnp.arange(GRID_W), rows).astype(jnp.float32)
    half = HEAD_DIM // 2
    inv_freq = ROPE_BASE ** (-jnp.arange(0, half, 2, dtype=jnp.float32) / half)

    def rot(u, pos):
        ang = pos[:, None] * inv_freq[None, :]
        cos = jnp.cos(ang)[None, :, None, :]
        sin = jnp.sin(ang)[None, :, None, :]
        u1, u2 = jnp.split(u.astype(jnp.float32), 2, axis=-1)
        return jnp.concatenate([u1 * cos - u2 * sin, u2 * cos + u1 * sin], axis=-1)

    out = jnp.concatenate([rot(t[..., :half], row_id), rot(t[..., half:], col_id)], axis=-1)
    return out.astype(t.dtype)


def softmax_with_sink(scores, sink):
    logits = jnp.concatenate([scores, jnp.broadcast_to(sink, scores.shape[:-1] + (1,))], axis=-1)
    return jax.nn.softmax(logits, axis=-1)[..., :-1]


def window_gqa_attention(q_c, k_c, v_c, q_l, k_l, v_l, sink, need_ctx):
    B_, L = q_l.shape[:2]
    scale = HEAD_DIM ** -0.5
    sink = sink.astype(jnp.float32).reshape(ATT_KV_HEADS, ATT_GROUP, 1, 1)
    o_c = None
    if need_ctx:
        Lc = q_c.shape[1]
        qg = q_c.reshape(B_, Lc, ATT_KV_HEADS, ATT_GROUP, HEAD_DIM)
        s = jnp.einsum('bqkgd,bskd->bkgqs', qg, k_c).astype(jnp.float32) * scale
        p = softmax_with_sink(s, sink).astype(v_c.dtype)
        o_c = jnp.einsum('bkgqs,bskd->bqkgd', p, v_c).reshape(B_, Lc, ATT_WIDTH)
    q_l = rope_2d(q_l)
    k_l = rope_2d(k_l)
    nb = L // ATT_BLOCK
    qb = q_l.reshape(B_, nb, ATT_BLOCK, ATT_KV_HEADS, ATT_GROUP, HEAD_DIM)

    def band(t):
        tp = jnp.pad(t, ((0, 0), (ATT_BLOCK, ATT_BLOCK), (0, 0), (0, 0)))
        tp = tp.reshape(B_, nb + 2, ATT_BLOCK, ATT_KV_HEADS, HEAD_DIM)
        return jnp.concatenate([tp[:, :-2], tp[:, 1:-1], tp[:, 2:]], axis=2)

    kw, vw = band(k_l), band(v_l)
    qpos = jnp.arange(nb)[:, None] * ATT_BLOCK + jnp.arange(ATT_BLOCK)[None, :]
    kpos = (jnp.arange(nb)[:, None] - 1) * ATT_BLOCK + jnp.arange(3 * ATT_BLOCK)[None, :]
    mask = ((jnp.abs(qpos[:, :, None] - kpos[:, None, :]) <= WINDOW)
            & (kpos[:, None, :] >= 0) & (kpos[:, None, :] < L))
    s_loc = jnp.einsum('bnqkgd,bnskd->bnkgqs', qb, kw).astype(jnp.float32) * scale
    s_loc = jnp.where(mask[None, :, None, None], s_loc, -jnp.inf)
    s_ctx = jnp.einsum('bnqkgd,bckd->bnkgqc', qb, k_c).astype(jnp.float32) * scale
    p = softmax_with_sink(jnp.concatenate([s_loc, s_ctx], axis=-1), sink).astype(v_l.dtype)
    p_loc, p_ctx = p[..., :3 * ATT_BLOCK], p[..., 3 * ATT_BLOCK:]
    o_l = (jnp.einsum('bnkgqs,bnskd->bnqkgd', p_loc, vw)
           + jnp.einsum('bnkgqc,bckd->bnqkgd', p_ctx, v_c))
    return o_c, o_l.reshape(B_, L, ATT_WIDTH)


def ssd_scan(x, dt, Bm, Cm, A, h0):
    f32 = jnp.float32
    xc = to_chunks(x.astype(f32), SSD_CHUNK)
    Bc = to_chunks(Bm.astype(f32), SSD_CHUNK)
    Cc = to_chunks(Cm.astype(f32), SSD_CHUNK)
    dtc = to_chunks(dt.astype(f32), SSD_CHUNK)
    acum = jnp.cumsum(dtc * A[:, None], axis=-1)
    lower = jnp.tril(jnp.ones((SSD_CHUNK, SSD_CHUNK), dtype=bool))
    decay = jnp.exp(jnp.where(lower, acum[..., :, None] - acum[..., None, :], -jnp.inf))
    scores = jnp.einsum('cbhin,cbhjn->cbhij', Cc, Bc) * decay * dtc[..., None, :]
    y_diag = jnp.einsum('cbhij,cbhjp->cbhip', scores, xc)
    a_last = acum[..., -1:]
    chunk_states = jnp.einsum('cbhj,cbhjn,cbhjp->cbhpn', jnp.exp(a_last - acum) * dtc, Bc, xc)

    def step(h, inp):
        dec, s = inp
        return dec[..., None, None] * h + s, h

    h_final, h_enter = lax.scan(step, h0, (jnp.exp(a_last[..., 0]), chunk_states))
    y_off = jnp.einsum('cbhin,cbhpn->cbhip', Cc, h_enter) * jnp.exp(acum)[..., None]
    return from_chunks(y_diag + y_off), h_final


def gdn_scan(q, k, v, g, beta, S0):
    f32 = jnp.float32
    qc = to_chunks(q.astype(f32), DN_CHUNK)
    kc = to_chunks(k.astype(f32), DN_CHUNK)
    vc = to_chunks(v.astype(f32), DN_CHUNK)
    gc = jnp.cumsum(to_chunks(g.astype(f32), DN_CHUNK), axis=-1)
    bc = to_chunks(beta.astype(f32), DN_CHUNK)
    incl = jnp.tril(jnp.ones((DN_CHUNK, DN_CHUNK), dtype=bool))
    strict = jnp.tril(jnp.ones((DN_CHUNK, DN_CHUNK), dtype=bool), k=-1)
    decay = jnp.exp(jnp.where(incl, gc[..., :, None] - gc[..., None, :], -jnp.inf))
    kb = kc * bc[..., None]
    a_mat = jnp.where(strict, jnp.einsum('cbhid,cbhjd->cbhij', kb, kc) * decay, 0.0)
    rhs = jnp.concatenate([vc * bc[..., None], kb * jnp.exp(gc)[..., None]], axis=-1)
    sol = lax.linalg.triangular_solve(a_mat, rhs, left_side=True, lower=True, unit_diagonal=True)
    u, w = sol[..., :DN_HEAD_V], sol[..., DN_HEAD_V:]
    attn_qk = jnp.einsum('cbhid,cbhjd->cbhij', qc, kc) * decay
    g_last = gc[..., -1]
    q_dec = qc * jnp.exp(gc)[..., None]
    k_dec = kc * jnp.exp(g_last[..., None] - gc)[..., None]

    def step(S, inp):
        u_i, w_i, qd_i, kd_i, a_i, gl_i = inp
        v_new = u_i - jnp.einsum('bhid,bhdv->bhiv', w_i, S)
        o = jnp.einsum('bhid,bhdv->bhiv', qd_i, S) + jnp.einsum('bhij,bhjv->bhiv', a_i, v_new)
        S = S * jnp.exp(gl_i)[..., None, None] + jnp.einsum('bhid,bhiv->bhdv', kd_i, v_new)
        return S, o

    S_final, o = lax.scan(step, S0, (u, w, q_dec, k_dec, attn_qk, g_last))
    return from_chunks(o), S_final


def run_direction(scan_fn, ctx_args, lat_args, dir_params, h0, reverse):
    flip = (lambda t: jnp.flip(t, axis=1)) if reverse else (lambda t: t)
    y_c, h_c = scan_fn(*[flip(a) for a in ctx_args], *dir_params, h0)
    y_l, _ = scan_fn(*[flip(a) for a in lat_args], *dir_params, h_c)
    return flip(y_c), flip(y_l)


def ssd_mixer(xbc_c, z_c, dtr_c, xbc_l, z_l, dtr_l, conv_w, conv_b, A_log, dt_bias, d_skip, norm_g):
    f32 = jnp.float32

    def prep(xbc, dt_raw):
        B_, L = xbc.shape[:2]
        xbc = jax.nn.silu(depthwise_conv(xbc, conv_w) + conv_b)
        xs, Bm, Cm = split_cols(xbc, (SSD_D_INNER, SSD_GROUPS * SSD_STATE, SSD_GROUPS * SSD_STATE))
        rep = SSD_HEADS // SSD_GROUPS
        xs = xs.reshape(B_, L, SSD_HEADS, SSD_HEAD_DIM)
        Bm = jnp.repeat(Bm.reshape(B_, L, SSD_GROUPS, SSD_STATE), rep, axis=2)
        Cm = jnp.repeat(Cm.reshape(B_, L, SSD_GROUPS, SSD_STATE), rep, axis=2)
        dt = jax.nn.softplus(dt_raw.astype(f32).reshape(B_, L, 2, SSD_HEADS) + dt_bias.astype(f32))
        return xs, Bm, Cm, dt

    xs_c, B_c, C_c, dt_c = prep(xbc_c, dtr_c)
    xs_l, B_l, C_l, dt_l = prep(xbc_l, dtr_l)
    A = -jnp.exp(A_log.astype(f32))
    h0 = jnp.zeros((xs_c.shape[0], SSD_HEADS, SSD_HEAD_DIM, SSD_STATE), f32)
    skip = d_skip.astype(f32)[:, None]
    y_c = skip * xs_c.astype(f32)
    y_l = skip * xs_l.astype(f32)
    for d in range(2):
        yc_d, yl_d = run_direction(ssd_scan, (xs_c, dt_c[:, :, d], B_c, C_c),
                                   (xs_l, dt_l[:, :, d], B_l, C_l), (A[d],), h0, reverse=(d == 1))
        y_c = y_c + yc_d
        y_l = y_l + yl_d

    def out(y, z):
        B_, L = y.shape[:2]
        gated = y.reshape(B_, L, SSD_D_INNER) * jax.nn.silu(z.astype(f32))
        gated = _normalize(gated.reshape(B_, L, SSD_GROUPS, SSD_D_INNER // SSD_GROUPS))
        return (gated.reshape(B_, L, SSD_D_INNER) * norm_g).astype(z.dtype)

    return out(y_c, z_c), out(y_l, z_l)


def gdn_mixer(qkv_c, z_c, ar_c, br_c, qkv_l, z_l, ar_l, br_l, conv_w, A_log, dt_bias, norm_g):
    f32 = jnp.float32

    def prep(qkv, a_raw, b_raw):
        B_, L = qkv.shape[:2]
        qkv = jax.nn.silu(depthwise_conv(qkv, conv_w))
        q, k, v = split_cols(qkv, (DN_D_K, DN_D_K, DN_D_V))
        q = l2norm(q.reshape(B_, L, DN_HEADS, DN_HEAD_K)) * DN_HEAD_K ** -0.5
        k = l2norm(k.reshape(B_, L, DN_HEADS, DN_HEAD_K))
        v = v.reshape(B_, L, DN_HEADS, DN_HEAD_V)
        beta = jax.nn.sigmoid(b_raw.astype(f32).reshape(B_, L, 2, DN_HEADS))
        g = -jnp.exp(A_log.astype(f32)) * jax.nn.softplus(
            a_raw.astype(f32).reshape(B_, L, 2, DN_HEADS) + dt_bias.astype(f32))
        return q, k, v, g, beta

    q_c, k_c, v_c, g_c, b_c = prep(qkv_c, ar_c, br_c)
    q_l, k_l, v_l, g_l, b_l = prep(qkv_l, ar_l, br_l)
    S0 = jnp.zeros((q_c.shape[0], DN_HEADS, DN_HEAD_K, DN_HEAD_V), f32)
    o_c, o_l = 0.0, 0.0
    for d in range(2):
        oc_d, ol_d = run_direction(gdn_scan, (q_c, k_c, v_c, g_c[:, :, d], b_c[:, :, d]),
                                   (q_l, k_l, v_l, g_l[:, :, d], b_l[:, :, d]), (), S0, reverse=(d == 1))
        o_c = o_c + oc_d
        o_l = o_l + ol_d

    def out(o, z):
        B_, L = o.shape[:2]
        zh = z.astype(f32).reshape(B_, L, DN_HEADS, DN_HEAD_V)
        y = _normalize(o) * norm_g * jax.nn.silu(zh)
        return y.reshape(B_, L, DN_D_V).astype(z.dtype)

    return out(o_c, z_c), out(o_l, z_l)


def token_mixer(h_c, h_l, w_in, w_out, sink, ssd_conv_w, ssd_conv_b, ssd_A_log, ssd_dt_bias, ssd_D,
                ssd_norm_g, dn_conv_w, dn_A_log, dn_dt_bias, dn_norm_g, need_ctx):
    (aq_c, ak_c, av_c, xbc_c, zs_c, dtr_c, qkv_c, zd_c, ar_c, br_c) = split_cols(h_c @ w_in, IN_SIZES)
    (aq_l, ak_l, av_l, xbc_l, zs_l, dtr_l, qkv_l, zd_l, ar_l, br_l) = split_cols(h_l @ w_in, IN_SIZES)

    def heads(t, n):
        return t.reshape(t.shape[:2] + (n, HEAD_DIM))

    att_c, att_l = window_gqa_attention(
        heads(aq_c, ATT_HEADS), heads(ak_c, ATT_KV_HEADS), heads(av_c, ATT_KV_HEADS),
        heads(aq_l, ATT_HEADS), heads(ak_l, ATT_KV_HEADS), heads(av_l, ATT_KV_HEADS), sink, need_ctx)
    ssd_c, ssd_l = ssd_mixer(xbc_c, zs_c, dtr_c, xbc_l, zs_l, dtr_l, ssd_conv_w, ssd_conv_b,
                             ssd_A_log, ssd_dt_bias, ssd_D, ssd_norm_g)
    dn_c, dn_l = gdn_mixer(qkv_c, zd_c, ar_c, br_c, qkv_l, zd_l, ar_l, br_l, dn_conv_w,
                           dn_A_log, dn_dt_bias, dn_norm_g)
    y_l = jnp.concatenate([att_l, ssd_l, dn_l], axis=-1) @ w_out
    y_c = jnp.concatenate([att_c, ssd_c, dn_c], axis=-1) @ w_out if need_ctx else None
    return y_c, y_l


def sq_relu_mlp(h, w1, w2):
    return jnp.square(jax.nn.relu(h @ w1)) @ w2


def setup_inputs(seed: int = 0) -> dict:
    key = jax.random.key(seed)
    ks = jax.random.split(key, 22)
    f32 = jnp.float32

    def nrm(k, shape, s):
        return jax.random.normal(k, shape, f32) * s

    def a_log_init(k, shape):
        return jnp.log(jax.random.uniform(k, shape, f32, 1.0, 16.0))

    def dt_bias_init(k, shape):
        dt = jnp.exp(jax.random.uniform(k, shape, f32, math.log(1e-3), math.log(1e-1)))
        return dt + jnp.log(-jnp.expm1(-dt))

    return {
        'x': nrm(ks[0], (BATCH, SEQ, D_MODEL), 1.0),
        'c': nrm(ks[1], (BATCH, D_MODEL), 1.0),
        'ctx': nrm(ks[2], (BATCH, CTX_LEN, D_MODEL), 1.0),
        'c_ctx': nrm(ks[3], (D_MODEL,), 1.0),
        'w_ada': nrm(ks[4], (DEPTH, D_MODEL, 6 * D_MODEL), 0.5 * D_MODEL ** -0.5),
        'b_ada': nrm(ks[5], (DEPTH, 6 * D_MODEL), 0.01),
        'norm_g': 1.0 + nrm(ks[6], (DEPTH, 4, D_MODEL), 0.05),
        'w_in': nrm(ks[7], (DEPTH, D_MODEL, D_IN), D_MODEL ** -0.5),
        'w_out': nrm(ks[8], (DEPTH, D_MIX, D_MODEL), D_MIX ** -0.5),
        'attn_sink': nrm(ks[9], (DEPTH, ATT_HEADS), 0.5),
        'ssd_conv_w': nrm(ks[10], (DEPTH, SSD_CONV_CH, SSD_CONV_W), SSD_CONV_W ** -0.5),
        'ssd_conv_b': nrm(ks[11], (DEPTH, SSD_CONV_CH), 0.01),
        'ssd_A_log': a_log_init(ks[12], (DEPTH, 2, SSD_HEADS)),
        'ssd_dt_bias': dt_bias_init(ks[13], (DEPTH, 2, SSD_HEADS)),
        'ssd_D': 1.0 + nrm(ks[14], (DEPTH, SSD_HEADS), 0.1),
        'ssd_norm_g': 1.0 + nrm(ks[15], (DEPTH, SSD_D_INNER), 0.05),
        'dn_conv_w': nrm(ks[16], (DEPTH, DN_CONV_CH, DN_CONV_W), DN_CONV_W ** -0.5),
        'dn_A_log': a_log_init(ks[17], (DEPTH, 2, DN_HEADS)),
        'dn_dt_bias': dt_bias_init(ks[18], (DEPTH, 2, DN_HEADS)),
        'dn_norm_g': 1.0 + nrm(ks[19], (DEPTH, DN_HEAD_V), 0.05),
        'w_mlp1': nrm(ks[20], (DEPTH, D_MODEL, D_FF), D_MODEL ** -0.5),
        'w_mlp2': nrm(ks[21], (DEPTH, D_FF, D_MODEL), D_FF ** -0.5),
    }


def reference(x, c, ctx, c_ctx, w_ada, b_ada, norm_g, w_in, w_out, attn_sink,
              ssd_conv_w, ssd_conv_b, ssd_A_log, ssd_dt_bias, ssd_D, ssd_norm_g,
              dn_conv_w, dn_A_log, dn_dt_bias, dn_norm_g, w_mlp1, w_mlp2):
    h_lat, h_ctx = x, ctx
    silu_c = jax.nn.silu(c)[:, None, :]
    silu_cc = jax.nn.silu(c_ctx)[None, None, :]
    for layer in range(DEPTH):
        need_ctx = layer < DEPTH - 1
        shift_a, scale_a, gate_a, shift_m, scale_m, gate_m = jnp.split(
            silu_c @ w_ada[layer] + b_ada[layer], 6, axis=-1)
        cshift_a, cscale_a, cgate_a, cshift_m, cscale_m, cgate_m = jnp.split(
            silu_cc @ w_ada[layer] + b_ada[layer], 6, axis=-1)
        a_lat = modulate(rms_norm(h_lat, norm_g[layer, 0]), shift_a, scale_a)
        a_ctx = modulate(rms_norm(h_ctx, norm_g[layer, 0]), cshift_a, cscale_a)
        y_ctx, y_lat = token_mixer(a_ctx, a_lat, w_in[layer], w_out[layer], attn_sink[layer],
                                   ssd_conv_w[layer], ssd_conv_b[layer], ssd_A_log[layer],
                                   ssd_dt_bias[layer], ssd_D[layer], ssd_norm_g[layer],
                                   dn_conv_w[layer], dn_A_log[layer], dn_dt_bias[layer],
                                   dn_norm_g[layer], need_ctx)
        h_lat = h_lat + gate_a * rms_norm(y_lat, norm_g[layer, 1])
        m_lat = modulate(rms_norm(h_lat, norm_g[layer, 2]), shift_m, scale_m)
        h_lat = h_lat + gate_m * rms_norm(sq_relu_mlp(m_lat, w_mlp1[layer], w_mlp2[layer]), norm_g[layer, 3])
        if need_ctx:
            h_ctx = h_ctx + cgate_a * rms_norm(y_ctx, norm_g[layer, 1])
            m_ctx = modulate(rms_norm(h_ctx, norm_g[layer, 2]), cshift_m, cscale_m)
            h_ctx = h_ctx + cgate_m * rms_norm(sq_relu_mlp(m_ctx, w_mlp1[layer], w_mlp2[layer]), norm_g[layer, 3])
    return h_lat
```

```python
import contextlib
import math
import numpy as np
import concourse.bass as bass
import concourse.mybir as mybir
from concourse.bass_utils import run_bass_kernel_spmd

F32 = mybir.dt.float32
BF16 = mybir.dt.bfloat16
AF = mybir.ActivationFunctionType
ALU = mybir.AluOpType
AX = mybir.AxisListType

import itertools
_UID = itertools.count()
EPOCH = 30000
NEPOCH = 10
D = 1024
LCTX = 256
LLAT = 4096
NT = LCTX + LLAT
NTILE = NT // 128
NCH = 30
NEG = -30000.0
EPS = 1e-6


class Bld:
    CE = ("pe", "act", "dve", "pool")

    def __init__(self, nc, stack):
        self.nc = nc
        self.engs = {"pe": nc.tensor, "act": nc.scalar, "dve": nc.vector, "pool": nc.gpsimd, "sp": nc.sync}
        self.cnt = {e: 0 for e in self.CE}
        self.sems = {e: [stack.enter_context(nc.semaphore(f"s_{e}_{k}")) for k in range(NEPOCH)] for e in self.CE}
        self.dq = {}
        for q, n in (("sp", 16), ("pool", 6)):
            self.dq[q] = {"sems": [stack.enter_context(nc.semaphore(f"d_{q}_{k}")) for k in range(n)],
                          "val": [0] * n, "next": 0}
        self.waited = {}
        self.dwaited = {}
        self.res = {}
        self.pe_rg = {}

    def _wait(self, eng, tok, rg=0):
        E = self.engs[eng]
        if tok[0] == "e":
            _, e2, c2 = tok
            if e2 == eng and eng == "pe" and self.pe_rg.get(c2, 0) == rg:
                return
            if self.waited.get((eng, e2), 0) >= c2:
                return
            ep, v = (c2 - 1) // EPOCH, (c2 - 1) % EPOCH + 1
            E.wait_ge(self.sems[e2][ep], v)
            self.waited[(eng, e2)] = c2
        else:
            _, q, idx, v = tok
            if self.dwaited.get((eng, q, idx), 0) >= v:
                return
            E.wait_ge(self.dq[q]["sems"][idx], v)
            self.dwaited[(eng, q, idx)] = v

    def _sync(self, eng, reads, writes, rg=0):
        toks = []
        for k in reads:
            r = self.res.get(k)
            if r and r["w"] is not None:
                toks.append(r["w"])
        for k in writes:
            r = self.res.get(k)
            if r:
                if r["w"] is not None:
                    toks.append(r["w"])
                toks.extend(r["r"])
        for t in toks:
            self._wait(eng, t, rg)

    def _mark(self, tok, reads, writes):
        for k in reads:
            r = self.res.setdefault(k, {"w": None, "r": []})
            r["r"].append(tok)
            if len(r["r"]) > 48:
                r["r"] = self._prune(r["r"])
        for k in writes:
            self.res[k] = {"w": tok, "r": []}

    @staticmethod
    def _prune(toks):
        best = {}
        for t in toks:
            key = (t[0], t[1]) if t[0] == "e" else (t[0], t[1], t[2])
            v = t[2] if t[0] == "e" else t[3]
            old = best.get(key)
            if old is None or v > (old[2] if old[0] == "e" else old[3]):
                best[key] = t
        return list(best.values())

    def op(self, eng, fn, R=(), W=(), rg=0):
        self._sync(eng, R, W, rg)
        inst = fn(self.engs[eng])
        self.cnt[eng] += 1
        c = self.cnt[eng]
        if eng == "pe" and rg:
            self.pe_rg[c] = rg
        inst.then_inc(self.sems[eng][(c - 1) // EPOCH], 1)
        self._mark(("e", eng, c), R, W)
        return inst

    def dma(self, q, out, in_, R=(), W=(), **kw):
        self._sync(q, R, W)
        d = self.dq[q]
        idx = d["next"]
        d["next"] = (idx + 1) % len(d["sems"])
        if d["val"][idx] > 0:
            self._wait(q, ("d", q, idx, d["val"][idx]))
        inst = self.engs[q].dma_start(out=out, in_=in_, **kw)
        d["val"][idx] += 16
        inst.then_inc(d["sems"][idx], 16)
        tok = ("d", q, idx, d["val"][idx])
        self._mark(tok, R, W)
        return tok

    def barrier(self):
        for eng in ("pe", "act", "dve", "pool", "sp"):
            for e2 in self.CE:
                if self.cnt[e2] > 0:
                    self._wait(eng, ("e", e2, self.cnt[e2]))
            for q, d in self.dq.items():
                for idx, v in enumerate(d["val"]):
                    if v > 0:
                        self._wait(eng, ("d", q, idx, v))
        self.res = {}


def _consts():
    c = {}
    i = np.arange(128)
    c["ident"] = np.eye(128, dtype=np.float32)
    c["ones"] = np.ones((128, 128), np.float32)
    blk = (i[:, None] // 64) == (i[None, :] // 64)
    tri_f = (i[:, None] <= i[None, :])
    tri_b = (i[:, None] >= i[None, :])
    c["tri"] = np.stack([tri_f, tri_b]).astype(np.float32)
    c["tris"] = np.stack([i[:, None] > i[None, :], i[:, None] < i[None, :]]).astype(np.float32)
    c["onesblk"] = blk.astype(np.float32)
    c["triblk"] = np.stack([tri_f & blk, tri_b & blk]).astype(np.float32)
    c["trisblk"] = np.stack([(i[:, None] > i[None, :]) & blk, (i[:, None] < i[None, :]) & blk]).astype(np.float32)
    def m(al):
        return np.where(al, 0.0, NEG).astype(np.float32)
    ssd_m = np.stack([m(i[None, :] >= i[:, None]), m(i[None, :] <= i[:, None])])
    c["ssd_negh"] = np.repeat(ssd_m[:, :, None, :], 6, axis=2).transpose(1, 0, 2, 3).reshape(128, 12, 128).copy()
    g_dt = np.stack([m((i[None, :] >= i[:, None]) & blk), m((i[None, :] <= i[:, None]) & blk)])
    g_d = np.stack([m((i[:, None] > i[None, :]) & blk), m((i[:, None] < i[None, :]) & blk)])
    c["gdn_negdt"] = np.repeat(g_dt[:, :, None, :], 6, axis=2).copy()
    c["gdn_negd"] = np.repeat(g_d[:, :, None, :], 6, axis=2).copy()
    chs = np.zeros((2, 128, 64), np.float32)
    chs[0, :64] = 1.0
    chs[1, 64:] = 1.0
    c["chsel"] = chs
    am = np.stack([m(i[:, None] >= i[None, :]), m(i[:, None] <= i[None, :])])
    c["att_neg"] = np.repeat(am[:, :, None, :], 2, axis=2).copy()
    t = np.arange(LLAT)
    row = (t // 64).astype(np.float64)
    col = (t % 64).astype(np.float64)
    inv = 10000.0 ** (-np.arange(0, 32, 2, dtype=np.float64) / 32.0)
    cos = np.zeros((64, LLAT)); sin = np.zeros((64, LLAT))
    for d in range(64):
        pos = row if d < 32 else col
        idx = d % 32
        ang = pos * inv[idx % 16]
        cos[d] = np.cos(ang)
        sin[d] = -np.sin(ang) if idx < 16 else np.sin(ang)
    c["rcos"] = np.concatenate([cos, cos]).astype(np.float32)
    c["rsin"] = np.concatenate([sin, sin]).astype(np.float32)
    return c


def _win_perm():
    def partner(d):
        return d + 16 if (d % 32) < 16 else d - 16
    cols = []
    aq = lambda h, d: h * 64 + d
    for hs in ((0, 2), (1, 3)):
        cols += [aq(h, d) for h in hs for d in range(64)]
    for hs in ((0, 2), (1, 3)):
        cols += [aq(h, partner(d)) for h in hs for d in range(64)]
    cols += [256 + kv * 64 + d for kv in range(2) for d in range(64)]
    cols += [256 + kv * 64 + partner(d) for kv in range(2) for d in range(64)]
    cols += list(range(384, 512))
    cols += list(range(512, 1408))
    cols += list(range(1408, 1792))
    cols += list(range(1804, 2956))
    cols += list(range(2956, 3340))
    small = list(range(1792, 1804)) + list(range(3340, 3352)) + list(range(3352, 3364))
    cols += small + [-1] * (128 - len(small))
    assert len(cols) == NCH * 128
    return np.array(cols)


CH_QA, CH_QB, CH_QAP, CH_QBP, CH_K, CH_KP, CH_V = 0, 1, 2, 3, 4, 5, 6
CH_SX, CH_SB, CH_SC, CH_SZ = 7, 10, 12, 14
CH_GQ, CH_GK, CH_GV, CH_GZ = 17, 20, 23, 26
CH_SM = 29


def build(depth=4, dbg=False):
    nc = bass.Bass("TRN2", target_bir_lowering=False)
    C = _consts()
    dt_in = lambda name, shape: nc.dram_tensor(name, list(shape), F32, kind="ExternalInput").ap()
    x_d = dt_in("x", (LLAT, D)); ctx_d = dt_in("ctx", (LCTX, D)); cv_d = dt_in("cvec", (2, D))
    wada_d = dt_in("w_ada", (4, D, 6 * D)); bada_d = dt_in("b_ada", (4, 6 * D)); ng_d = dt_in("norm_g", (4, 4, D))
    win_d = dt_in("w_in", (4, D, NCH * 128)); wout_d = dt_in("w_out", (4, D, D))
    sink_d = dt_in("attn_sink", (4, 4))
    scw_d = dt_in("ssd_conv_w", (4, 896, 5)); scb_d = dt_in("ssd_conv_b", (4, 896))
    sal_d = dt_in("ssd_A_log", (4, 12)); sdb_d = dt_in("ssd_dt_bias", (4, 12)); sD_d = dt_in("ssd_D", (4, 6))
    sng_d = dt_in("ssd_norm_g", (4, 384))
    gcw_d = dt_in("dn_conv_w", (4, 1152, 5)); gal_d = dt_in("dn_A_log", (4, 12)); gdb_d = dt_in("dn_dt_bias", (4, 12))
    gng_d = dt_in("dn_norm_g", (4, 64))
    w1_d = dt_in("w_mlp1", (4, D, 4 * D)); w2_d = dt_in("w_mlp2", (4, 4 * D, D))
    cd = {k: dt_in("c_" + k, v.shape) for k, v in C.items()}
    out_d = nc.dram_tensor("out", [LLAT, D], F32, kind="ExternalOutput").ap()
    skind = "ExternalOutput" if dbg else "Internal"
    HT = nc.dram_tensor("HT", [D, NT], F32, kind=skind).ap()
    PT = nc.dram_tensor("PT", [NCH * 128, NT], F32, kind=skind).ap()
    MIXT = nc.dram_tensor("MIXT", [D, NT], F32, kind=skind).ap()
    GDTOK = nc.dram_tensor("GDTOK", [NTILE, 128, 1152], F32, kind="Internal").ap()

    with contextlib.ExitStack() as st:
        b = Bld(nc, st)
        pball = st.enter_context(nc.psum_tensor("pball", [128, 8, 512], F32))
        PB = lambda k: "pb%d" % k
        gT = lambda name, shape, dt=F32: st.enter_context(nc.sbuf_tensor(name + "_%d" % next(_UID), list(shape), dt))
        ident = gT("ident", (128, 128)); ones = gT("ones", (128, 128))
        MV = gT("MV", (128, 6, 8, 2))
        b.dma("sp", ident[:], cd["ident"], W=["ident"])
        b.dma("sp", ones[:], cd["ones"], W=["ones"])

        def act(out, in_, func, R, W, **kw):
            return b.op("act", lambda e: e.activation(out=out, in_=in_, func=func, **kw), R, W)

        def mm(out, lhsT, rhs, start, stop, R, W, rg=0):
            return b.op("pe", lambda e: e.matmul(out, lhsT=lhsT, rhs=rhs, start=start, stop=stop), R, W, rg)

        def tr(out, in_, R, W, idn=None):
            idn = ident[:] if idn is None else idn
            return b.op("pe", lambda e: e.transpose(out=out, in_=in_, identity=idn), list(R) + ["ident"], W)

        def tt(out, in0, in1, op, R, W, eng="dve"):
            return b.op(eng, lambda e: e.tensor_tensor(out=out, in0=in0, in1=in1, op=op), R, W)

        def ts(out, in0, s1, s2, op0, op1, R, W, eng="dve"):
            if op1 is None:
                return b.op(eng, lambda e: e.tensor_scalar(out=out, in0=in0, scalar1=s1, scalar2=None, op0=op0), R, W)
            return b.op(eng, lambda e: e.tensor_scalar(out=out, in0=in0, scalar1=s1, scalar2=s2, op0=op0, op1=op1), R, W)

        def stt(out, in0, scalar, in1, op0, op1, R, W):
            return b.op("dve", lambda e: e.scalar_tensor_tensor(out=out, in0=in0, scalar=scalar, in1=in1, op0=op0, op1=op1), R, W)

        def cp(out, in_, R, W, eng="dve"):
            if eng == "act":
                return act(out, in_, AF.Copy, R, W)
            return b.op(eng, lambda e: e.tensor_copy(out=out, in_=in_), R, W)

        def recip(out, in_, R, W):
            return b.op("dve", lambda e: e.reciprocal(out=out, in_=in_), R, W)

        def memset(ap, val, W, eng="pool"):
            return b.op(eng, lambda e: e.memset(ap, val), (), W)

        def rstd_from(out, ssq, scale, R, W):
            act(out, ssq, AF.Sqrt, R, W, scale=scale, bias=epsc[:, 0:1])
            recip(out, out, W, W)

        epsc = gT("epsc", (128, 1))
        memset(epsc[:], EPS, ["epsc"])

        with contextlib.ExitStack() as ps:
            T = lambda name, shape, dt=F32: ps.enter_context(nc.sbuf_tensor(name + "_%d" % next(_UID), list(shape), dt))
            xin = [T("i_x%d" % k, (128, D)) for k in range(2)]
            xo = [T("i_o%d" % k, (128, 8, 128)) for k in range(2)]
            for t in range(NTILE):
                k = t % 2
                src = ctx_d[t * 128:(t + 1) * 128, :] if t < 2 else x_d[(t - 2) * 128:(t - 1) * 128, :]
                b.dma("sp", xin[k][:], src, W=["i_x%d" % k])
                for c in range(8):
                    bank = (t % 2) * 2 + c // 4
                    tr(pball[:, bank, (c % 4) * 128:(c % 4 + 1) * 128], xin[k][:, c * 128:(c + 1) * 128], ["i_x%d" % k], [PB(bank)])
                for hf in range(2):
                    bank = (t % 2) * 2 + hf
                    cp(xo[k][:, hf * 4:(hf + 1) * 4, :], pball[:, bank, :].rearrange("p (c t) -> p c t", c=4), [], [PB(bank), "i_o%d" % k],
                       eng="act" if hf else "dve")
                b.dma("sp", HT[:, t * 128:(t + 1) * 128].rearrange("(c p) t -> p c t", p=128), xo[k][:], R=["i_o%d" % k], W=["HT"])
            b.barrier()

        blocks512 = [(0, 256)] + [(256 + 512 * i, 512) for i in range(8)]
        blocks256 = [(256 * i, 256) for i in range(17)]

        def norm_mod(ht, TB, sq, rstd, aT, kmul, kshift, s, tag, R):
            act(sq[:, :, :TB], ht[:, :, :TB], AF.Square, R, [tag + "sq"])
            for c in range(8):
                mm(pball[:, 0, :TB], ones[:], sq[:, c, :TB], c == 0, c == 7, [tag + "sq", "ones"], [PB(0)])
            rstd_from(rstd[:, :TB], pball[:, 0, :TB], 1.0 / D, [], [PB(0), tag + "rstd"])
            tt(sq[:, :, :TB], ht[:, :, :TB], rstd[:, :TB].unsqueeze(1).broadcast_to([128, 8, TB]), ALU.mult,
               list(R) + [tag + "rstd"], [tag + "sq"])
            for c in range(8):
                act(aT[:, c, :TB], sq[:, c, :TB], AF.Identity, [tag + "sq", "MV"], [tag + "aT"],
                    scale=MV[:, kmul, c, s:s + 1], bias=MV[:, kshift, c, s:s + 1])

        def post_norm_add(ht, yT, TB, sq, rstd, kg, s, tag):
            act(sq[:, :, :TB], yT[:, :, :TB], AF.Square, [tag + "yT"], [tag + "sq"])
            for c in range(8):
                mm(pball[:, 0, :TB], ones[:], sq[:, c, :TB], c == 0, c == 7, [tag + "sq", "ones"], [PB(0)])
            rstd_from(rstd[:, :TB], pball[:, 0, :TB], 1.0 / D, [], [PB(0), tag + "rstd"])
            tt(sq[:, :, :TB], yT[:, :, :TB], rstd[:, :TB].unsqueeze(1).broadcast_to([128, 8, TB]), ALU.mult,
               [tag + "yT", tag + "rstd"], [tag + "sq"])
            for c in range(8):
                stt(ht[:, c, :TB], sq[:, c, :TB], MV[:, kg, c, s:s + 1], ht[:, c, :TB], ALU.mult, ALU.add,
                    [tag + "sq", "MV"], [tag + "ht"])

        def conv_seg(xin, acc, rows, t0, Lb, seg_lo, seg_hi, cw, cbias, ch, tag):
            lo = max(t0 - 2, seg_lo); hi = min(t0 + Lb + 2, seg_hi)
            if lo > t0 - 2:
                memset(xin[:, 0:2], 0.0, [tag + "xin"])
            if hi < t0 + Lb + 2:
                memset(xin[:, Lb + 2:Lb + 4], 0.0, [tag + "xin"])
            b.dma("sp", xin[:, lo - (t0 - 2):hi - (t0 - 2)], PT[rows, lo:hi], R=["PT"], W=[tag + "xin"])
            if cbias is None:
                ts(acc[:, :Lb], xin[:, 0:Lb], cw[:, ch, 0:1], None, ALU.mult, None, [tag + "xin", tag + "cw"], [tag + "acc"])
            else:
                ts(acc[:, :Lb], xin[:, 0:Lb], cw[:, ch, 0:1], cbias[:, ch:ch + 1], ALU.mult, ALU.add, [tag + "xin", tag + "cw"], [tag + "acc"])
            for j in range(1, 5):
                stt(acc[:, :Lb], xin[:, j:j + Lb], cw[:, ch, j:j + 1], acc[:, :Lb], ALU.mult, ALU.add, [tag + "xin", tag + "cw"], [tag + "acc"])

        def decay(E_out, Ekey, g1, g2, dirs, lhs_ones, TRI, negh, banks, rhs1, rhs2, tag, R):
            nh = len(dirs)
            d0 = 0
            while d0 < nh:
                d1 = d0
                while d1 < nh and dirs[d1] == dirs[d0]:
                    d1 += 1
                n = d1 - d0
                tt(rhs1[:, d0:d1, :], g1[:, d0:d1].unsqueeze(2).broadcast_to([128, n, 128]),
                   TRI[dirs[d0]].unsqueeze(1).broadcast_to([128, n, 128]), ALU.mult, list(R) + [tag + "c"], [tag + "rhs1"])
                d0 = d1
            cp(rhs2[:, 0:nh, :], g2[:, 0:nh].unsqueeze(2).broadcast_to([128, nh, 128]), R, [tag + "rhs2"], eng="pool")
            for bi, bank in enumerate(banks):
                h0 = 4 * bi; h1 = min(nh, h0 + 4); n = h1 - h0
                if n <= 0:
                    break
                mm(pball[:, bank, 0:n * 128], lhs_ones, rhs1[:, h0:h1, :].rearrange("p h f -> p (h f)"), True, False,
                   [tag + "rhs1", tag + "c"], [PB(bank)])
                a0 = h0
                while a0 < h1:
                    a1 = a0
                    while a1 < h1 and dirs[a1] == dirs[a0]:
                        a1 += 1
                    mm(pball[:, bank, (a0 - h0) * 128:(a1 - h0) * 128], TRI[dirs[a0]], rhs2[:, a0:a1, :].rearrange("p h f -> p (h f)"),
                       False, False, [tag + "rhs2", tag + "c"], [PB(bank)])
                    a0 = a1
                mm(pball[:, bank, 0:n * 128], ident[:], negh[:, h0:h1, :].rearrange("p h f -> p (h f)"), False, True,
                   ["ident", tag + "c"], [PB(bank)])
                act(E_out[:, h0:h1, :].rearrange("p h f -> p (h f)"), pball[:, bank, 0:n * 128], AF.Exp, [], [PB(bank), Ekey])

        def ssd_phase(L, need_ctx):
            with contextlib.ExitStack() as ps:
                T = lambda name, shape, dt=F32: ps.enter_context(nc.sbuf_tensor(name + "_%d" % next(_UID), list(shape), dt))
                xtok = T("s_xtok", (128, NTILE, 384), BF16); Btok = T("s_Btok", (128, NTILE, 256), BF16)
                BT = T("s_BT", (128, 2, NT), BF16); CT = T("s_CT", (128, 2, NT), BF16)
                dttok = T("s_dttok", (128, NTILE, 12)); atok = T("s_atok", (128, NTILE, 12)); natok = T("s_natok", (128, NTILE, 12))
                cw = T("s_cw", (128, 7, 5)); cb = T("s_cb", (128, 7)); Abc = T("s_Abc", (128, 12)); dtb = T("s_dtb", (12, 1))
                Dbc = T("s_Dbc", (128, 6)); sng = T("s_sng", (128, 3))
                with nc.allow_non_contiguous_dma(reason="tiny vectors"):
                    b.dma("sp", cw[:], scw_d[L].rearrange("(c p) j -> p c j", p=128), W=["s_cw"])
                    b.dma("sp", cb[:], scb_d[L].rearrange("(c p) -> p c", p=128), W=["s_cw"])
                    b.dma("sp", dtb[:], sdb_d[L].rearrange("(p o) -> p o", o=1), W=["s_dtb"])
                    b.dma("sp", sng[:], sng_d[L].rearrange("(c p) -> p c", p=128), W=["s_sng"])
                b.dma("sp", Abc[:], sal_d[L].partition_broadcast(128), W=["s_Abc"])
                b.dma("sp", Dbc[:], sD_d[L].partition_broadcast(128), W=["s_Dbc"])
                act(Abc[:], Abc[:], AF.Exp, ["s_Abc"], ["s_Abc"])
                with contextlib.ExitStack() as ps2:
                    T2 = lambda name, shape, dt=F32: ps2.enter_context(nc.sbuf_tensor(name + "_%d" % next(_UID), list(shape), dt))
                    xin = T2("s_xin", (128, LLAT + 4)); acc = T2("s_acc", (128, LLAT)); cvb = T2("s_cv", (128, LLAT))
                    dtT = T2("s_dtT", (12, NT))
                    for ch in range(7):
                        for (t0, Lb) in ((0, LCTX), (LCTX, LLAT)):
                            conv_seg(xin, acc, slice((CH_SX + ch) * 128, (CH_SX + ch + 1) * 128), t0, Lb, t0, t0 + Lb, cw, cb, ch, "s_")
                            if ch < 3 or ch in (3, 4):
                                act(cvb[:, :Lb], acc[:, :Lb], AF.Silu, ["s_acc"], ["s_cv"])
                                if ch in (3, 4):
                                    cp(BT[:, ch - 3, t0:t0 + Lb], cvb[:, :Lb], ["s_cv"], ["s_BT"], eng="pool")
                                for q4 in range(Lb // 512 if Lb >= 512 else 1):
                                    nt4 = min(4, Lb // 128)
                                    bank = 1 + q4 % 4
                                    for j in range(nt4):
                                        tr(pball[:, bank, j * 128:(j + 1) * 128], cvb[:, (q4 * 4 + j) * 128:(q4 * 4 + j + 1) * 128], ["s_cv"], [PB(bank)])
                                    tl0 = t0 // 128 + q4 * 4
                                    dst = xtok[:, tl0:tl0 + nt4, ch * 128:(ch + 1) * 128] if ch < 3 else Btok[:, tl0:tl0 + nt4, (ch - 3) * 128:(ch - 2) * 128]
                                    cp(dst, pball[:, bank, 0:nt4 * 128].rearrange("p (t f) -> p t f", t=nt4), [], [PB(bank), "s_xtok" if ch < 3 else "s_Btok"],
                                       eng="act" if q4 % 2 else "dve")
                            else:
                                act(CT[:, ch - 5, t0:t0 + Lb], acc[:, :Lb], AF.Silu, ["s_acc"], ["s_CT"])
                    b.dma("sp", dtT[:], PT[CH_SM * 128:CH_SM * 128 + 12, :], R=["PT"], W=["s_dtT"])
                    act(dtT[:], dtT[:], AF.Exp, ["s_dtT", "s_dtb"], ["s_dtT"], bias=dtb[:, 0:1])
                    act(dtT[:], dtT[:], AF.Ln, ["s_dtT"], ["s_dtT"], bias=1.0)
                    for t in range(NTILE):
                        tr(pball[:, 7, (t % 32) * 12:(t % 32) * 12 + 12], dtT[:, t * 128:(t + 1) * 128], ["s_dtT"], [PB(7)], idn=ident[0:12, 0:12])
                        if t == 31 or t == NTILE - 1:
                            tl0 = 0 if t == 31 else 32
                            n = t - tl0 + 1
                            cp(dttok[:, tl0:tl0 + n, :], pball[:, 7, 0:n * 12].rearrange("p (t h) -> p t h", h=12), [], [PB(7), "s_dttok"])
                    stt(atok[:], dttok[:], -1.0, Abc[:].unsqueeze(1).broadcast_to([128, NTILE, 12]), ALU.mult, ALU.mult, ["s_dttok", "s_Abc"], ["s_atok"])
                    ts(natok[:], atok[:], -1.0, None, ALU.mult, None, ["s_atok"], ["s_natok"])
                    b.barrier()
                Senter = T("s_Sent", (128, NTILE, 2, 6, 64), BF16); eacum = T("s_eacum", (128, NTILE, 12))
                tri = T("s_tri", (128, 2, 128)); tris = T("s_tris", (128, 2, 128)); negh = T("s_negh", (128, 12, 128))
                rhs1 = T("s_rhs1", (128, 12, 128)); rhs2 = T("s_rhs2", (128, 12, 128)); E = T("s_E", (128, 12, 128))
                MT = T("s_MT", (128, 12, 128), BF16); xdt = T("s_xdt", (128, 12, 64), BF16); xw = T("s_xw", (128, 6, 64), BF16)
                S = T("s_S", (128, 6, 64)); wexp = T("s_wexp", (128, 6)); elast = T("s_elast", (128, 6)); wv = T("s_w", (128, 6))
                tA = T("s_tA", (128, 6, 64)); tB = T("s_tB", (128, 6, 64)); zT = T("s_zT", (128, 3, 128)); sz = T("s_sz", (128, 384))
                gated = T("s_gated", (128, 384)); junk = T("s_junk", (128, 192)); ssq = T("s_ssq", (128, 2)); ostg = T("s_ostg", (128, 3, 128))
                b.dma("sp", tri[:], cd["tri"].rearrange("d p f -> p d f"), W=["s_c"])
                b.dma("sp", tris[:], cd["tris"].rearrange("d p f -> p d f"), W=["s_c"])
                b.dma("sp", negh[:], cd["ssd_negh"], W=["s_c"])
                TRI = [tri[:, 0, :], tri[:, 1, :]]
                for d in range(2):
                    order = list(range(NTILE)) if d == 0 else [1, 0] + list(range(NTILE - 1, 1, -1))
                    memset(S[:], 0.0, ["s_S"])
                    hs = slice(6 * d, 6 * d + 6)
                    for c in order:
                        mm(pball[:, 7, 0:6], tri[:, d, :], atok[:, c, hs], True, True, ["s_c", "s_atok"], [PB(7)])
                        mm(pball[:, 7, 6:12], tris[:, d, :], atok[:, c, hs], True, True, ["s_c", "s_atok"], [PB(7)])
                        mm(pball[:, 7, 12:18], ones[:], atok[:, c, hs], True, True, ["ones", "s_atok"], [PB(7)])
                        act(eacum[:, c, hs], pball[:, 7, 0:6], AF.Exp, [], [PB(7), "s_eacum"])
                        act(wexp[:], pball[:, 7, 6:12], AF.Exp, [], [PB(7), "s_wexp"])
                        act(elast[:], pball[:, 7, 12:18], AF.Exp, [], [PB(7), "s_elast"])
                        tt(wv[:], wexp[:], dttok[:, c, hs], ALU.mult, ["s_wexp", "s_dttok"], ["s_w"])
                        tt(xw[:], xtok[:, c, :].rearrange("p (h e) -> p h e", h=6), wv[:].unsqueeze(2).broadcast_to([128, 6, 64]), ALU.mult,
                           ["s_xtok", "s_w"], ["s_xw"])
                        for g in range(2):
                            mm(pball[:, 6, g * 192:(g + 1) * 192], Btok[:, c, g * 128:(g + 1) * 128],
                               xw[:, 3 * g:3 * g + 3, :].rearrange("p h e -> p (h e)"), True, True, ["s_Btok", "s_xw"], [PB(6)])
                        cp(Senter[:, c, d], S[:], ["s_S"], ["s_Sent"], eng="pool")
                        tt(S[:], S[:], elast[:].unsqueeze(2).broadcast_to([128, 6, 64]), ALU.mult, ["s_elast"], ["s_S"])
                        tt(S[:], S[:], pball[:, 6, 0:384].rearrange("p (h e) -> p h e", h=6), ALU.add, [], [PB(6), "s_S"])
                for c in range(NTILE):
                    if c < 2 and not need_ctx:
                        continue
                    cs = slice(c * 128, (c + 1) * 128)
                    b.dma("sp", zT[:], PT[CH_SZ * 128:(CH_SZ + 3) * 128, cs].rearrange("(c p) t -> p c t", p=128), R=["PT"], W=["s_zT"])
                    decay(E, "s_E", atok[:, c, :], natok[:, c, :], [0] * 6 + [1] * 6, ones[:], TRI, negh, [0, 1, 2], rhs1, rhs2, "s_", ["s_atok", "s_natok"])
                    for g in range(2):
                        mm(pball[:, 3, g * 128:(g + 1) * 128], BT[:, g, cs], CT[:, g, cs], True, True, ["s_BT", "s_CT"], [PB(3)])
                    for d in range(2):
                        for g in range(2):
                            h0 = d * 6 + 3 * g
                            tt(MT[:, h0:h0 + 3, :], E[:, h0:h0 + 3, :], pball[:, 3, g * 128:(g + 1) * 128].unsqueeze(1).broadcast_to([128, 3, 128]),
                               ALU.mult, ["s_E"], [PB(3), "s_MT"])
                        tt(xdt[:, 6 * d:6 * d + 6, :], xtok[:, c, :].rearrange("p (h e) -> p h e", h=6),
                           dttok[:, c, 6 * d:6 * d + 6].unsqueeze(2).broadcast_to([128, 6, 64]), ALU.mult, ["s_xtok", "s_dttok"], ["s_xdt"], eng="pool")
                    for h in range(6):
                        for d in range(2):
                            mm(pball[:, 4, h * 64:(h + 1) * 64], MT[:, d * 6 + h, :], xdt[:, d * 6 + h, :], d == 0, d == 1, ["s_MT", "s_xdt"], [PB(4)])
                    for d in range(2):
                        for g in range(2):
                            mm(pball[:, 5 + d, g * 192:(g + 1) * 192], CT[:, g, cs], Senter[:, c, d, 3 * g:3 * g + 3, :].rearrange("p h e -> p (h e)"),
                               True, True, ["s_CT", "s_Sent"], [PB(5 + d)])
                    v6 = lambda bank: pball[:, bank, 0:384].rearrange("p (h e) -> p h e", h=6)
                    tt(tA[:], v6(5), eacum[:, c, 0:6].unsqueeze(2).broadcast_to([128, 6, 64]), ALU.mult, ["s_eacum"], [PB(5), "s_tA"])
                    tt(tB[:], v6(6), eacum[:, c, 6:12].unsqueeze(2).broadcast_to([128, 6, 64]), ALU.mult, ["s_eacum"], [PB(6), "s_tB"])
                    tt(tA[:], tA[:], tB[:], ALU.add, ["s_tB"], ["s_tA"], eng="pool")
                    tt(tA[:], tA[:], v6(4), ALU.add, [], [PB(4), "s_tA"])
                    tt(tB[:], xtok[:, c, :].rearrange("p (h e) -> p h e", h=6), Dbc[:].unsqueeze(2).broadcast_to([128, 6, 64]), ALU.mult,
                       ["s_xtok", "s_Dbc"], ["s_tB"], eng="pool")
                    tt(tA[:], tA[:], tB[:], ALU.add, ["s_tB"], ["s_tA"], eng="pool")
                    for ch in range(3):
                        tr(pball[:, 7, ch * 128:(ch + 1) * 128], zT[:, ch, :], ["s_zT"], [PB(7)])
                    act(sz[:], pball[:, 7, 0:384], AF.Silu, [], [PB(7), "s_sz"])
                    tt(gated[:], tA[:].rearrange("p h e -> p (h e)"), sz[:], ALU.mult, ["s_tA", "s_sz"], ["s_gated"])
                    for g in range(2):
                        act(junk[:], gated[:, g * 192:(g + 1) * 192], AF.Square, ["s_gated"], ["s_junk", "s_ssq"], accum_out=ssq[:, g:g + 1])
                    rstd_from(ssq[:], ssq[:], 1.0 / 192, ["s_ssq"], ["s_ssq"])
                    tt(gated[:].rearrange("p (g f) -> p g f", g=2), gated[:].rearrange("p (g f) -> p g f", g=2),
                       ssq[:].unsqueeze(2).broadcast_to([128, 2, 192]), ALU.mult, ["s_ssq"], ["s_gated"])
                    for ch in range(3):
                        tr(pball[:, 7, ch * 128:(ch + 1) * 128], gated[:, ch * 128:(ch + 1) * 128], ["s_gated"], [PB(7)])
                    for ch in range(3):
                        act(ostg[:, ch, :], pball[:, 7, ch * 128:(ch + 1) * 128], AF.Copy, ["s_sng"], [PB(7), "s_ostg"], scale=sng[:, ch:ch + 1])
                    b.dma("sp", MIXT[256:640, cs].rearrange("(c p) t -> p c t", p=128), ostg[:], R=["s_ostg"], W=["MIXT"])
                b.barrier()

        def gdn_phase(L, need_ctx):
            with contextlib.ExitStack() as ps:
                T = lambda name, shape, dt=F32: ps.enter_context(nc.sbuf_tensor(name + "_%d" % next(_UID), list(shape), dt))
                gtok = T("g_gtok", (128, NTILE, 12)); ngtok = T("g_ngtok", (128, NTILE, 12)); btok = T("g_btok", (128, NTILE, 12)); nbtok = T("g_nbtok", (128, NTILE, 12))
                cw = T("g_cw", (128, 9, 5)); Abc = T("g_Abc", (12, 1)); dtb = T("g_dtb", (12, 1)); gng = T("g_gng", (128, 64))
                with nc.allow_non_contiguous_dma(reason="tiny vectors"):
                    b.dma("sp", cw[:], gcw_d[L].rearrange("(c p) j -> p c j", p=128), W=["g_cw"])
                    b.dma("sp", dtb[:], gdb_d[L].rearrange("(p o) -> p o", o=1), W=["g_dtb"])
                    b.dma("sp", Abc[:], gal_d[L].rearrange("(p o) -> p o", o=1), W=["g_Abc"])
                b.dma("sp", gng[:], gng_d[L].partition_broadcast(128), W=["g_gng"])
                act(Abc[:], Abc[:], AF.Exp, ["g_Abc"], ["g_Abc"])
                with contextlib.ExitStack() as ps2:
                    T2 = lambda name, shape, dt=F32: ps2.enter_context(nc.sbuf_tensor(name + "_%d" % next(_UID), list(shape), dt))
                    xin = T2("g_xin", (128, 1028)); acc = T2("g_acc", (128, 1024)); cv9 = T2("g_cv9", (128, 9, 1024))
                    tokb = [T2("g_tokb%d" % k, (128, 18, 64)) for k in range(2)]; sqs = T2("g_sqs", (128, 12, 64)); ssq = T2("g_ssq", (128, 12))
                    aT_ = T2("g_aT", (12, NT)); bT_ = T2("g_bT", (12, NT))
                    for (t0, Lb, lo, hi) in [(0, 256, 0, 256)] + [(256 + 1024 * i, 1024, 256, NT) for i in range(4)]:
                        for ch in range(9):
                            conv_seg(xin, acc, slice((CH_GQ + ch) * 128, (CH_GQ + ch + 1) * 128), t0, Lb, lo, hi, cw, None, ch, "g_")
                            act(cv9[:, ch, :Lb], acc[:, :Lb], AF.Silu, ["g_acc"], ["g_cv9"])
                        for j in range(Lb // 128):
                            tile = t0 // 128 + j
                            k = tile % 2
                            for ch in range(9):
                                bank = 1 + ch // 3
                                tr(pball[:, bank, (ch % 3) * 128:(ch % 3 + 1) * 128], cv9[:, ch, j * 128:(j + 1) * 128], ["g_cv9"], [PB(bank)])
                            for qk in range(2):
                                act(sqs[:, qk * 6:(qk + 1) * 6, :].rearrange("p h e -> p (h e)"), pball[:, 1 + qk, 0:384], AF.Square, [], [PB(1 + qk), "g_sqs"])
                            b.op("dve", lambda e: e.tensor_reduce(out=ssq[:], in_=sqs[:], axis=AX.X, op=ALU.add), ["g_sqs"], ["g_ssq"])
                            rstd_from(ssq[:], ssq[:], 1.0, ["g_ssq"], ["g_ssq"])
                            ts(ssq[:, 0:6], ssq[:, 0:6], 0.125, None, ALU.mult, None, ["g_ssq"], ["g_ssq"])
                            for qk in range(2):
                                tt(tokb[k][:, qk * 6:(qk + 1) * 6, :], pball[:, 1 + qk, 0:384].rearrange("p (h e) -> p h e", h=6),
                                   ssq[:, qk * 6:(qk + 1) * 6].unsqueeze(2).broadcast_to([128, 6, 64]), ALU.mult, ["g_ssq"], [PB(1 + qk), "g_tokb%d" % k])
                            cp(tokb[k][:, 12:18, :].rearrange("p h e -> p (h e)"), pball[:, 3, 0:384], [], [PB(3), "g_tokb%d" % k], eng="act")
                            b.dma("sp", GDTOK[tile], tokb[k][:].rearrange("p h e -> p (h e)"), R=["g_tokb%d" % k], W=["GDTOK"])
                    b.dma("sp", aT_[:], PT[CH_SM * 128 + 12:CH_SM * 128 + 24, :], R=["PT"], W=["g_aT"])
                    b.dma("sp", bT_[:], PT[CH_SM * 128 + 24:CH_SM * 128 + 36, :], R=["PT"], W=["g_bT"])
                    act(aT_[:], aT_[:], AF.Exp, ["g_dtb"], ["g_aT"], bias=dtb[:, 0:1])
                    act(aT_[:], aT_[:], AF.Ln, [], ["g_aT"], bias=1.0)
                    ts(aT_[:], aT_[:], Abc[:, 0:1], -1.0, ALU.mult, ALU.mult, ["g_Abc"], ["g_aT"])
                    act(bT_[:], bT_[:], AF.Exp, [], ["g_bT"], scale=-1.0)
                    ts(bT_[:], bT_[:], 1.0, None, ALU.add, None, [], ["g_bT"])
                    recip(bT_[:], bT_[:], [], ["g_bT"])
                    for (srcT, dst, key) in ((aT_, gtok, "g_gtok"), (bT_, btok, "g_btok")):
                        for t in range(NTILE):
                            tr(pball[:, 7, (t % 32) * 12:(t % 32) * 12 + 12], srcT[:, t * 128:(t + 1) * 128], ["g_aT", "g_bT"], [PB(7)], idn=ident[0:12, 0:12])
                            if t == 31 or t == NTILE - 1:
                                tl0 = 0 if t == 31 else 32
                                n = t - tl0 + 1
                                cp(dst[:, tl0:tl0 + n, :], pball[:, 7, 0:n * 12].rearrange("p (t h) -> p t h", h=12), [], [PB(7), key])
                    ts(ngtok[:], gtok[:], -1.0, None, ALU.mult, None, ["g_gtok"], ["g_ngtok"])
                    ts(nbtok[:], btok[:], -1.0, None, ALU.mult, None, ["g_btok"], ["g_nbtok"])
                    b.barrier()
                Oacc = T("g_Oacc", (128, NTILE, 384))
                onesblk = T("g_onesblk", (128, 128)); triblk = T("g_triblk", (128, 2, 128)); trisblk = T("g_trisblk", (128, 2, 128))
                negdt = T("g_negdt", (128, 2, 6, 128)); negd = T("g_negd", (128, 2, 6, 128)); chsel = T("g_chsel", (128, 2, 64))
                tok = [T("g_tok%d" % k, (128, 18, 64)) for k in range(2)]
                rhs1 = T("g_rhs1", (128, 6, 128)); rhs2 = T("g_rhs2", (128, 6, 128)); E1 = T("g_E1", (128, 6, 128)); E2 = T("g_E2", (128, 6, 128))
                qkT = T("g_qkT", (64, 12, 128)); P = T("g_P", (128, 6, 128)); NTT = T("g_NT", (128, 6, 2, 128)); attnT = T("g_attnT", (128, 6, 128))
                vb = T("g_vb", (128, 6, 64)); kbe = T("g_kbe", (128, 6, 64)); qd = T("g_qd", (128, 6, 64)); kdec = T("g_kdec", (128, 6, 64))
                u = T("g_u", (128, 6, 64)); wT = T("g_wT", (64, 6, 128)); qdT = T("g_qdT", (64, 6, 128)); vnew = T("g_vnew", (128, 6, 64))
                S = T("g_S", (64, 6, 64)); egc = T("g_egc", (128, 6)); ekd = T("g_ekd", (128, 6)); elast = T("g_elast", (64, 2, 6)); be = T("g_be", (128, 6))
                zT = T("g_zT", (128, 3, 128)); sz = T("g_sz", (128, 384)); of = T("g_of", (128, 6, 64)); sq6 = T("g_sq6", (128, 6, 64)); ss6 = T("g_ss6", (128, 6))
                ostg = T("g_ostg", (128, 3, 128))
                b.dma("sp", onesblk[:], cd["onesblk"], W=["g_c"])
                b.dma("sp", triblk[:], cd["triblk"].rearrange("d p f -> p d f"), W=["g_c"])
                b.dma("sp", trisblk[:], cd["trisblk"].rearrange("d p f -> p d f"), W=["g_c"])
                b.dma("sp", negdt[:], cd["gdn_negdt"].rearrange("d p h f -> p d h f"), W=["g_c"])
                b.dma("sp", negd[:], cd["gdn_negd"].rearrange("d p h f -> p d h f"), W=["g_c"])
                b.dma("sp", chsel[:], cd["chsel"].rearrange("c p f -> p c f"), W=["g_c"])
                TRIB = [triblk[:, 0, :], triblk[:, 1, :]]
                v6 = lambda bank: pball[:, bank, 0:384].rearrange("p (h e) -> p h e", h=6)
                for d in range(2):
                    order = list(range(NTILE)) if d == 0 else [1, 0] + list(range(NTILE - 1, 1, -1))
                    memset(S[:], 0.0, ["g_S"])
                    hs = slice(6 * d, 6 * d + 6)
                    for ti, tile in enumerate(order):
                        tk = tok[ti % 2]; tkk = "g_tok%d" % (ti % 2)
                        cs = slice(tile * 128, (tile + 1) * 128)
                        b.dma("sp", tk[:].rearrange("p h e -> p (h e)"), GDTOK[tile], R=["GDTOK"], W=[tkk])
                        g6 = gtok[:, tile, hs]; ng6 = ngtok[:, tile, hs]; b6 = btok[:, tile, hs]; nb6 = nbtok[:, tile, hs]
                        RG = ["g_gtok", "g_ngtok"]
                        decay(E1, "g_E1", g6, ng6, [d] * 6, onesblk[:], TRIB, negdt[:, d], [0, 1], rhs1, rhs2, "g_", RG)
                        decay(E2, "g_E2", ng6, g6, [d] * 6, onesblk[:], TRIB, negd[:, d], [2, 3], rhs1, rhs2, "g_", RG)
                        mm(pball[:, 7, 0:6], triblk[:, d, :], g6, True, True, ["g_c", "g_gtok"], [PB(7)])
                        mm(pball[:, 7, 6:12], trisblk[:, d, :], g6, True, True, ["g_c", "g_gtok"], [PB(7)])
                        for c in range(2):
                            mm(pball[0:64, 7, 12 + 6 * c:18 + 6 * c], chsel[:, c, :], g6, True, True, ["g_c", "g_gtok"], [PB(7)])
                        act(egc[:], pball[:, 7, 0:6], AF.Exp, [], [PB(7), "g_egc"])
                        act(ekd[:], pball[:, 7, 6:12], AF.Exp, [], [PB(7), "g_ekd"])
                        act(elast[:].rearrange("p c h -> p (c h)"), pball[0:64, 7, 12:24], AF.Exp, [], [PB(7), "g_elast"])
                        for x in range(12):
                            bank = 4 + x // 4
                            tr(pball[0:64, bank, (x % 4) * 128:(x % 4 + 1) * 128], tk[:, x, :], [tkk], [PB(bank)])
                        for q3 in range(3):
                            cp(qkT[:, q3 * 4:(q3 + 1) * 4, :].rearrange("p x t -> p (x t)"), pball[0:64, 4 + q3, :], [], [PB(4 + q3), "g_qkT"],
                               eng="act" if q3 == 1 else "dve")
                        for h in range(6):
                            bank = h // 2; off = (h % 2) * 256
                            mm(pball[:, bank, off:off + 128], qkT[:, 6 + h, :], qkT[:, 6 + h, :], True, True, ["g_qkT"], [PB(bank)])
                            mm(pball[:, bank, off + 128:off + 256], qkT[:, 6 + h, :], qkT[:, h, :], True, True, ["g_qkT"], [PB(bank)])
                        for h in range(6):
                            bank = h // 2; off = (h % 2) * 256
                            stt(P[:, h, :], pball[:, bank, off:off + 128], nb6[:, h:h + 1], E2[:, h, :], ALU.mult, ALU.mult, ["g_nbtok", "g_E2"], [PB(bank), "g_P"])
                            tt(attnT[:, h, :], pball[:, bank, off + 128:off + 256], E1[:, h, :], ALU.mult, ["g_E1"], [PB(bank), "g_attnT"])
                        for h in range(6):
                            bank = 3 + h // 3
                            tr(pball[:, bank, (h % 3) * 128:(h % 3 + 1) * 128], P[:, h, :], ["g_P"], [PB(bank)])
                        for grp in range(2):
                            cp(NTT[:, 3 * grp:3 * grp + 3, 0, :], pball[:, 3 + grp, 0:384].rearrange("p (h f) -> p h f", h=3), [], [PB(3 + grp), "g_N"], eng="act")
                        tt(NTT[:, :, 1, :], NTT[:, :, 0, :], ident[:].unsqueeze(1).broadcast_to([128, 6, 128]), ALU.add, ["g_N", "ident"], ["g_T"])
                        for m in range(6):
                            for grp in range(2):
                                bk0 = 3 * grp
                                hsl = slice(3 * grp, 3 * grp + 3)
                                for hh in range(3):
                                    h = 3 * grp + hh
                                    bank = bk0 + hh
                                    if m == 0:
                                        mm(pball[:, bank, 0:128], P[:, h, :], NTT[:, h, 0, :], True, True, ["g_P", "g_N"], [PB(bank)])
                                        mm(pball[:, bank, 256:384], NTT[:, h, 0, :], P[:, h, :], True, True, ["g_P", "g_N"], [PB(bank)])
                                    elif m < 5:
                                        mm(pball[:, bank, 0:256], P[:, h, :], NTT[:, h, :, :].rearrange("p s f -> p (s f)"), True, True, ["g_P", "g_N", "g_T"], [PB(bank)])
                                        mm(pball[:, bank, 256:384], NTT[:, h, 0, :], P[:, h, :], True, True, ["g_P", "g_N"], [PB(bank)])
                                    else:
                                        mm(pball[:, bank, 128:256], P[:, h, :], NTT[:, h, 1, :], True, True, ["g_P", "g_T"], [PB(bank)])
                                PBS = [PB(bk0), PB(bk0 + 1), PB(bk0 + 2)]
                                if m < 5:
                                    cp(NTT[:, hsl, 0, :], pball[:, bk0:bk0 + 3, 0:128], [], PBS + ["g_N"], eng="act")
                                    cp(P[:, hsl, :], pball[:, bk0:bk0 + 3, 256:384], [], PBS + ["g_P"], eng="act" if grp else "dve")
                                if m > 0:
                                    tt(NTT[:, hsl, 1, :], NTT[:, hsl, 1, :], pball[:, bk0:bk0 + 3, 128:256], ALU.add, [], PBS + ["g_T"])
                        tt(vb[:], tk[:, 12:18, :], b6.unsqueeze(2).broadcast_to([128, 6, 64]), ALU.mult, [tkk, "g_btok"], ["g_vb"], eng="pool")
                        tt(be[:], b6, egc[:], ALU.mult, ["g_btok", "g_egc"], ["g_be"])
                        tt(kbe[:], tk[:, 6:12, :], be[:].unsqueeze(2).broadcast_to([128, 6, 64]), ALU.mult, [tkk, "g_be"], ["g_kbe"])
                        tt(qd[:], tk[:, 0:6, :], egc[:].unsqueeze(2).broadcast_to([128, 6, 64]), ALU.mult, [tkk, "g_egc"], ["g_qd"], eng="pool")
                        tt(kdec[:], tk[:, 6:12, :], ekd[:].unsqueeze(2).broadcast_to([128, 6, 64]), ALU.mult, [tkk, "g_ekd"], ["g_kdec"])
                        for h in range(6):
                            mm(pball[:, 6, h * 64:(h + 1) * 64], NTT[:, h, 1, :], vb[:, h, :], True, True, ["g_T", "g_vb"], [PB(6)])
                        cp(u[:], v6(6), [], [PB(6), "g_u"])
                        for h in range(6):
                            mm(pball[0:64, h // 3, (h % 3) * 128:(h % 3 + 1) * 128], kbe[:, h, :], NTT[:, h, 1, :], True, True, ["g_kbe", "g_T"], [PB(h // 3)])
                        for grp in range(2):
                            cp(wT[:, 3 * grp:3 * grp + 3, :].rearrange("p h t -> p (h t)"), pball[0:64, grp, 0:384], [], [PB(grp), "g_wT"], eng="act")
                        for h in range(6):
                            tr(pball[0:64, 2 + h // 3, (h % 3) * 128:(h % 3 + 1) * 128], qd[:, h, :], ["g_qd"], [PB(2 + h // 3)])
                        for grp in range(2):
                            cp(qdT[:, 3 * grp:3 * grp + 3, :].rearrange("p h t -> p (h t)"), pball[0:64, 2 + grp, 0:384], [], [PB(2 + grp), "g_qdT"])
                        for c in ((0, 1) if d == 0 else (1, 0)):
                            pc = slice(64 * c, 64 * c + 64)
                            rgc = 64 * c
                            for h in range(6):
                                mm(pball[pc, 4, h * 64:(h + 1) * 64], wT[:, h, pc], S[:, h, :], True, True, ["g_wT", "g_S"], [PB(4)])
                            tt(vnew[pc], u[pc], pball[pc, 4, 0:384].rearrange("p (h e) -> p h e", h=6), ALU.subtract, ["g_u"], [PB(4), "g_vnew"])
                            for h in range(6):
                                mm(pball[pc, 5, h * 64:(h + 1) * 64], qdT[:, h, pc], S[:, h, :], True, False, ["g_qdT", "g_S"], [PB(5)])
                                mm(pball[pc, 5, h * 64:(h + 1) * 64], attnT[pc, h, pc], vnew[pc, h, :], False, True, ["g_attnT", "g_vnew"], [PB(5)], rg=rgc)
                            for h in range(6):
                                mm(pball[0:64, 7, h * 64:(h + 1) * 64], kdec[pc, h, :], vnew[pc, h, :], True, True, ["g_kdec", "g_vnew"], [PB(7)], rg=rgc)
                            tt(S[:], S[:], elast[:, c, :].unsqueeze(2).broadcast_to([64, 6, 64]), ALU.mult, ["g_elast"], ["g_S"])
                            tt(S[:], S[:], pball[0:64, 7, 0:384].rearrange("p (h e) -> p h e", h=6), ALU.add, [], [PB(7), "g_S"])
                        if d == 0:
                            cp(Oacc[:, tile, :], pball[:, 5, 0:384], [], [PB(5), "g_Oacc"], eng="act")
                            continue
                        if tile < 2 and not need_ctx:
                            continue
                        b.dma("sp", zT[:], PT[CH_GZ * 128:(CH_GZ + 3) * 128, cs].rearrange("(c p) t -> p c t", p=128), R=["PT"], W=["g_zT"])
                        tt(of[:], Oacc[:, tile, :].rearrange("p (h e) -> p h e", h=6), v6(5), ALU.add, ["g_Oacc"], [PB(5), "g_of"])
                        act(sq6[:], of[:], AF.Square, ["g_of"], ["g_sq6"])
                        b.op("dve", lambda e: e.tensor_reduce(out=ss6[:], in_=sq6[:], axis=AX.X, op=ALU.add), ["g_sq6"], ["g_ss6"])
                        rstd_from(ss6[:], ss6[:], 1.0 / 64, ["g_ss6"], ["g_ss6"])
                        tt(of[:], of[:], ss6[:].unsqueeze(2).broadcast_to([128, 6, 64]), ALU.mult, ["g_ss6"], ["g_of"])
                        tt(of[:], of[:], gng[:].unsqueeze(1).broadcast_to([128, 6, 64]), ALU.mult, ["g_gng"], ["g_of"], eng="pool")
                        for ch in range(3):
                            tr(pball[:, 6, ch * 128:(ch + 1) * 128], zT[:, ch, :], ["g_zT"], [PB(6)])
                        act(sz[:], pball[:, 6, 0:384], AF.Silu, [], [PB(6), "g_sz"])
                        tt(sz[:], sz[:], of[:].rearrange("p h e -> p (h e)"), ALU.mult, ["g_of"], ["g_sz"])
                        for ch in range(3):
                            tr(pball[:, 6, ch * 128:(ch + 1) * 128], sz[:, ch * 128:(ch + 1) * 128], ["g_sz"], [PB(6)])
                        cp(ostg[:].rearrange("p c t -> p (c t)"), pball[:, 6, 0:384], [], [PB(6), "g_ostg"], eng="act")
                        b.dma("sp", MIXT[640:1024, cs].rearrange("(c p) t -> p c t", p=128), ostg[:], R=["g_ostg"], W=["MIXT"])
                b.barrier()

        def mlp_phase(L, need_ctx):
            with contextlib.ExitStack() as ps:
                T = lambda name, shape, dt=F32: ps.enter_context(nc.sbuf_tensor(name + "_%d" % next(_UID), list(shape), dt))
                Wout = T("m5_wout", (128, 8, D), BF16); W1 = T("m5_w1", (128, 8, 4 * D), BF16); W2 = T("m5_w2", (128, 32, D), BF16)
                ht = T("m5_htb", (128, 8, 256)); mxb = T("m5_mxb", (128, 8, 256), BF16); yT = T("m5_yTb", (128, 8, 256)); sq = T("m5_sqb", (128, 8, 256))
                rstd = T("m5_rstdb", (128, 256)); mT = T("m5_aTb", (128, 8, 256), BF16); h1T = T("m5_h1T", (128, 32, 256), BF16)
                rtmp = [T("m5_r%d" % k, (128, 256)) for k in range(2)]
                for c in range(8):
                    b.dma("pool", Wout[:, c, :], wout_d[L, c * 128:(c + 1) * 128, :], W=["m5_wout"])
                    for hf in range(2):
                        b.dma("pool", W1[:, c, hf * 2048:(hf + 1) * 2048], w1_d[L, c * 128:(c + 1) * 128, hf * 2048:(hf + 1) * 2048], W=["m5_w1"])
                for fc in range(32):
                    b.dma("pool", W2[:, fc, :], w2_d[L, fc * 128:(fc + 1) * 128, :], W=["m5_w2"])
                for (t0, TB) in blocks256:
                    if t0 < LCTX and not need_ctx:
                        continue
                    s = 1 if t0 < LCTX else 0
                    b.dma("sp", ht[:], HT[:, t0:t0 + TB].rearrange("(c p) t -> p c t", p=128), R=["HT"], W=["m5_ht"])
                    b.dma("pool", mxb[:], MIXT[:, t0:t0 + TB].rearrange("(c p) t -> p c t", p=128), R=["MIXT"], W=["m5_mxb"])
                    for oc in range(8):
                        bank = 1 + oc % 4
                        for c in range(8):
                            mm(pball[:, bank, :TB], Wout[:, c, oc * 128:(oc + 1) * 128], mxb[:, c, :], c == 0, c == 7, ["m5_wout", "m5_mxb"], [PB(bank)])
                        cp(yT[:, oc, :], pball[:, bank, :TB], [], [PB(bank), "m5_yT"], eng="act" if oc % 2 else "dve")
                    post_norm_add(ht, yT, TB, sq, rstd, 2, s, "m5_")
                    norm_mod(ht, TB, sq, rstd, mT, 3, 4, s, "m5_", ["m5_ht"])
                    for fc in range(32):
                        bank = 1 + fc % 4
                        for c in range(8):
                            mm(pball[:, bank, :TB], W1[:, c, fc * 128:(fc + 1) * 128], mT[:, c, :], c == 0, c == 7, ["m5_w1", "m5_aT"], [PB(bank)])
                        k = fc % 2
                        act(rtmp[k][:], pball[:, bank, :TB], AF.Relu, [], [PB(bank), "m5_r%d" % k])
                        tt(h1T[:, fc, :], rtmp[k][:], rtmp[k][:], ALU.mult, ["m5_r%d" % k], ["m5_h1T"], eng="pool" if fc % 4 == 3 else "dve")
                    for oc in range(8):
                        bank = 5 + oc % 3
                        for fc in range(32):
                            mm(pball[:, bank, :TB], W2[:, fc, oc * 128:(oc + 1) * 128], h1T[:, fc, :], fc == 0, fc == 31, ["m5_w2", "m5_h1T"], [PB(bank)])
                        cp(yT[:, oc, :], pball[:, bank, :TB], [], [PB(bank), "m5_yT"], eng="act" if oc % 2 else "dve")
                    post_norm_add(ht, yT, TB, sq, rstd, 5, s, "m5_")
                    b.dma("sp", HT[:, t0:t0 + TB].rearrange("(c p) t -> p c t", p=128), ht[:], R=["m5_ht"], W=["HT"])
                b.barrier()

        for L in range(depth):
            need_ctx = L < depth - 1
            with contextlib.ExitStack() as ps:
                T = lambda name, shape, dt=F32: ps.enter_context(nc.sbuf_tensor(name + "_%d" % next(_UID), list(shape), dt))
                wa = [T("m_wa%d" % k, (128, 8, 1024)) for k in range(2)]
                sc = T("m_sc", (128, 8, 2)); modv = T("m_mod", (128, 48, 2)); bad = T("m_bad", (128, 48)); ng = T("m_ng", (128, 4, 8))
                with nc.allow_non_contiguous_dma(reason="tiny vectors"):
                    for s_ in range(2):
                        b.dma("sp", sc[:, :, s_], cv_d[s_].rearrange("(c p) -> p c", p=128), W=["m_sc"])
                    b.dma("sp", bad[:], bada_d[L].rearrange("(c p) -> p c", p=128), W=["m_bad"])
                    for k_ in range(4):
                        b.dma("sp", ng[:, k_, :], ng_d[L, k_].rearrange("(c p) -> p c", p=128), W=["m_ng"])
                act(sc[:], sc[:], AF.Silu, ["m_sc"], ["m_sc"])
                for blk in range(6):
                    k = blk % 2
                    for c in range(8):
                        b.dma("sp", wa[k][:, c, :], wada_d[L, c * 128:(c + 1) * 128, blk * 1024:(blk + 1) * 1024], W=[("m_wa", k, c)])
                    for fcl in range(8):
                        fc = blk * 8 + fcl
                        for c in range(8):
                            mm(pball[:, 1, fc * 2:fc * 2 + 2], wa[k][:, c, fcl * 128:(fcl + 1) * 128], sc[:, c, :], c == 0, c == 7,
                               [("m_wa", k, c), "m_sc"], [PB(1)])
                    for c in range(8):
                        b.res.setdefault(("m_wa", k, c), {"w": None, "r": []})
                tt(modv[:], pball[:, 1, 0:96].rearrange("p (f s) -> p f s", s=2), bad[:].unsqueeze(2).broadcast_to([128, 48, 2]),
                   ALU.add, ["m_bad"], [PB(1), "m_mod"])
                g_bc = lambda k: ng[:, k, :].unsqueeze(2).broadcast_to([128, 8, 2])
                stt(MV[:, 0], modv[:, 8:16, :], 1.0, g_bc(0), ALU.add, ALU.mult, ["m_mod", "m_ng"], ["MV"])
                cp(MV[:, 1], modv[:, 0:8, :], ["m_mod"], ["MV"])
                tt(MV[:, 2], modv[:, 16:24, :], g_bc(1), ALU.mult, ["m_mod", "m_ng"], ["MV"])
                stt(MV[:, 3], modv[:, 32:40, :], 1.0, g_bc(2), ALU.add, ALU.mult, ["m_mod", "m_ng"], ["MV"])
                cp(MV[:, 4], modv[:, 24:32, :], ["m_mod"], ["MV"])
                tt(MV[:, 5], modv[:, 40:48, :], g_bc(3), ALU.mult, ["m_mod", "m_ng"], ["MV"])
                b.barrier()

            with contextlib.ExitStack() as ps:
                T = lambda name, shape, dt=F32: ps.enter_context(nc.sbuf_tensor(name + "_%d" % next(_UID), list(shape), dt))
                Win = T("p1_win", (128, 8, NCH * 128), BF16)
                ht = [T("p1_ht%d" % k, (128, 8, 512)) for k in range(2)]
                sq = T("p1_sq", (128, 8, 512)); rstd = T("p1_rstd", (128, 512))
                aT = [T("p1_aT%d" % k, (128, 8, 512), BF16) for k in range(2)]
                stg = [T("p1_st%d" % k, (128, 512)) for k in range(4)]
                for c in range(8):
                    for hf in range(2):
                        b.dma("pool", Win[:, c, hf * 1920:(hf + 1) * 1920], win_d[L, c * 128:(c + 1) * 128, hf * 1920:(hf + 1) * 1920], W=["p1_win"])
                for bi, (t0, TB) in enumerate(blocks512):
                    k = bi % 2
                    s = 1 if t0 < LCTX else 0
                    b.dma("sp", ht[k][:, :, :TB], HT[:, t0:t0 + TB].rearrange("(c p) t -> p c t", p=128), R=["HT"], W=["p1_ht%d" % k])
                    norm_mod(ht[k], TB, sq, rstd, aT[k], 0, 1, s, "p1_", ["p1_ht%d" % k])
                    for fc in range(NCH):
                        bank = 1 + fc % 6
                        for c in range(8):
                            mm(pball[:, bank, :TB], Win[:, c, fc * 128:(fc + 1) * 128], aT[k][:, c, :TB], c == 0, c == 7,
                               ["p1_win", "p1_aT"], [PB(bank)])
                        sk = fc % 4
                        cp(stg[sk][:, :TB], pball[:, bank, :TB], [], [PB(bank), "p1_st%d" % sk], eng="act" if fc % 2 else "dve")
                        b.dma("sp", PT[fc * 128:(fc + 1) * 128, t0:t0 + TB], stg[sk][:, :TB], R=["p1_st%d" % sk], W=["PT"])
                b.barrier()

            with contextlib.ExitStack() as ps:
                T = lambda name, shape, dt=F32: ps.enter_context(nc.sbuf_tensor(name + "_%d" % next(_UID), list(shape), dt))
                rcos = T("a_cos", (128, LLAT)); rsin = T("a_sin", (128, LLAT))
                qrT = T("a_qrT", (128, 2, NT), BF16); krT = T("a_krT", (128, NT), BF16)
                vtok = T("a_vtok", (128, NTILE, 2, 65), BF16)
                pin = T("a_pin", (128, 7, 512)); t1 = T("a_t1", (128, 512)); t2 = T("a_t2", (128, 512))
                identb = T("a_idb", (128, 128), BF16); negm = T("a_negm", (128, 2, 2, 128)); negmb = T("a_negmb", (128, 2, 2, 128), BF16)
                esink = T("a_esink", (128, 4))
                pT = T("a_pT", (128, 2, 5, 2, 128), BF16)
                o4 = T("a_o4", (128, 4, 64)); den = T("a_den", (128, 4)); ostg = T("a_ostg", (128, 2, 128))
                b.dma("sp", rcos[:], cd["rcos"], W=["a_cos"]); b.dma("sp", rsin[:], cd["rsin"], W=["a_sin"])
                b.dma("sp", negm[:], cd["att_neg"].rearrange("m p h q -> p m h q"), W=["a_negm"])
                cp(negmb[:], negm[:], ["a_negm"], ["a_negmb"]); cp(identb[:], ident[:], ["ident"], ["a_idb"])
                b.dma("sp", esink[:], sink_d[L].partition_broadcast(128), W=["a_esink"])
                act(esink[:], esink[:], AF.Exp, ["a_esink"], ["a_esink"])
                memset(vtok[:, :, :, 64:65], 1.0, ["a_vtok"])
                for (t0, TB) in blocks512:
                    b.dma("sp", pin[:, :, :TB], PT[0:7 * 128, t0:t0 + TB].rearrange("(c p) t -> p c t", p=128), R=["PT"], W=["a_pin"])
                    if t0 < LCTX:
                        for hh in range(2):
                            cp(qrT[:, hh, t0:t0 + TB], pin[:, CH_QA + hh, :TB], ["a_pin"], ["a_qrT"])
                        cp(krT[:, t0:t0 + TB], pin[:, CH_K, :TB], ["a_pin"], ["a_krT"], eng="act")
                    else:
                        l0 = t0 - LCTX
                        for (src, srcp, dst, key) in ((CH_QA, CH_QAP, qrT[:, 0, t0:t0 + TB], "a_qrT"), (CH_QB, CH_QBP, qrT[:, 1, t0:t0 + TB], "a_qrT"),
                                                      (CH_K, CH_KP, krT[:, t0:t0 + TB], "a_krT")):
                            tt(t1[:, :TB], pin[:, src, :TB], rcos[:, l0:l0 + TB], ALU.mult, ["a_pin", "a_cos"], ["a_t1"])
                            tt(t2[:, :TB], pin[:, srcp, :TB], rsin[:, l0:l0 + TB], ALU.mult, ["a_pin", "a_sin"], ["a_t2"], eng="pool")
                            tt(dst, t1[:, :TB], t2[:, :TB], ALU.add, ["a_t1", "a_t2"], [key])
                    for j in range(TB // 128):
                        tile = t0 // 128 + j
                        tr(pball[:, 7, 0:128], pin[:, CH_V, j * 128:(j + 1) * 128], ["a_pin"], [PB(7)])
                        cp(vtok[:, tile, :, 0:64], pball[:, 7, 0:128].rearrange("p (k d) -> p k d", k=2), [], [PB(7), "a_vtok"])
                for qt in range(NTILE):
                    if qt < 2 and not need_ctx:
                        continue
                    if qt < 2:
                        keys = [(0, None), (1, None)]
                    else:
                        keys = [(0, None), (1, None), (qt, None)]
                        if qt > 2:
                            keys.append((qt - 1, 0))
                        if qt < NTILE - 1:
                            keys.append((qt + 1, 1))
                    nk = len(keys)
                    qs = slice(qt * 128, (qt + 1) * 128)
                    for kv in range(2):
                        rg = 64 * kv
                        pr = slice(64 * kv, 64 * kv + 64)
                        for s_, (kt, mk) in enumerate(keys):
                            bank = kv * 3 + s_ // 2
                            off = (s_ % 2) * 256
                            o_ap = pball[:, bank, off:off + 256]
                            mm(o_ap, krT[pr, kt * 128:(kt + 1) * 128], qrT[pr, :, qs], True, mk is None, ["a_krT", "a_qrT"], [PB(bank)], rg=rg)
                            if mk is not None:
                                mm(o_ap, identb[:], negmb[:, mk].rearrange("p h q -> p (h q)"), False, True, ["a_idb", "a_negmb"], [PB(bank)])
                        for bk in range((nk + 1) // 2):
                            ns = min(2, nk - 2 * bk)
                            act(pT[:, kv, 2 * bk:2 * bk + ns].rearrange("p s h q -> p (s h q)"), pball[:, kv * 3 + bk, 0:ns * 256], AF.Exp,
                                [], [PB(kv * 3 + bk), ("a_pT", kv)], scale=0.125)
                        for hh in range(2):
                            h = 2 * kv + hh
                            for s_, (kt, mk) in enumerate(keys):
                                mm(pball[:, 6, h * 65:(h + 1) * 65], pT[:, kv, s_, hh, :], vtok[:, kt, kv, :], s_ == 0, s_ == nk - 1,
                                   [("a_pT", kv), "a_vtok"], [PB(6)])
                    o65 = pball[:, 6, 0:260].rearrange("p (h e) -> p h e", h=4)
                    tt(den[:], o65[:, :, 64], esink[:], ALU.add, ["a_esink"], [PB(6), "a_den"])
                    recip(den[:], den[:], ["a_den"], ["a_den"])
                    tt(o4[:], o65[:, :, 0:64], den[:].unsqueeze(2).broadcast_to([128, 4, 64]), ALU.mult, ["a_den"], [PB(6), "a_o4"])
                    o4f = o4[:].rearrange("p h d -> p (h d)")
                    for cc in range(2):
                        tr(pball[:, 7, cc * 128:(cc + 1) * 128], o4f[:, cc * 128:(cc + 1) * 128], ["a_o4"], [PB(7)])
                    cp(ostg[:], pball[:, 7, 0:256].rearrange("p (c t) -> p c t", c=2), [], [PB(7), "a_ostg"], eng="act")
                    b.dma("sp", MIXT[0:256, qs].rearrange("(c p) t -> p c t", p=128), ostg[:], R=["a_ostg"], W=["MIXT"])
                b.barrier()
            ssd_phase(L, need_ctx)
            gdn_phase(L, need_ctx)
            mlp_phase(L, need_ctx)

        with contextlib.ExitStack() as ps:
            T = lambda name, shape, dt=F32: ps.enter_context(nc.sbuf_tensor(name + "_%d" % next(_UID), list(shape), dt))
            hin = [T("f_h%d" % k, (128, 8, 128)) for k in range(2)]
            xo = [T("f_o%d" % k, (128, D)) for k in range(2)]
            for t in range(2, NTILE):
                k = t % 2
                b.dma("sp", hin[k][:], HT[:, t * 128:(t + 1) * 128].rearrange("(c p) t -> p c t", p=128), R=["HT"], W=["f_h%d" % k])
                for c in range(8):
                    bank = (t % 2) * 2 + c // 4
                    tr(pball[:, bank, (c % 4) * 128:(c % 4 + 1) * 128], hin[k][:, c, :], ["f_h%d" % k], [PB(bank)])
                for hf in range(2):
                    bank = (t % 2) * 2 + hf
                    cp(xo[k][:, hf * 512:(hf + 1) * 512], pball[:, bank, :], [], [PB(bank), "f_o%d" % k], eng="act" if hf else "dve")
                b.dma("sp", out_d[(t - 2) * 128:(t - 1) * 128, :], xo[k][:], R=["f_o%d" % k], W=["out"])
            b.barrier()
    return nc


_CACHE = {}


def kernel(**inputs):
    inp = {k: np.asarray(v) for k, v in inputs.items()}
    depth = 4
    if "nc" not in _CACHE:
        _CACHE["nc"] = build(depth)
    nc = _CACHE["nc"]
    perm = _win_perm()
    w_in = inp["w_in"]
    w_in_p = np.zeros((w_in.shape[0], D, NCH * 128), np.float32)
    ok = perm >= 0
    w_in_p[:, :, ok] = w_in[:, :, perm[ok]]
    shared = {
        "w_ada": inp["w_ada"], "b_ada": inp["b_ada"], "norm_g": inp["norm_g"], "w_in": w_in_p, "w_out": inp["w_out"],
        "attn_sink": inp["attn_sink"], "ssd_conv_w": inp["ssd_conv_w"], "ssd_conv_b": inp["ssd_conv_b"],
        "ssd_A_log": inp["ssd_A_log"].reshape(4, 12), "ssd_dt_bias": inp["ssd_dt_bias"].reshape(4, 12), "ssd_D": inp["ssd_D"],
        "ssd_norm_g": inp["ssd_norm_g"], "dn_conv_w": inp["dn_conv_w"], "dn_A_log": inp["dn_A_log"].reshape(4, 12),
        "dn_dt_bias": inp["dn_dt_bias"].reshape(4, 12), "dn_norm_g": inp["dn_norm_g"], "w_mlp1": inp["w_mlp1"], "w_mlp2": inp["w_mlp2"],
    }
    shared = {k: np.ascontiguousarray(v, dtype=np.float32) for k, v in shared.items()}
    for k, v in _consts().items():
        shared["c_" + k] = np.ascontiguousarray(v, dtype=np.float32)
    in_maps = []
    for c in range(8):
        m = dict(shared)
        m["x"] = np.ascontiguousarray(inp["x"][c], dtype=np.float32)
        m["ctx"] = np.ascontiguousarray(inp["ctx"][c], dtype=np.float32)
        m["cvec"] = np.ascontiguousarray(np.stack([inp["c"][c], inp["c_ctx"]]), dtype=np.float32)
        in_maps.append(m)
    res = run_bass_kernel_spmd(nc, in_maps, core_ids=list(range(8)))
    return np.stack([np.asarray(r["out"], dtype=np.float32) for r in res.results], axis=0)
```

```python
import contextlib
import math
import numpy as np
import concourse.bass as bass
import concourse.mybir as mybir
from concourse.bass_utils import run_bass_kernel_spmd

F32 = mybir.dt.float32
BF16 = mybir.dt.bfloat16
AF = mybir.ActivationFunctionType
ALU = mybir.AluOpType
AX = mybir.AxisListType

import itertools
_UID = itertools.count()
EPOCH = 30000
NEPOCH = 10
D = 1024
LCTX = 256
LLAT = 4096
NT = LCTX + LLAT
NTILE = NT // 128
NCH = 30
NEG = -30000.0
EPS = 1e-6


class Bld:
    CE = ("pe", "act", "dve", "pool")

    def __init__(self, nc, stack):
        self.nc = nc
        self.engs = {"pe": nc.tensor, "act": nc.scalar, "dve": nc.vector, "pool": nc.gpsimd, "sp": nc.sync}
        self.cnt = {e: 0 for e in self.CE}
        self.sems = {e: [stack.enter_context(nc.semaphore(f"s_{e}_{k}")) for k in range(NEPOCH)] for e in self.CE}
        self.dq = {}
        for q, n in (("sp", 16), ("pool", 6)):
            self.dq[q] = {"sems": [stack.enter_context(nc.semaphore(f"d_{q}_{k}")) for k in range(n)],
                          "val": [0] * n, "next": 0}
        self.waited = {}
        self.dwaited = {}
        self.res = {}
        self.pe_rg = {}

    def _wait(self, eng, tok, rg=0):
        E = self.engs[eng]
        if tok[0] == "e":
            _, e2, c2 = tok
            if e2 == eng and eng == "pe" and self.pe_rg.get(c2, 0) == rg:
                return
            if self.waited.get((eng, e2), 0) >= c2:
                return
            ep, v = (c2 - 1) // EPOCH, (c2 - 1) % EPOCH + 1
            E.wait_ge(self.sems[e2][ep], v)
            self.waited[(eng, e2)] = c2
        else:
            _, q, idx, v = tok
            if self.dwaited.get((eng, q, idx), 0) >= v:
                return
            E.wait_ge(self.dq[q]["sems"][idx], v)
            self.dwaited[(eng, q, idx)] = v

    def _sync(self, eng, reads, writes, rg=0):
        toks = []
        for k in reads:
            r = self.res.get(k)
            if r and r["w"] is not None:
                toks.append(r["w"])
        for k in writes:
            r = self.res.get(k)
            if r:
                if r["w"] is not None:
                    toks.append(r["w"])
                toks.extend(r["r"])
        for t in toks:
            self._wait(eng, t, rg)

    def _mark(self, tok, reads, writes):
        for k in reads:
            r = self.res.setdefault(k, {"w": None, "r": []})
            r["r"].append(tok)
            if len(r["r"]) > 48:
                r["r"] = self._prune(r["r"])
        for k in writes:
            self.res[k] = {"w": tok, "r": []}

    @staticmethod
    def _prune(toks):
        best = {}
        for t in toks:
            key = (t[0], t[1]) if t[0] == "e" else (t[0], t[1], t[2])
            v = t[2] if t[0] == "e" else t[3]
            old = best.get(key)
            if old is None or v > (old[2] if old[0] == "e" else old[3]):
                best[key] = t
        return list(best.values())

    def op(self, eng, fn, R=(), W=(), rg=0):
        self._sync(eng, R, W, rg)
        inst = fn(self.engs[eng])
        self.cnt[eng] += 1
        c = self.cnt[eng]
        if eng == "pe" and rg:
            self.pe_rg[c] = rg
        inst.then_inc(self.sems[eng][(c - 1) // EPOCH], 1)
        self._mark(("e", eng, c), R, W)
        return inst

    def dma(self, q, out, in_, R=(), W=(), **kw):
        self._sync(q, R, W)
        d = self.dq[q]
        idx = d["next"]
        d["next"] = (idx + 1) % len(d["sems"])
        if d["val"][idx] > 0:
            self._wait(q, ("d", q, idx, d["val"][idx]))
        inst = self.engs[q].dma_start(out=out, in_=in_, **kw)
        d["val"][idx] += 16
        inst.then_inc(d["sems"][idx], 16)
        tok = ("d", q, idx, d["val"][idx])
        self._mark(tok, R, W)
        return tok

    def barrier(self):
        for eng in ("pe", "act", "dve", "pool", "sp"):
            for e2 in self.CE:
                if self.cnt[e2] > 0:
                    self._wait(eng, ("e", e2, self.cnt[e2]))
            for q, d in self.dq.items():
                for idx, v in enumerate(d["val"]):
                    if v > 0:
                        self._wait(eng, ("d", q, idx, v))
        self.res = {}


def _consts():
    c = {}
    i = np.arange(128)
    c["ident"] = np.eye(128, dtype=np.float32)
    c["ones"] = np.ones((128, 128), np.float32)
    blk = (i[:, None] // 64) == (i[None, :] // 64)
    tri_f = (i[:, None] <= i[None, :])
    tri_b = (i[:, None] >= i[None, :])
    c["tri"] = np.stack([tri_f, tri_b]).astype(np.float32)
    c["tris"] = np.stack([i[:, None] > i[None, :], i[:, None] < i[None, :]]).astype(np.float32)
    c["onesblk"] = blk.astype(np.float32)
    c["triblk"] = np.stack([tri_f & blk, tri_b & blk]).astype(np.float32)
    c["trisblk"] = np.stack([(i[:, None] > i[None, :]) & blk, (i[:, None] < i[None, :]) & blk]).astype(np.float32)
    def m(al):
        return np.where(al, 0.0, NEG).astype(np.float32)
    ssd_m = np.stack([m(i[None, :] >= i[:, None]), m(i[None, :] <= i[:, None])])
    c["ssd_negh"] = np.repeat(ssd_m[:, :, None, :], 6, axis=2).transpose(1, 0, 2, 3).reshape(128, 12, 128).copy()
    g_dt = np.stack([m((i[None, :] >= i[:, None]) & blk), m((i[None, :] <= i[:, None]) & blk)])
    g_d = np.stack([m((i[:, None] > i[None, :]) & blk), m((i[:, None] < i[None, :]) & blk)])
    c["gdn_negdt"] = np.repeat(g_dt[:, :, None, :], 6, axis=2).copy()
    c["gdn_negd"] = np.repeat(g_d[:, :, None, :], 6, axis=2).copy()
    c["gdn_s01"] = (g_d == 0.0).astype(np.float32)
    chs = np.zeros((2, 128, 64), np.float32)
    chs[0, :64] = 1.0
    chs[1, 64:] = 1.0
    c["chsel"] = chs
    am = np.stack([m(i[:, None] >= i[None, :]), m(i[:, None] <= i[None, :])])
    c["att_neg"] = np.repeat(am[:, :, None, :], 2, axis=2).copy()
    t = np.arange(LLAT)
    row = (t // 64).astype(np.float64)
    col = (t % 64).astype(np.float64)
    inv = 10000.0 ** (-np.arange(0, 32, 2, dtype=np.float64) / 32.0)
    cos = np.zeros((64, LLAT)); sin = np.zeros((64, LLAT))
    for d in range(64):
        pos = row if d < 32 else col
        idx = d % 32
        ang = pos * inv[idx % 16]
        cos[d] = np.cos(ang)
        sin[d] = -np.sin(ang) if idx < 16 else np.sin(ang)
    c["rcos"] = np.concatenate([cos, cos]).astype(np.float32)
    c["rsin"] = np.concatenate([sin, sin]).astype(np.float32)
    return c


def _win_perm():
    def partner(d):
        return d + 16 if (d % 32) < 16 else d - 16
    cols = []
    aq = lambda h, d: h * 64 + d
    for hs in ((0, 2), (1, 3)):
        cols += [aq(h, d) for h in hs for d in range(64)]
    for hs in ((0, 2), (1, 3)):
        cols += [aq(h, partner(d)) for h in hs for d in range(64)]
    cols += [256 + kv * 64 + d for kv in range(2) for d in range(64)]
    cols += [256 + kv * 64 + partner(d) for kv in range(2) for d in range(64)]
    cols += list(range(384, 512))
    cols += list(range(512, 1408))
    cols += list(range(1408, 1792))
    cols += list(range(1804, 2956))
    cols += list(range(2956, 3340))
    small = list(range(1792, 1804)) + list(range(3340, 3352)) + list(range(3352, 3364))
    cols += small + [-1] * (128 - len(small))
    assert len(cols) == NCH * 128
    return np.array(cols)


CH_QA, CH_QB, CH_QAP, CH_QBP, CH_K, CH_KP, CH_V = 0, 1, 2, 3, 4, 5, 6
CH_SX, CH_SB, CH_SC, CH_SZ = 7, 10, 12, 14
CH_GQ, CH_GK, CH_GV, CH_GZ = 17, 20, 23, 26
CH_SM = 29


def build(depth=4, dbg=False, skip=()):
    nc = bass.Bass("TRN2", target_bir_lowering=False)
    C = _consts()
    dt_in = lambda name, shape: nc.dram_tensor(name, list(shape), F32, kind="ExternalInput").ap()
    x_d = dt_in("x", (LLAT, D)); ctx_d = dt_in("ctx", (LCTX, D)); cv_d = dt_in("cvec", (2, D))
    wada_d = dt_in("w_ada", (4, D, 6 * D)); bada_d = dt_in("b_ada", (4, 6 * D)); ng_d = dt_in("norm_g", (4, 4, D))
    win_d = dt_in("w_in", (4, D, NCH * 128)); wout_d = dt_in("w_out", (4, D, D))
    sink_d = dt_in("attn_sink", (4, 4))
    scw_d = dt_in("ssd_conv_w", (4, 896, 5)); scb_d = dt_in("ssd_conv_b", (4, 896))
    sal_d = dt_in("ssd_A_log", (4, 12)); sdb_d = dt_in("ssd_dt_bias", (4, 12)); sD_d = dt_in("ssd_D", (4, 6))
    sng_d = dt_in("ssd_norm_g", (4, 384))
    gcw_d = dt_in("dn_conv_w", (4, 1152, 5)); gal_d = dt_in("dn_A_log", (4, 12)); gdb_d = dt_in("dn_dt_bias", (4, 12))
    gng_d = dt_in("dn_norm_g", (4, 64))
    w1_d = dt_in("w_mlp1", (4, D, 4 * D)); w2_d = dt_in("w_mlp2", (4, 4 * D, D))
    cd = {k: dt_in("c_" + k, v.shape) for k, v in C.items()}
    out_d = nc.dram_tensor("out", [LLAT, D], F32, kind="ExternalOutput").ap()
    skind = "ExternalOutput" if dbg else "Internal"
    HT = nc.dram_tensor("HT", [D, NT], F32, kind=skind).ap()
    PT = nc.dram_tensor("PT", [NCH * 128, NT], F32, kind=skind).ap()
    MIXT = nc.dram_tensor("MIXT", [D, NT], F32, kind=skind).ap()
    GDTOK = nc.dram_tensor("GDTOK", [NTILE, 128, 1152], F32, kind="Internal").ap()

    with contextlib.ExitStack() as st:
        b = Bld(nc, st)
        pball = st.enter_context(nc.psum_tensor("pball", [128, 8, 512], F32))
        PB = lambda k: "pb%d" % k
        gT = lambda name, shape, dt=F32: st.enter_context(nc.sbuf_tensor(name + "_%d" % next(_UID), list(shape), dt))
        ident = gT("ident", (128, 128)); ones = gT("ones", (128, 128))
        MV = gT("MV", (128, 6, 8, 2))
        b.dma("sp", ident[:], cd["ident"], W=["ident"])
        b.dma("sp", ones[:], cd["ones"], W=["ones"])

        def act(out, in_, func, R, W, **kw):
            return b.op("act", lambda e: e.activation(out=out, in_=in_, func=func, **kw), R, W)

        def mm(out, lhsT, rhs, start, stop, R, W, rg=0):
            return b.op("pe", lambda e: e.matmul(out, lhsT=lhsT, rhs=rhs, start=start, stop=stop), R, W, rg)

        def tr(out, in_, R, W, idn=None):
            idn = ident[:] if idn is None else idn
            return b.op("pe", lambda e: e.transpose(out=out, in_=in_, identity=idn), list(R) + ["ident"], W)

        def tt(out, in0, in1, op, R, W, eng="dve"):
            return b.op(eng, lambda e: e.tensor_tensor(out=out, in0=in0, in1=in1, op=op), R, W)

        def ts(out, in0, s1, s2, op0, op1, R, W, eng="dve"):
            if op1 is None:
                return b.op(eng, lambda e: e.tensor_scalar(out=out, in0=in0, scalar1=s1, scalar2=None, op0=op0), R, W)
            return b.op(eng, lambda e: e.tensor_scalar(out=out, in0=in0, scalar1=s1, scalar2=s2, op0=op0, op1=op1), R, W)

        def stt(out, in0, scalar, in1, op0, op1, R, W):
            return b.op("dve", lambda e: e.scalar_tensor_tensor(out=out, in0=in0, scalar=scalar, in1=in1, op0=op0, op1=op1), R, W)

        def cp(out, in_, R, W, eng="dve"):
            if eng == "act":
                return act(out, in_, AF.Copy, R, W)
            return b.op(eng, lambda e: e.tensor_copy(out=out, in_=in_), R, W)

        def recip(out, in_, R, W):
            return b.op("dve", lambda e: e.reciprocal(out=out, in_=in_), R, W)

        def memset(ap, val, W, eng="pool"):
            return b.op(eng, lambda e: e.memset(ap, val), (), W)

        def rstd_from(out, ssq, scale, R, W):
            act(out, ssq, AF.Sqrt, R, W, scale=scale, bias=epsc[:, 0:1])
            recip(out, out, W, W)

        epsc = gT("epsc", (128, 1))
        memset(epsc[:], EPS, ["epsc"])

        with contextlib.ExitStack() as ps:
            T = lambda name, shape, dt=F32: ps.enter_context(nc.sbuf_tensor(name + "_%d" % next(_UID), list(shape), dt))
            xin = [T("i_x%d" % k, (128, D)) for k in range(2)]
            xo = [T("i_o%d" % k, (128, 8, 128)) for k in range(2)]
            for t in range(NTILE):
                k = t % 2
                src = ctx_d[t * 128:(t + 1) * 128, :] if t < 2 else x_d[(t - 2) * 128:(t - 1) * 128, :]
                b.dma("sp", xin[k][:], src, W=["i_x%d" % k])
                for c in range(8):
                    bank = (t % 2) * 2 + c // 4
                    tr(pball[:, bank, (c % 4) * 128:(c % 4 + 1) * 128], xin[k][:, c * 128:(c + 1) * 128], ["i_x%d" % k], [PB(bank)])
                for hf in range(2):
                    bank = (t % 2) * 2 + hf
                    cp(xo[k][:, hf * 4:(hf + 1) * 4, :], pball[:, bank, :].rearrange("p (c t) -> p c t", c=4), [], [PB(bank), "i_o%d" % k],
                       eng="act" if hf else "dve")
                b.dma("sp", HT[:, t * 128:(t + 1) * 128].rearrange("(c p) t -> p c t", p=128), xo[k][:], R=["i_o%d" % k], W=["HT"])
            b.barrier()

        blocks512 = [(0, 256)] + [(256 + 512 * i, 512) for i in range(8)]
        blocks256 = [(256 * i, 256) for i in range(17)]

        def norm_mod(ht, TB, sq, rstd, aT, kmul, kshift, s, tag, R):
            act(sq[:, :, :TB], ht[:, :, :TB], AF.Square, R, [tag + "sq"])
            for c in range(8):
                mm(pball[:, 0, :TB], ones[:], sq[:, c, :TB], c == 0, c == 7, [tag + "sq", "ones"], [PB(0)])
            rstd_from(rstd[:, :TB], pball[:, 0, :TB], 1.0 / D, [], [PB(0), tag + "rstd"])
            tt(sq[:, :, :TB], ht[:, :, :TB], rstd[:, :TB].unsqueeze(1).broadcast_to([128, 8, TB]), ALU.mult,
               list(R) + [tag + "rstd"], [tag + "sq"])
            for c in range(8):
                act(aT[:, c, :TB], sq[:, c, :TB], AF.Identity, [tag + "sq", "MV"], [tag + "aT"],
                    scale=MV[:, kmul, c, s:s + 1], bias=MV[:, kshift, c, s:s + 1])

        def post_norm_add(ht, yT, TB, sq, rstd, kg, s, tag):
            act(sq[:, :, :TB], yT[:, :, :TB], AF.Square, [tag + "yT"], [tag + "sq"])
            for c in range(8):
                mm(pball[:, 0, :TB], ones[:], sq[:, c, :TB], c == 0, c == 7, [tag + "sq", "ones"], [PB(0)])
            rstd_from(rstd[:, :TB], pball[:, 0, :TB], 1.0 / D, [], [PB(0), tag + "rstd"])
            tt(sq[:, :, :TB], yT[:, :, :TB], rstd[:, :TB].unsqueeze(1).broadcast_to([128, 8, TB]), ALU.mult,
               [tag + "yT", tag + "rstd"], [tag + "sq"])
            for c in range(8):
                stt(ht[:, c, :TB], sq[:, c, :TB], MV[:, kg, c, s:s + 1], ht[:, c, :TB], ALU.mult, ALU.add,
                    [tag + "sq", "MV"], [tag + "ht"])

        def conv_seg(xin, acc, rows, t0, Lb, seg_lo, seg_hi, cw, cbias, ch, tag):
            lo = max(t0 - 2, seg_lo); hi = min(t0 + Lb + 2, seg_hi)
            if lo > t0 - 2:
                memset(xin[:, 0:2], 0.0, [tag + "xin"])
            if hi < t0 + Lb + 2:
                memset(xin[:, Lb + 2:Lb + 4], 0.0, [tag + "xin"])
            b.dma("sp", xin[:, lo - (t0 - 2):hi - (t0 - 2)], PT[rows, lo:hi], R=["PT"], W=[tag + "xin"])
            if cbias is None:
                ts(acc[:, :Lb], xin[:, 0:Lb], cw[:, ch, 0:1], None, ALU.mult, None, [tag + "xin", tag + "cw"], [tag + "acc"])
            else:
                ts(acc[:, :Lb], xin[:, 0:Lb], cw[:, ch, 0:1], cbias[:, ch:ch + 1], ALU.mult, ALU.add, [tag + "xin", tag + "cw"], [tag + "acc"])
            for j in range(1, 5):
                stt(acc[:, :Lb], xin[:, j:j + Lb], cw[:, ch, j:j + 1], acc[:, :Lb], ALU.mult, ALU.add, [tag + "xin", tag + "cw"], [tag + "acc"])

        def decay(E_out, Ekey, g1, g2, dirs, lhs_ones, TRI, negh, banks, rhs1, rhs2, tag, R):
            nh = len(dirs)
            d0 = 0
            while d0 < nh:
                d1 = d0
                while d1 < nh and dirs[d1] == dirs[d0]:
                    d1 += 1
                n = d1 - d0
                tt(rhs1[:, d0:d1, :], g1[:, d0:d1].unsqueeze(2).broadcast_to([128, n, 128]),
                   TRI[dirs[d0]].unsqueeze(1).broadcast_to([128, n, 128]), ALU.mult, list(R) + [tag + "c"], [tag + "rhs1"])
                d0 = d1
            cp(rhs2[:, 0:nh, :], g2[:, 0:nh].unsqueeze(2).broadcast_to([128, nh, 128]), R, [tag + "rhs2"], eng="pool")
            for bi, bank in enumerate(banks):
                h0 = 4 * bi; h1 = min(nh, h0 + 4); n = h1 - h0
                if n <= 0:
                    break
                mm(pball[:, bank, 0:n * 128], lhs_ones, rhs1[:, h0:h1, :].rearrange("p h f -> p (h f)"), True, False,
                   [tag + "rhs1", tag + "c"], [PB(bank)])
                a0 = h0
                while a0 < h1:
                    a1 = a0
                    while a1 < h1 and dirs[a1] == dirs[a0]:
                        a1 += 1
                    mm(pball[:, bank, (a0 - h0) * 128:(a1 - h0) * 128], TRI[dirs[a0]], rhs2[:, a0:a1, :].rearrange("p h f -> p (h f)"),
                       False, False, [tag + "rhs2", tag + "c"], [PB(bank)])
                    a0 = a1
                mm(pball[:, bank, 0:n * 128], ident[:], negh[:, h0:h1, :].rearrange("p h f -> p (h f)"), False, True,
                   ["ident", tag + "c"], [PB(bank)])
                act(E_out[:, h0:h1, :].rearrange("p h f -> p (h f)"), pball[:, bank, 0:n * 128], AF.Exp, [], [PB(bank), Ekey])

        def ssd_phase(L, need_ctx):
            with contextlib.ExitStack() as ps:
                T = lambda name, shape, dt=F32: ps.enter_context(nc.sbuf_tensor(name + "_%d" % next(_UID), list(shape), dt))
                xtok = T("s_xtok", (128, NTILE, 384), BF16); Btok = T("s_Btok", (128, NTILE, 256), BF16)
                BT = T("s_BT", (128, 2, NT), BF16); CT = T("s_CT", (128, 2, NT), BF16)
                dttok = T("s_dttok", (128, NTILE, 12)); atok = T("s_atok", (128, NTILE, 12)); natok = T("s_natok", (128, NTILE, 12))
                cw = T("s_cw", (128, 7, 5)); cb = T("s_cb", (128, 7)); Abc = T("s_Abc", (128, 12)); dtb = T("s_dtb", (12, 1))
                Dbc = T("s_Dbc", (128, 6)); sng = T("s_sng", (128, 3))
                with nc.allow_non_contiguous_dma(reason="tiny vectors"):
                    b.dma("sp", cw[:], scw_d[L].rearrange("(c p) j -> p c j", p=128), W=["s_cw"])
                    b.dma("sp", cb[:], scb_d[L].rearrange("(c p) -> p c", p=128), W=["s_cw"])
                    b.dma("sp", dtb[:], sdb_d[L].rearrange("(p o) -> p o", o=1), W=["s_dtb"])
                    b.dma("sp", sng[:], sng_d[L].rearrange("(c p) -> p c", p=128), W=["s_sng"])
                b.dma("sp", Abc[:], sal_d[L].partition_broadcast(128), W=["s_Abc"])
                b.dma("sp", Dbc[:], sD_d[L].partition_broadcast(128), W=["s_Dbc"])
                act(Abc[:], Abc[:], AF.Exp, ["s_Abc"], ["s_Abc"])
                with contextlib.ExitStack() as ps2:
                    T2 = lambda name, shape, dt=F32: ps2.enter_context(nc.sbuf_tensor(name + "_%d" % next(_UID), list(shape), dt))
                    xin = T2("s_xin", (128, LLAT + 4)); acc = T2("s_acc", (128, LLAT)); cvb = T2("s_cv", (128, LLAT))
                    dtT = T2("s_dtT", (12, NT))
                    for ch in range(7):
                        for (t0, Lb) in ((0, LCTX), (LCTX, LLAT)):
                            conv_seg(xin, acc, slice((CH_SX + ch) * 128, (CH_SX + ch + 1) * 128), t0, Lb, t0, t0 + Lb, cw, cb, ch, "s_")
                            if ch < 3 or ch in (3, 4):
                                act(cvb[:, :Lb], acc[:, :Lb], AF.Silu, ["s_acc"], ["s_cv"])
                                if ch in (3, 4):
                                    cp(BT[:, ch - 3, t0:t0 + Lb], cvb[:, :Lb], ["s_cv"], ["s_BT"], eng="pool")
                                for q4 in range(Lb // 512 if Lb >= 512 else 1):
                                    nt4 = min(4, Lb // 128)
                                    bank = 1 + q4 % 4
                                    for j in range(nt4):
                                        tr(pball[:, bank, j * 128:(j + 1) * 128], cvb[:, (q4 * 4 + j) * 128:(q4 * 4 + j + 1) * 128], ["s_cv"], [PB(bank)])
                                    tl0 = t0 // 128 + q4 * 4
                                    dst = xtok[:, tl0:tl0 + nt4, ch * 128:(ch + 1) * 128] if ch < 3 else Btok[:, tl0:tl0 + nt4, (ch - 3) * 128:(ch - 2) * 128]
                                    cp(dst, pball[:, bank, 0:nt4 * 128].rearrange("p (t f) -> p t f", t=nt4), [], [PB(bank), "s_xtok" if ch < 3 else "s_Btok"],
                                       eng="act" if q4 % 2 else "dve")
                            else:
                                act(CT[:, ch - 5, t0:t0 + Lb], acc[:, :Lb], AF.Silu, ["s_acc"], ["s_CT"])
                    b.dma("sp", dtT[:], PT[CH_SM * 128:CH_SM * 128 + 12, :], R=["PT"], W=["s_dtT"])
                    act(dtT[:], dtT[:], AF.Exp, ["s_dtT", "s_dtb"], ["s_dtT"], bias=dtb[:, 0:1])
                    act(dtT[:], dtT[:], AF.Ln, ["s_dtT"], ["s_dtT"], bias=1.0)
                    for t in range(NTILE):
                        tr(pball[:, 7, (t % 32) * 12:(t % 32) * 12 + 12], dtT[:, t * 128:(t + 1) * 128], ["s_dtT"], [PB(7)], idn=ident[0:12, 0:12])
                        if t == 31 or t == NTILE - 1:
                            tl0 = 0 if t == 31 else 32
                            n = t - tl0 + 1
                            cp(dttok[:, tl0:tl0 + n, :], pball[:, 7, 0:n * 12].rearrange("p (t h) -> p t h", h=12), [], [PB(7), "s_dttok"])
                    stt(atok[:], dttok[:], -1.0, Abc[:].unsqueeze(1).broadcast_to([128, NTILE, 12]), ALU.mult, ALU.mult, ["s_dttok", "s_Abc"], ["s_atok"])
                    ts(natok[:], atok[:], -1.0, None, ALU.mult, None, ["s_atok"], ["s_natok"])
                    b.barrier()
                Senter = T("s_Sent", (128, NTILE, 2, 6, 64), BF16); eacum = T("s_eacum", (128, NTILE, 12)); acum = T("s_acum", (128, NTILE, 12))
                tri = T("s_tri", (128, 2, 128)); tris = T("s_tris", (128, 2, 128)); negh = T("s_negh", (128, 12, 128))
                rhs1 = T("s_rhs1", (128, 12, 128)); rhs2 = T("s_rhs2", (128, 12, 128)); E = T("s_E", (128, 12, 128))
                MT = T("s_MT", (128, 12, 128), BF16); xdt = T("s_xdt", (128, 12, 64), BF16); xw = T("s_xw", (128, 6, 64), BF16)
                S = T("s_S", (128, 6, 64)); wexp = T("s_wexp", (128, 6)); elast = T("s_elast", (128, 6)); wv = T("s_w", (128, 6))
                tA = T("s_tA", (128, 6, 64)); tB = T("s_tB", (128, 6, 64)); zT = T("s_zT", (128, 3, 128)); sz = T("s_sz", (128, 384))
                gated = T("s_gated", (128, 384)); junk = T("s_junk", (128, 192)); ssq = T("s_ssq", (128, 2)); ostg = T("s_ostg", (128, 3, 128))
                b.dma("sp", tri[:], cd["tri"].rearrange("d p f -> p d f"), W=["s_c"])
                b.dma("sp", tris[:], cd["tris"].rearrange("d p f -> p d f"), W=["s_c"])
                b.dma("sp", negh[:], cd["ssd_negh"], W=["s_c"])
                TRI = [tri[:, 0, :], tri[:, 1, :]]
                for d in range(2):
                    order = list(range(NTILE)) if d == 0 else [1, 0] + list(range(NTILE - 1, 1, -1))
                    memset(S[:], 0.0, ["s_S"])
                    hs = slice(6 * d, 6 * d + 6)
                    for c in order:
                        mm(pball[:, 7, 0:6], tri[:, d, :], atok[:, c, hs], True, True, ["s_c", "s_atok"], [PB(7)])
                        mm(pball[:, 7, 6:12], tris[:, d, :], atok[:, c, hs], True, True, ["s_c", "s_atok"], [PB(7)])
                        mm(pball[:, 7, 12:18], ones[:], atok[:, c, hs], True, True, ["ones", "s_atok"], [PB(7)])
                        act(eacum[:, c, hs], pball[:, 7, 0:6], AF.Exp, [], [PB(7), "s_eacum"])
                        cp(acum[:, c, hs], pball[:, 7, 0:6], [], [PB(7), "s_acum"])
                        act(wexp[:], pball[:, 7, 6:12], AF.Exp, [], [PB(7), "s_wexp"])
                        act(elast[:], pball[:, 7, 12:18], AF.Exp, [], [PB(7), "s_elast"])
                        tt(wv[:], wexp[:], dttok[:, c, hs], ALU.mult, ["s_wexp", "s_dttok"], ["s_w"])
                        tt(xw[:], xtok[:, c, :].rearrange("p (h e) -> p h e", h=6), wv[:].unsqueeze(2).broadcast_to([128, 6, 64]), ALU.mult,
                           ["s_xtok", "s_w"], ["s_xw"])
                        for g in range(2):
                            mm(pball[:, 6, g * 192:(g + 1) * 192], Btok[:, c, g * 128:(g + 1) * 128],
                               xw[:, 3 * g:3 * g + 3, :].rearrange("p h e -> p (h e)"), True, True, ["s_Btok", "s_xw"], [PB(6)])
                        cp(Senter[:, c, d], S[:], ["s_S"], ["s_Sent"], eng="pool")
                        tt(S[:], S[:], elast[:].unsqueeze(2).broadcast_to([128, 6, 64]), ALU.mult, ["s_elast"], ["s_S"])
                        tt(S[:], S[:], pball[:, 6, 0:384].rearrange("p (h e) -> p h e", h=6), ALU.add, [], [PB(6), "s_S"])
                for c in range(NTILE):
                    if c < 2 and not need_ctx:
                        continue
                    cs = slice(c * 128, (c + 1) * 128)
                    b.dma("sp", zT[:], PT[CH_SZ * 128:(CH_SZ + 3) * 128, cs].rearrange("(c p) t -> p c t", p=128), R=["PT"], W=["s_zT"])
                    cp(rhs1[:], acum[:, c, :].unsqueeze(2).broadcast_to([128, 12, 128]), ["s_acum"], ["s_rhs1"], eng="pool")
                    for hd in range(12):
                        tr(pball[:, hd // 4, (hd % 4) * 128:(hd % 4 + 1) * 128], rhs1[:, hd, :], ["s_rhs1"], [PB(hd // 4)])
                    for hd in range(12):
                        stt(rhs2[:, hd, :], pball[:, hd // 4, (hd % 4) * 128:(hd % 4 + 1) * 128], acum[:, c, hd:hd + 1], negh[:, hd, :], ALU.subtract, ALU.add,
                            ["s_acum", "s_c"], [PB(hd // 4), "s_rhs2"])
                    act(E[:].rearrange("p h f -> p (h f)"), rhs2[:].rearrange("p h f -> p (h f)"), AF.Exp, ["s_rhs2"], ["s_E"])
                    for g in range(2):
                        mm(pball[:, 3, g * 128:(g + 1) * 128], BT[:, g, cs], CT[:, g, cs], True, True, ["s_BT", "s_CT"], [PB(3)])
                    for d in range(2):
                        for g in range(2):
                            h0 = d * 6 + 3 * g
                            tt(MT[:, h0:h0 + 3, :], E[:, h0:h0 + 3, :], pball[:, 3, g * 128:(g + 1) * 128].unsqueeze(1).broadcast_to([128, 3, 128]),
                               ALU.mult, ["s_E"], [PB(3), "s_MT"])
                        tt(xdt[:, 6 * d:6 * d + 6, :], xtok[:, c, :].rearrange("p (h e) -> p h e", h=6),
                           dttok[:, c, 6 * d:6 * d + 6].unsqueeze(2).broadcast_to([128, 6, 64]), ALU.mult, ["s_xtok", "s_dttok"], ["s_xdt"], eng="pool")
                    for h in range(6):
                        for d in range(2):
                            mm(pball[:, 4, h * 64:(h + 1) * 64], MT[:, d * 6 + h, :], xdt[:, d * 6 + h, :], d == 0, d == 1, ["s_MT", "s_xdt"], [PB(4)])
                    for d in range(2):
                        for g in range(2):
                            mm(pball[:, 5 + d, g * 192:(g + 1) * 192], CT[:, g, cs], Senter[:, c, d, 3 * g:3 * g + 3, :].rearrange("p h e -> p (h e)"),
                               True, True, ["s_CT", "s_Sent"], [PB(5 + d)])
                    v6 = lambda bank: pball[:, bank, 0:384].rearrange("p (h e) -> p h e", h=6)
                    tt(tA[:], v6(5), eacum[:, c, 0:6].unsqueeze(2).broadcast_to([128, 6, 64]), ALU.mult, ["s_eacum"], [PB(5), "s_tA"])
                    tt(tB[:], v6(6), eacum[:, c, 6:12].unsqueeze(2).broadcast_to([128, 6, 64]), ALU.mult, ["s_eacum"], [PB(6), "s_tB"])
                    tt(tA[:], tA[:], tB[:], ALU.add, ["s_tB"], ["s_tA"], eng="pool")
                    tt(tA[:], tA[:], v6(4), ALU.add, [], [PB(4), "s_tA"])
                    tt(tB[:], xtok[:, c, :].rearrange("p (h e) -> p h e", h=6), Dbc[:].unsqueeze(2).broadcast_to([128, 6, 64]), ALU.mult,
                       ["s_xtok", "s_Dbc"], ["s_tB"], eng="pool")
                    tt(tA[:], tA[:], tB[:], ALU.add, ["s_tB"], ["s_tA"], eng="pool")
                    for ch in range(3):
                        tr(pball[:, 7, ch * 128:(ch + 1) * 128], zT[:, ch, :], ["s_zT"], [PB(7)])
                    act(sz[:], pball[:, 7, 0:384], AF.Silu, [], [PB(7), "s_sz"])
                    tt(gated[:], tA[:].rearrange("p h e -> p (h e)"), sz[:], ALU.mult, ["s_tA", "s_sz"], ["s_gated"])
                    for g in range(2):
                        act(junk[:], gated[:, g * 192:(g + 1) * 192], AF.Square, ["s_gated"], ["s_junk", "s_ssq"], accum_out=ssq[:, g:g + 1])
                    rstd_from(ssq[:], ssq[:], 1.0 / 192, ["s_ssq"], ["s_ssq"])
                    tt(gated[:].rearrange("p (g f) -> p g f", g=2), gated[:].rearrange("p (g f) -> p g f", g=2),
                       ssq[:].unsqueeze(2).broadcast_to([128, 2, 192]), ALU.mult, ["s_ssq"], ["s_gated"])
                    for ch in range(3):
                        tr(pball[:, 7, ch * 128:(ch + 1) * 128], gated[:, ch * 128:(ch + 1) * 128], ["s_gated"], [PB(7)])
                    for ch in range(3):
                        act(ostg[:, ch, :], pball[:, 7, ch * 128:(ch + 1) * 128], AF.Copy, ["s_sng"], [PB(7), "s_ostg"], scale=sng[:, ch:ch + 1])
                    b.dma("sp", MIXT[256:640, cs].rearrange("(c p) t -> p c t", p=128), ostg[:], R=["s_ostg"], W=["MIXT"])
                b.barrier()

        def gdn_phase(L, need_ctx):
            with contextlib.ExitStack() as ps:
                T = lambda name, shape, dt=F32: ps.enter_context(nc.sbuf_tensor(name + "_%d" % next(_UID), list(shape), dt))
                gtok = T("g_gtok", (128, NTILE, 12)); ngtok = T("g_ngtok", (128, NTILE, 12)); btok = T("g_btok", (128, NTILE, 12)); nbtok = T("g_nbtok", (128, NTILE, 12))
                cw = T("g_cw", (128, 9, 5)); Abc = T("g_Abc", (12, 1)); dtb = T("g_dtb", (12, 1)); gng = T("g_gng", (128, 64))
                with nc.allow_non_contiguous_dma(reason="tiny vectors"):
                    b.dma("sp", cw[:], gcw_d[L].rearrange("(c p) j -> p c j", p=128), W=["g_cw"])
                    b.dma("sp", dtb[:], gdb_d[L].rearrange("(p o) -> p o", o=1), W=["g_dtb"])
                    b.dma("sp", Abc[:], gal_d[L].rearrange("(p o) -> p o", o=1), W=["g_Abc"])
                b.dma("sp", gng[:], gng_d[L].partition_broadcast(128), W=["g_gng"])
                act(Abc[:], Abc[:], AF.Exp, ["g_Abc"], ["g_Abc"])
                with contextlib.ExitStack() as ps2:
                    T2 = lambda name, shape, dt=F32: ps2.enter_context(nc.sbuf_tensor(name + "_%d" % next(_UID), list(shape), dt))
                    xin = T2("g_xin", (128, 1028)); acc = T2("g_acc", (128, 1024)); cv9 = T2("g_cv9", (128, 9, 1024))
                    tokb = [T2("g_tokb%d" % k, (128, 18, 64)) for k in range(2)]; sqs = T2("g_sqs", (128, 12, 64)); ssq = T2("g_ssq", (128, 12))
                    aT_ = T2("g_aT", (12, NT)); bT_ = T2("g_bT", (12, NT))
                    for (t0, Lb, lo, hi) in [(0, 256, 0, 256)] + [(256 + 1024 * i, 1024, 256, NT) for i in range(4)]:
                        for ch in range(9):
                            conv_seg(xin, acc, slice((CH_GQ + ch) * 128, (CH_GQ + ch + 1) * 128), t0, Lb, lo, hi, cw, None, ch, "g_")
                            act(cv9[:, ch, :Lb], acc[:, :Lb], AF.Silu, ["g_acc"], ["g_cv9"])
                        for j in range(Lb // 128):
                            tile = t0 // 128 + j
                            k = tile % 2
                            for ch in range(9):
                                bank = 1 + ch // 3
                                tr(pball[:, bank, (ch % 3) * 128:(ch % 3 + 1) * 128], cv9[:, ch, j * 128:(j + 1) * 128], ["g_cv9"], [PB(bank)])
                            for qk in range(2):
                                act(sqs[:, qk * 6:(qk + 1) * 6, :].rearrange("p h e -> p (h e)"), pball[:, 1 + qk, 0:384], AF.Square, [], [PB(1 + qk), "g_sqs"])
                            b.op("dve", lambda e: e.tensor_reduce(out=ssq[:], in_=sqs[:], axis=AX.X, op=ALU.add), ["g_sqs"], ["g_ssq"])
                            rstd_from(ssq[:], ssq[:], 1.0, ["g_ssq"], ["g_ssq"])
                            ts(ssq[:, 0:6], ssq[:, 0:6], 0.125, None, ALU.mult, None, ["g_ssq"], ["g_ssq"])
                            for qk in range(2):
                                tt(tokb[k][:, qk * 6:(qk + 1) * 6, :], pball[:, 1 + qk, 0:384].rearrange("p (h e) -> p h e", h=6),
                                   ssq[:, qk * 6:(qk + 1) * 6].unsqueeze(2).broadcast_to([128, 6, 64]), ALU.mult, ["g_ssq"], [PB(1 + qk), "g_tokb%d" % k])
                            cp(tokb[k][:, 12:18, :].rearrange("p h e -> p (h e)"), pball[:, 3, 0:384], [], [PB(3), "g_tokb%d" % k], eng="act")
                            b.dma("sp", GDTOK[tile], tokb[k][:].rearrange("p h e -> p (h e)"), R=["g_tokb%d" % k], W=["GDTOK"])
                    b.dma("sp", aT_[:], PT[CH_SM * 128 + 12:CH_SM * 128 + 24, :], R=["PT"], W=["g_aT"])
                    b.dma("sp", bT_[:], PT[CH_SM * 128 + 24:CH_SM * 128 + 36, :], R=["PT"], W=["g_bT"])
                    act(aT_[:], aT_[:], AF.Exp, ["g_dtb"], ["g_aT"], bias=dtb[:, 0:1])
                    act(aT_[:], aT_[:], AF.Ln, [], ["g_aT"], bias=1.0)
                    ts(aT_[:], aT_[:], Abc[:, 0:1], -1.0, ALU.mult, ALU.mult, ["g_Abc"], ["g_aT"])
                    act(bT_[:], bT_[:], AF.Exp, [], ["g_bT"], scale=-1.0)
                    ts(bT_[:], bT_[:], 1.0, None, ALU.add, None, [], ["g_bT"])
                    recip(bT_[:], bT_[:], [], ["g_bT"])
                    for (srcT, dst, key) in ((aT_, gtok, "g_gtok"), (bT_, btok, "g_btok")):
                        for t in range(NTILE):
                            tr(pball[:, 7, (t % 32) * 12:(t % 32) * 12 + 12], srcT[:, t * 128:(t + 1) * 128], ["g_aT", "g_bT"], [PB(7)], idn=ident[0:12, 0:12])
                            if t == 31 or t == NTILE - 1:
                                tl0 = 0 if t == 31 else 32
                                n = t - tl0 + 1
                                cp(dst[:, tl0:tl0 + n, :], pball[:, 7, 0:n * 12].rearrange("p (t h) -> p t h", h=12), [], [PB(7), key])
                    ts(ngtok[:], gtok[:], -1.0, None, ALU.mult, None, ["g_gtok"], ["g_ngtok"])
                    ts(nbtok[:], btok[:], -1.0, None, ALU.mult, None, ["g_btok"], ["g_nbtok"])
                    b.barrier()
                Oacc = T("g_Oacc", (128, NTILE, 384))
                onesblk = T("g_onesblk", (128, 128)); triblk = T("g_triblk", (128, 2, 128)); trisblk = T("g_trisblk", (128, 2, 128))
                negdt = T("g_negdt", (128, 2, 6, 128)); s01 = T("g_s01", (128, 2, 128)); chsel = T("g_chsel", (128, 2, 64))
                zT = T("g_zT", (128, 3, 128)); sz = T("g_sz", (128, 384)); of = T("g_of", (128, 6, 64)); sq6 = T("g_sq6", (128, 6, 64)); ss6 = T("g_ss6", (128, 6))
                ostg = T("g_ostg", (128, 3, 128))
                BUF = []
                for d in range(2):
                    n = lambda s_: "g%d_%s" % (d, s_)
                    BUF.append(dict(
                        tok=T(n("tok"), (128, 18, 64)), rhs1=T(n("rhs1"), (128, 6, 128)), rhs2=T(n("rhs2"), (128, 6, 128)),
                        E1=T(n("E1"), (128, 6, 128)), E2=T(n("E2"), (128, 6, 128)), qkT=T(n("qkT"), (64, 12, 128)), P=T(n("P"), (128, 6, 128)),
                        NTT=T(n("NT"), (128, 6, 2, 128)), attnT=T(n("attnT"), (128, 6, 128)), vb=T(n("vb"), (128, 6, 64)), kbe=T(n("kbe"), (128, 6, 64)),
                        qd=T(n("qd"), (128, 6, 64)), kdec=T(n("kdec"), (128, 6, 64)), u=T(n("u"), (128, 6, 64)), wT=T(n("wT"), (64, 6, 128)),
                        qdT=T(n("qdT"), (64, 6, 128)), vnew=T(n("vnew"), (128, 6, 64)), S=T(n("S"), (64, 6, 64)), egc=T(n("egc"), (128, 6)),
                        ekd=T(n("ekd"), (128, 6)), elast=T(n("elast"), (64, 2, 6)), be=T(n("be"), (128, 6)), gc6=T(n("gc6"), (128, 6))))
                b.dma("sp", onesblk[:], cd["onesblk"], W=["g_c"])
                b.dma("sp", triblk[:], cd["triblk"].rearrange("d p f -> p d f"), W=["g_c"])
                b.dma("sp", trisblk[:], cd["trisblk"].rearrange("d p f -> p d f"), W=["g_c"])
                b.dma("sp", negdt[:], cd["gdn_negdt"].rearrange("d p h f -> p d h f"), W=["g_c"])
                b.dma("sp", s01[:], cd["gdn_s01"].rearrange("d p f -> p d f"), W=["g_c"])
                b.dma("sp", chsel[:], cd["chsel"].rearrange("c p f -> p c f"), W=["g_c"])
                b.barrier()
                TRIB = [triblk[:, 0, :], triblk[:, 1, :]]
                step_of = [{}, {}]
                orders = [list(range(NTILE)), [1, 0] + list(range(NTILE - 1, 1, -1))]
                for d in range(2):
                    for i_, t_ in enumerate(orders[d]):
                        step_of[d][t_] = i_

                def tile_gen(d, tile):
                    Bf = BUF[d]
                    K_ = lambda s_: "g%d_%s" % (d, s_)
                    bk = [4 * d + j for j in range(4)]
                    tk = Bf["tok"]; tkk = K_("tok")
                    rhs1, rhs2, E1, E2, qkT, P, NTT, attnT = (Bf[x] for x in ("rhs1", "rhs2", "E1", "E2", "qkT", "P", "NTT", "attnT"))
                    vb, kbe, qd, kdec, u, wT, qdT, vnew, S, egc, ekd, elast, be = (Bf[x] for x in ("vb", "kbe", "qd", "kdec", "u", "wT", "qdT", "vnew", "S", "egc", "ekd", "elast", "be"))
                    hs = slice(6 * d, 6 * d + 6)
                    cs = slice(tile * 128, (tile + 1) * 128)
                    v6 = lambda bank: pball[:, bank, 0:384].rearrange("p (h e) -> p h e", h=6)
                    b.dma("sp", tk[:].rearrange("p h e -> p (h e)"), GDTOK[tile], R=["GDTOK"], W=[tkk])
                    g6 = gtok[:, tile, hs]; ng6 = ngtok[:, tile, hs]; b6 = btok[:, tile, hs]; nb6 = nbtok[:, tile, hs]
                    sb = bk[3]
                    mm(pball[:, sb, 256:262], triblk[:, d, :], g6, True, True, [], [PB(sb)])
                    mm(pball[:, sb, 262:268], trisblk[:, d, :], g6, True, True, [], [PB(sb)])
                    for c in range(2):
                        mm(pball[0:64, sb, 268 + 6 * c:274 + 6 * c], chsel[:, c, :], g6, True, True, [], [PB(sb)])
                    cp(Bf["gc6"][:], pball[:, sb, 256:262], [], [PB(sb), K_("gc6")])
                    act(egc[:], pball[:, sb, 256:262], AF.Exp, [], [PB(sb), K_("egc")])
                    act(ekd[:], pball[:, sb, 262:268], AF.Exp, [], [PB(sb), K_("ekd")])
                    act(elast[:].rearrange("p c h -> p (c h)"), pball[0:64, sb, 268:280], AF.Exp, [], [PB(sb), K_("elast")])
                    cp(rhs1[:], Bf["gc6"][:].unsqueeze(2).broadcast_to([128, 6, 128]), [K_("gc6")], [K_("rhs1")], eng="pool")
                    yield
                    for h in range(6):
                        bank = bk[h // 4]
                        tr(pball[:, bank, (h % 4) * 128:(h % 4 + 1) * 128], rhs1[:, h, :], [K_("rhs1")], [PB(bank)])
                    for h in range(6):
                        bank = bk[h // 4]
                        stt(rhs2[:, h, :], pball[:, bank, (h % 4) * 128:(h % 4 + 1) * 128], Bf["gc6"][:, h:h + 1], negdt[:, d, h, :], ALU.subtract, ALU.add,
                            [K_("gc6")], [PB(bank), K_("rhs2")])
                    act(E1[:].rearrange("p h f -> p (h f)"), rhs2[:].rearrange("p h f -> p (h f)"), AF.Exp, [K_("rhs2")], [K_("E1")])
                    yield
                    for h in range(6):
                        bank = bk[2 + h // 4]
                        tr(pball[:, bank, (h % 4) * 128:(h % 4 + 1) * 128], E1[:, h, :], [K_("E1")], [PB(bank)])
                    tt(E2[:, 0:4, :], pball[:, bk[2], :].rearrange("p (h f) -> p h f", h=4), s01[:, d, :].unsqueeze(1).broadcast_to([128, 4, 128]), ALU.mult,
                       [], [PB(bk[2]), K_("E2")])
                    tt(E2[:, 4:6, :], pball[:, bk[3], 0:256].rearrange("p (h f) -> p h f", h=2), s01[:, d, :].unsqueeze(1).broadcast_to([128, 2, 128]), ALU.mult,
                       [], [PB(bk[3]), K_("E2")])
                    yield
                    for x in range(12):
                        bank = bk[x // 4]
                        tr(pball[0:64, bank, (x % 4) * 128:(x % 4 + 1) * 128], tk[:, x, :], [tkk], [PB(bank)])
                    for q3 in range(3):
                        cp(qkT[:, q3 * 4:(q3 + 1) * 4, :].rearrange("p x t -> p (x t)"), pball[0:64, bk[q3], :], [], [PB(bk[q3]), K_("qkT")],
                           eng="act" if q3 == 1 else "dve")
                    yield
                    for h in range(6):
                        bank = bk[h // 2]; off = (h % 2) * 256
                        mm(pball[:, bank, off:off + 128], qkT[:, 6 + h, :], qkT[:, 6 + h, :], True, True, [K_("qkT")], [PB(bank)])
                        mm(pball[:, bank, off + 128:off + 256], qkT[:, 6 + h, :], qkT[:, h, :], True, True, [K_("qkT")], [PB(bank)])
                    for h in range(6):
                        bank = bk[h // 2]; off = (h % 2) * 256
                        stt(P[:, h, :], pball[:, bank, off:off + 128], nb6[:, h:h + 1], E2[:, h, :], ALU.mult, ALU.mult, [K_("E2")], [PB(bank), K_("P")])
                        tt(attnT[:, h, :], pball[:, bank, off + 128:off + 256], E1[:, h, :], ALU.mult, [K_("E1")], [PB(bank), K_("attnT")])
                    yield
                    for h in range(6):
                        bank = bk[3 if h < 3 else 0]
                        tr(pball[:, bank, (h % 3) * 128:(h % 3 + 1) * 128], P[:, h, :], [K_("P")], [PB(bank)])
                    for grp in range(2):
                        bank = bk[3 if grp == 0 else 0]
                        cp(NTT[:, 3 * grp:3 * grp + 3, 0, :], pball[:, bank, 0:384].rearrange("p (h f) -> p h f", h=3), [], [PB(bank), K_("N")], eng="act")
                    tt(NTT[:, :, 1, :], NTT[:, :, 0, :], ident[:].unsqueeze(1).broadcast_to([128, 6, 128]), ALU.add, [K_("N")], [K_("T")])
                    tt(vb[:], tk[:, 12:18, :], b6.unsqueeze(2).broadcast_to([128, 6, 64]), ALU.mult, [tkk], [K_("vb")], eng="pool")
                    tt(be[:], b6, egc[:], ALU.mult, [K_("egc")], [K_("be")])
                    tt(kbe[:], tk[:, 6:12, :], be[:].unsqueeze(2).broadcast_to([128, 6, 64]), ALU.mult, [tkk, K_("be")], [K_("kbe")])
                    tt(qd[:], tk[:, 0:6, :], egc[:].unsqueeze(2).broadcast_to([128, 6, 64]), ALU.mult, [tkk, K_("egc")], [K_("qd")], eng="pool")
                    tt(kdec[:], tk[:, 6:12, :], ekd[:].unsqueeze(2).broadcast_to([128, 6, 64]), ALU.mult, [tkk, K_("ekd")], [K_("kdec")])
                    yield
                    for m in range(6):
                        for grp in range(2):
                            hsl = slice(3 * grp, 3 * grp + 3)
                            for hh in range(3):
                                h = 3 * grp + hh
                                bank = bk[hh]
                                if m == 0:
                                    mm(pball[:, bank, 0:128], P[:, h, :], NTT[:, h, 0, :], True, True, [K_("P"), K_("N")], [PB(bank)])
                                    mm(pball[:, bank, 256:384], NTT[:, h, 0, :], P[:, h, :], True, True, [K_("P"), K_("N")], [PB(bank)])
                                elif m < 5:
                                    mm(pball[:, bank, 0:256], P[:, h, :], NTT[:, h, :, :].rearrange("p s f -> p (s f)"), True, True, [K_("P"), K_("N"), K_("T")], [PB(bank)])
                                    mm(pball[:, bank, 256:384], NTT[:, h, 0, :], P[:, h, :], True, True, [K_("P"), K_("N")], [PB(bank)])
                                else:
                                    mm(pball[:, bank, 128:256], P[:, h, :], NTT[:, h, 1, :], True, True, [K_("P"), K_("T")], [PB(bank)])
                            PBS = [PB(bk[0]), PB(bk[1]), PB(bk[2])]
                            b0 = bk[0]
                            if m < 5:
                                cp(NTT[:, hsl, 0, :], pball[:, b0:b0 + 3, 0:128], [], PBS + [K_("N")], eng="act")
                                cp(P[:, hsl, :], pball[:, b0:b0 + 3, 256:384], [], PBS + [K_("P")], eng="act" if grp else "dve")
                            if m > 0:
                                tt(NTT[:, hsl, 1, :], NTT[:, hsl, 1, :], pball[:, b0:b0 + 3, 128:256], ALU.add, [], PBS + [K_("T")])
                            yield
                    for h in range(6):
                        mm(pball[:, bk[3], h * 64:(h + 1) * 64], NTT[:, h, 1, :], vb[:, h, :], True, True, [K_("T"), K_("vb")], [PB(bk[3])])
                    cp(u[:], v6(bk[3]), [], [PB(bk[3]), K_("u")])
                    for h in range(6):
                        bank = bk[h // 3]
                        mm(pball[0:64, bank, (h % 3) * 128:(h % 3 + 1) * 128], kbe[:, h, :], NTT[:, h, 1, :], True, True, [K_("kbe"), K_("T")], [PB(bank)])
                    for grp in range(2):
                        cp(wT[:, 3 * grp:3 * grp + 3, :].rearrange("p h t -> p (h t)"), pball[0:64, bk[grp], 0:384], [], [PB(bk[grp]), K_("wT")], eng="act")
                    yield
                    for h in range(6):
                        bank = bk[2 + h // 3]
                        tr(pball[0:64, bank, (h % 3) * 128:(h % 3 + 1) * 128], qd[:, h, :], [K_("qd")], [PB(bank)])
                    for grp in range(2):
                        cp(qdT[:, 3 * grp:3 * grp + 3, :].rearrange("p h t -> p (h t)"), pball[0:64, bk[2 + grp], 0:384], [], [PB(bk[2 + grp]), K_("qdT")])
                    yield
                    for c in ((0, 1) if d == 0 else (1, 0)):
                        pc = slice(64 * c, 64 * c + 64)
                        rgc = 64 * c
                        for h in range(6):
                            mm(pball[pc, bk[0], h * 64:(h + 1) * 64], wT[:, h, pc], S[:, h, :], True, True, [K_("wT"), K_("S")], [PB(bk[0])])
                        tt(vnew[pc], u[pc], pball[pc, bk[0], 0:384].rearrange("p (h e) -> p h e", h=6), ALU.subtract, [K_("u")], [PB(bk[0]), K_("vnew")])
                        yield
                        for h in range(6):
                            mm(pball[pc, bk[1], h * 64:(h + 1) * 64], qdT[:, h, pc], S[:, h, :], True, False, [K_("qdT"), K_("S")], [PB(bk[1])])
                            mm(pball[pc, bk[1], h * 64:(h + 1) * 64], attnT[pc, h, pc], vnew[pc, h, :], False, True, [K_("attnT"), K_("vnew")], [PB(bk[1])], rg=rgc)
                        for h in range(6):
                            mm(pball[0:64, bk[2], h * 64:(h + 1) * 64], kdec[pc, h, :], vnew[pc, h, :], True, True, [K_("kdec"), K_("vnew")], [PB(bk[2])], rg=rgc)
                        tt(S[:], S[:], elast[:, c, :].unsqueeze(2).broadcast_to([64, 6, 64]), ALU.mult, [K_("elast")], [K_("S")])
                        tt(S[:], S[:], pball[0:64, bk[2], 0:384].rearrange("p (h e) -> p h e", h=6), ALU.add, [], [PB(bk[2]), K_("S")])
                        yield
                    first = step_of[d][tile] < step_of[1 - d][tile]
                    if first:
                        cp(Oacc[:, tile, :], pball[:, bk[1], 0:384], [], [PB(bk[1]), ("g_Oacc", tile)], eng="act")
                        return
                    if tile < 2 and not need_ctx:
                        return
                    b.dma("sp", zT[:], PT[CH_GZ * 128:(CH_GZ + 3) * 128, cs].rearrange("(c p) t -> p c t", p=128), R=["PT"], W=["g_zT"])
                    tt(of[:], Oacc[:, tile, :].rearrange("p (h e) -> p h e", h=6), v6(bk[1]), ALU.add, [("g_Oacc", tile)], [PB(bk[1]), "g_of"])
                    act(sq6[:], of[:], AF.Square, ["g_of"], ["g_sq6"])
                    b.op("dve", lambda e: e.tensor_reduce(out=ss6[:], in_=sq6[:], axis=AX.X, op=ALU.add), ["g_sq6"], ["g_ss6"])
                    rstd_from(ss6[:], ss6[:], 1.0 / 64, ["g_ss6"], ["g_ss6"])
                    tt(of[:], of[:], ss6[:].unsqueeze(2).broadcast_to([128, 6, 64]), ALU.mult, ["g_ss6"], ["g_of"])
                    tt(of[:], of[:], gng[:].unsqueeze(1).broadcast_to([128, 6, 64]), ALU.mult, ["g_gng"], ["g_of"], eng="pool")
                    for ch in range(3):
                        tr(pball[:, bk[3], ch * 128:(ch + 1) * 128], zT[:, ch, :], ["g_zT"], [PB(bk[3])])
                    act(sz[:], pball[:, bk[3], 0:384], AF.Silu, [], [PB(bk[3]), "g_sz"])
                    tt(sz[:], sz[:], of[:].rearrange("p h e -> p (h e)"), ALU.mult, ["g_of"], ["g_sz"])
                    for ch in range(3):
                        tr(pball[:, bk[3], ch * 128:(ch + 1) * 128], sz[:, ch * 128:(ch + 1) * 128], ["g_sz"], [PB(bk[3])])
                    cp(ostg[:].rearrange("p c t -> p (c t)"), pball[:, bk[3], 0:384], [], [PB(bk[3]), "g_ostg"], eng="act")
                    b.dma("sp", MIXT[640:1024, cs].rearrange("(c p) t -> p c t", p=128), ostg[:], R=["g_ostg"], W=["MIXT"])

                def dir_gen(d):
                    memset(BUF[d]["S"][:], 0.0, ["g%d_S" % d])
                    for tile in orders[d]:
                        yield from tile_gen(d, tile)

                gens = [dir_gen(0), dir_gen(1)]
                while gens:
                    for g_ in list(gens):
                        try:
                            next(g_)
                        except StopIteration:
                            gens.remove(g_)
                b.barrier()

        def mlp_phase(L, need_ctx):
            with contextlib.ExitStack() as ps:
                T = lambda name, shape, dt=F32: ps.enter_context(nc.sbuf_tensor(name + "_%d" % next(_UID), list(shape), dt))
                Wout = T("m5_wout", (128, 8, D), BF16); W1 = T("m5_w1", (128, 8, 4 * D), BF16); W2 = T("m5_w2", (128, 32, D), BF16)
                ht = T("m5_htb", (128, 8, 256)); mxb = T("m5_mxb", (128, 8, 256), BF16); yT = T("m5_yTb", (128, 8, 256)); sq = T("m5_sqb", (128, 8, 256))
                rstd = T("m5_rstdb", (128, 256)); mT = T("m5_aTb", (128, 8, 256), BF16); h1T = T("m5_h1T", (128, 32, 256), BF16)
                rtmp = [T("m5_r%d" % k, (128, 256)) for k in range(2)]
                for c in range(8):
                    b.dma("pool", Wout[:, c, :], wout_d[L, c * 128:(c + 1) * 128, :], W=["m5_wout"])
                    for hf in range(2):
                        b.dma("pool", W1[:, c, hf * 2048:(hf + 1) * 2048], w1_d[L, c * 128:(c + 1) * 128, hf * 2048:(hf + 1) * 2048], W=["m5_w1"])
                for fc in range(32):
                    b.dma("pool", W2[:, fc, :], w2_d[L, fc * 128:(fc + 1) * 128, :], W=["m5_w2"])
                for (t0, TB) in blocks256:
                    if t0 < LCTX and not need_ctx:
                        continue
                    s = 1 if t0 < LCTX else 0
                    b.dma("sp", ht[:], HT[:, t0:t0 + TB].rearrange("(c p) t -> p c t", p=128), R=["HT"], W=["m5_ht"])
                    b.dma("pool", mxb[:], MIXT[:, t0:t0 + TB].rearrange("(c p) t -> p c t", p=128), R=["MIXT"], W=["m5_mxb"])
                    for oc in range(8):
                        bank = 1 + oc % 4
                        for c in range(8):
                            mm(pball[:, bank, :TB], Wout[:, c, oc * 128:(oc + 1) * 128], mxb[:, c, :], c == 0, c == 7, ["m5_wout", "m5_mxb"], [PB(bank)])
                        cp(yT[:, oc, :], pball[:, bank, :TB], [], [PB(bank), "m5_yT"], eng="act" if oc % 2 else "dve")
                    post_norm_add(ht, yT, TB, sq, rstd, 2, s, "m5_")
                    norm_mod(ht, TB, sq, rstd, mT, 3, 4, s, "m5_", ["m5_ht"])
                    for fc in range(32):
                        bank = 1 + fc % 4
                        for c in range(8):
                            mm(pball[:, bank, :TB], W1[:, c, fc * 128:(fc + 1) * 128], mT[:, c, :], c == 0, c == 7, ["m5_w1", "m5_aT"], [PB(bank)])
                        k = fc % 2
                        act(rtmp[k][:], pball[:, bank, :TB], AF.Relu, [], [PB(bank), "m5_r%d" % k])
                        tt(h1T[:, fc, :], rtmp[k][:], rtmp[k][:], ALU.mult, ["m5_r%d" % k], ["m5_h1T"], eng="pool" if fc % 4 == 3 else "dve")
                    for oc in range(8):
                        bank = 5 + oc % 3
                        for fc in range(32):
                            mm(pball[:, bank, :TB], W2[:, fc, oc * 128:(oc + 1) * 128], h1T[:, fc, :], fc == 0, fc == 31, ["m5_w2", "m5_h1T"], [PB(bank)])
                        cp(yT[:, oc, :], pball[:, bank, :TB], [], [PB(bank), "m5_yT"], eng="act" if oc % 2 else "dve")
                    post_norm_add(ht, yT, TB, sq, rstd, 5, s, "m5_")
                    b.dma("sp", HT[:, t0:t0 + TB].rearrange("(c p) t -> p c t", p=128), ht[:], R=["m5_ht"], W=["HT"])
                b.barrier()

        for L in range(depth):
            need_ctx = L < depth - 1
            with contextlib.ExitStack() as ps:
                T = lambda name, shape, dt=F32: ps.enter_context(nc.sbuf_tensor(name + "_%d" % next(_UID), list(shape), dt))
                wa = [T("m_wa%d" % k, (128, 8, 1024)) for k in range(2)]
                sc = T("m_sc", (128, 8, 2)); modv = T("m_mod", (128, 48, 2)); bad = T("m_bad", (128, 48)); ng = T("m_ng", (128, 4, 8))
                with nc.allow_non_contiguous_dma(reason="tiny vectors"):
                    for s_ in range(2):
                        b.dma("sp", sc[:, :, s_], cv_d[s_].rearrange("(c p) -> p c", p=128), W=["m_sc"])
                    b.dma("sp", bad[:], bada_d[L].rearrange("(c p) -> p c", p=128), W=["m_bad"])
                    for k_ in range(4):
                        b.dma("sp", ng[:, k_, :], ng_d[L, k_].rearrange("(c p) -> p c", p=128), W=["m_ng"])
                act(sc[:], sc[:], AF.Silu, ["m_sc"], ["m_sc"])
                for blk in range(6):
                    k = blk % 2
                    for c in range(8):
                        b.dma("sp", wa[k][:, c, :], wada_d[L, c * 128:(c + 1) * 128, blk * 1024:(blk + 1) * 1024], W=[("m_wa", k, c)])
                    for fcl in range(8):
                        fc = blk * 8 + fcl
                        for c in range(8):
                            mm(pball[:, 1, fc * 2:fc * 2 + 2], wa[k][:, c, fcl * 128:(fcl + 1) * 128], sc[:, c, :], c == 0, c == 7,
                               [("m_wa", k, c), "m_sc"], [PB(1)])
                    for c in range(8):
                        b.res.setdefault(("m_wa", k, c), {"w": None, "r": []})
                tt(modv[:], pball[:, 1, 0:96].rearrange("p (f s) -> p f s", s=2), bad[:].unsqueeze(2).broadcast_to([128, 48, 2]),
                   ALU.add, ["m_bad"], [PB(1), "m_mod"])
                g_bc = lambda k: ng[:, k, :].unsqueeze(2).broadcast_to([128, 8, 2])
                stt(MV[:, 0], modv[:, 8:16, :], 1.0, g_bc(0), ALU.add, ALU.mult, ["m_mod", "m_ng"], ["MV"])
                cp(MV[:, 1], modv[:, 0:8, :], ["m_mod"], ["MV"])
                tt(MV[:, 2], modv[:, 16:24, :], g_bc(1), ALU.mult, ["m_mod", "m_ng"], ["MV"])
                stt(MV[:, 3], modv[:, 32:40, :], 1.0, g_bc(2), ALU.add, ALU.mult, ["m_mod", "m_ng"], ["MV"])
                cp(MV[:, 4], modv[:, 24:32, :], ["m_mod"], ["MV"])
                tt(MV[:, 5], modv[:, 40:48, :], g_bc(3), ALU.mult, ["m_mod", "m_ng"], ["MV"])
                b.barrier()

            with contextlib.ExitStack() as ps:
                T = lambda name, shape, dt=F32: ps.enter_context(nc.sbuf_tensor(name + "_%d" % next(_UID), list(shape), dt))
                Win = T("p1_win", (128, 8, NCH * 128), BF16)
                ht = [T("p1_ht%d" % k, (128, 8, 512)) for k in range(2)]
                sq = T("p1_sq", (128, 8, 512)); rstd = T("p1_rstd", (128, 512))
                aT = [T("p1_aT%d" % k, (128, 8, 512), BF16) for k in range(2)]
                stg = [T("p1_st%d" % k, (128, 512)) for k in range(4)]
                for c in range(8):
                    for hf in range(2):
                        b.dma("pool", Win[:, c, hf * 1920:(hf + 1) * 1920], win_d[L, c * 128:(c + 1) * 128, hf * 1920:(hf + 1) * 1920], W=["p1_win"])
                for bi, (t0, TB) in enumerate(blocks512):
                    k = bi % 2
                    s = 1 if t0 < LCTX else 0
                    b.dma("sp", ht[k][:, :, :TB], HT[:, t0:t0 + TB].rearrange("(c p) t -> p c t", p=128), R=["HT"], W=["p1_ht%d" % k])
                    norm_mod(ht[k], TB, sq, rstd, aT[k], 0, 1, s, "p1_", ["p1_ht%d" % k])
                    for fc in range(NCH):
                        bank = 1 + fc % 6
                        for c in range(8):
                            mm(pball[:, bank, :TB], Win[:, c, fc * 128:(fc + 1) * 128], aT[k][:, c, :TB], c == 0, c == 7,
                               ["p1_win", "p1_aT"], [PB(bank)])
                        sk = fc % 4
                        cp(stg[sk][:, :TB], pball[:, bank, :TB], [], [PB(bank), "p1_st%d" % sk], eng="act" if fc % 2 else "dve")
                        b.dma("sp", PT[fc * 128:(fc + 1) * 128, t0:t0 + TB], stg[sk][:, :TB], R=["p1_st%d" % sk], W=["PT"])
                b.barrier()

            with contextlib.ExitStack() as ps:
                T = lambda name, shape, dt=F32: ps.enter_context(nc.sbuf_tensor(name + "_%d" % next(_UID), list(shape), dt))
                rcos = T("a_cos", (128, LLAT)); rsin = T("a_sin", (128, LLAT))
                qrT = T("a_qrT", (128, 2, NT), BF16); krT = T("a_krT", (128, NT), BF16)
                vtok = T("a_vtok", (128, NTILE, 2, 65), BF16)
                pin = T("a_pin", (128, 7, 512)); t1 = T("a_t1", (128, 512)); t2 = T("a_t2", (128, 512))
                identb = T("a_idb", (128, 128), BF16); negm = T("a_negm", (128, 2, 2, 128)); negmb = T("a_negmb", (128, 2, 2, 128), BF16)
                esink = T("a_esink", (128, 4))
                pT = T("a_pT", (128, 2, 5, 2, 128), BF16)
                o4 = T("a_o4", (128, 4, 64)); den = T("a_den", (128, 4)); ostg = T("a_ostg", (128, 2, 128))
                b.dma("sp", rcos[:], cd["rcos"], W=["a_cos"]); b.dma("sp", rsin[:], cd["rsin"], W=["a_sin"])
                b.dma("sp", negm[:], cd["att_neg"].rearrange("m p h q -> p m h q"), W=["a_negm"])
                cp(negmb[:], negm[:], ["a_negm"], ["a_negmb"]); cp(identb[:], ident[:], ["ident"], ["a_idb"])
                b.dma("sp", esink[:], sink_d[L].partition_broadcast(128), W=["a_esink"])
                act(esink[:], esink[:], AF.Exp, ["a_esink"], ["a_esink"])
                memset(vtok[:, :, :, 64:65], 1.0, ["a_vtok"])
                for (t0, TB) in blocks512:
                    b.dma("sp", pin[:, :, :TB], PT[0:7 * 128, t0:t0 + TB].rearrange("(c p) t -> p c t", p=128), R=["PT"], W=["a_pin"])
                    if t0 < LCTX:
                        for hh in range(2):
                            cp(qrT[:, hh, t0:t0 + TB], pin[:, CH_QA + hh, :TB], ["a_pin"], ["a_qrT"])
                        cp(krT[:, t0:t0 + TB], pin[:, CH_K, :TB], ["a_pin"], ["a_krT"], eng="act")
                    else:
                        l0 = t0 - LCTX
                        for (src, srcp, dst, key) in ((CH_QA, CH_QAP, qrT[:, 0, t0:t0 + TB], "a_qrT"), (CH_QB, CH_QBP, qrT[:, 1, t0:t0 + TB], "a_qrT"),
                                                      (CH_K, CH_KP, krT[:, t0:t0 + TB], "a_krT")):
                            tt(t1[:, :TB], pin[:, src, :TB], rcos[:, l0:l0 + TB], ALU.mult, ["a_pin", "a_cos"], ["a_t1"])
                            tt(t2[:, :TB], pin[:, srcp, :TB], rsin[:, l0:l0 + TB], ALU.mult, ["a_pin", "a_sin"], ["a_t2"], eng="pool")
                            tt(dst, t1[:, :TB], t2[:, :TB], ALU.add, ["a_t1", "a_t2"], [key])
                    for j in range(TB // 128):
                        tile = t0 // 128 + j
                        tr(pball[:, 7, 0:128], pin[:, CH_V, j * 128:(j + 1) * 128], ["a_pin"], [PB(7)])
                        cp(vtok[:, tile, :, 0:64], pball[:, 7, 0:128].rearrange("p (k d) -> p k d", k=2), [], [PB(7), "a_vtok"])
                for qt in range(NTILE):
                    if qt < 2 and not need_ctx:
                        continue
                    if qt < 2:
                        keys = [(0, None), (1, None)]
                    else:
                        keys = [(0, None), (1, None), (qt, None)]
                        if qt > 2:
                            keys.append((qt - 1, 0))
                        if qt < NTILE - 1:
                            keys.append((qt + 1, 1))
                    nk = len(keys)
                    qs = slice(qt * 128, (qt + 1) * 128)
                    for kv in range(2):
                        rg = 64 * kv
                        pr = slice(64 * kv, 64 * kv + 64)
                        for s_, (kt, mk) in enumerate(keys):
                            bank = kv * 3 + s_ // 2
                            off = (s_ % 2) * 256
                            o_ap = pball[:, bank, off:off + 256]
                            mm(o_ap, krT[pr, kt * 128:(kt + 1) * 128], qrT[pr, :, qs], True, mk is None, ["a_krT", "a_qrT"], [PB(bank)], rg=rg)
                            if mk is not None:
                                mm(o_ap, identb[:], negmb[:, mk].rearrange("p h q -> p (h q)"), False, True, ["a_idb", "a_negmb"], [PB(bank)])
                        for bk in range((nk + 1) // 2):
                            ns = min(2, nk - 2 * bk)
                            act(pT[:, kv, 2 * bk:2 * bk + ns].rearrange("p s h q -> p (s h q)"), pball[:, kv * 3 + bk, 0:ns * 256], AF.Exp,
                                [], [PB(kv * 3 + bk), ("a_pT", kv)], scale=0.125)
                        for hh in range(2):
                            h = 2 * kv + hh
                            for s_, (kt, mk) in enumerate(keys):
                                mm(pball[:, 6, h * 65:(h + 1) * 65], pT[:, kv, s_, hh, :], vtok[:, kt, kv, :], s_ == 0, s_ == nk - 1,
                                   [("a_pT", kv), "a_vtok"], [PB(6)])
                    o65 = pball[:, 6, 0:260].rearrange("p (h e) -> p h e", h=4)
                    tt(den[:], o65[:, :, 64], esink[:], ALU.add, ["a_esink"], [PB(6), "a_den"])
                    recip(den[:], den[:], ["a_den"], ["a_den"])
                    tt(o4[:], o65[:, :, 0:64], den[:].unsqueeze(2).broadcast_to([128, 4, 64]), ALU.mult, ["a_den"], [PB(6), "a_o4"])
                    o4f = o4[:].rearrange("p h d -> p (h d)")
                    for cc in range(2):
                        tr(pball[:, 7, cc * 128:(cc + 1) * 128], o4f[:, cc * 128:(cc + 1) * 128], ["a_o4"], [PB(7)])
                    cp(ostg[:], pball[:, 7, 0:256].rearrange("p (c t) -> p c t", c=2), [], [PB(7), "a_ostg"], eng="act")
                    b.dma("sp", MIXT[0:256, qs].rearrange("(c p) t -> p c t", p=128), ostg[:], R=["a_ostg"], W=["MIXT"])
                b.barrier()
            if "ssd" not in skip:
                ssd_phase(L, need_ctx)
            if "gdn" not in skip:
                gdn_phase(L, need_ctx)
            if "mlp" not in skip:
                mlp_phase(L, need_ctx)

        with contextlib.ExitStack() as ps:
            T = lambda name, shape, dt=F32: ps.enter_context(nc.sbuf_tensor(name + "_%d" % next(_UID), list(shape), dt))
            hin = [T("f_h%d" % k, (128, 8, 128)) for k in range(2)]
            xo = [T("f_o%d" % k, (128, D)) for k in range(2)]
            for t in range(2, NTILE):
                k = t % 2
                b.dma("sp", hin[k][:], HT[:, t * 128:(t + 1) * 128].rearrange("(c p) t -> p c t", p=128), R=["HT"], W=["f_h%d" % k])
                for c in range(8):
                    bank = (t % 2) * 2 + c // 4
                    tr(pball[:, bank, (c % 4) * 128:(c % 4 + 1) * 128], hin[k][:, c, :], ["f_h%d" % k], [PB(bank)])
                for hf in range(2):
                    bank = (t % 2) * 2 + hf
                    cp(xo[k][:, hf * 512:(hf + 1) * 512], pball[:, bank, :], [], [PB(bank), "f_o%d" % k], eng="act" if hf else "dve")
                b.dma("sp", out_d[(t - 2) * 128:(t - 1) * 128, :], xo[k][:], R=["f_o%d" % k], W=["out"])
            b.barrier()
    return nc


_CACHE = {}


def kernel(**inputs):
    inp = {k: np.asarray(v) for k, v in inputs.items()}
    depth = 4
    if "nc" not in _CACHE:
        _CACHE["nc"] = build(depth)
    nc = _CACHE["nc"]
    perm = _win_perm()
    w_in = inp["w_in"]
    w_in_p = np.zeros((w_in.shape[0], D, NCH * 128), np.float32)
    ok = perm >= 0
    w_in_p[:, :, ok] = w_in[:, :, perm[ok]]
    shared = {
        "w_ada": inp["w_ada"], "b_ada": inp["b_ada"], "norm_g": inp["norm_g"], "w_in": w_in_p, "w_out": inp["w_out"],
        "attn_sink": inp["attn_sink"], "ssd_conv_w": inp["ssd_conv_w"], "ssd_conv_b": inp["ssd_conv_b"],
        "ssd_A_log": inp["ssd_A_log"].reshape(4, 12), "ssd_dt_bias": inp["ssd_dt_bias"].reshape(4, 12), "ssd_D": inp["ssd_D"],
        "ssd_norm_g": inp["ssd_norm_g"], "dn_conv_w": inp["dn_conv_w"], "dn_A_log": inp["dn_A_log"].reshape(4, 12),
        "dn_dt_bias": inp["dn_dt_bias"].reshape(4, 12), "dn_norm_g": inp["dn_norm_g"], "w_mlp1": inp["w_mlp1"], "w_mlp2": inp["w_mlp2"],
    }
    shared = {k: np.ascontiguousarray(v, dtype=np.float32) for k, v in shared.items()}
    for k, v in _consts().items():
        shared["c_" + k] = np.ascontiguousarray(v, dtype=np.float32)
    in_maps = []
    for c in range(8):
        m = dict(shared)
        m["x"] = np.ascontiguousarray(inp["x"][c], dtype=np.float32)
        m["ctx"] = np.ascontiguousarray(inp["ctx"][c], dtype=np.float32)
        m["cvec"] = np.ascontiguousarray(np.stack([inp["c"][c], inp["c_ctx"]]), dtype=np.float32)
        in_maps.append(m)
    res = run_bass_kernel_spmd(nc, in_maps, core_ids=list(range(8)))
    return np.stack([np.asarray(r["out"], dtype=np.float32) for r in res.results], axis=0)
```

```python
import contextlib
import math
import numpy as np
import concourse.bass as bass
import concourse.mybir as mybir
from concourse.bass_utils import run_bass_kernel_spmd

F32 = mybir.dt.float32
BF16 = mybir.dt.bfloat16
AF = mybir.ActivationFunctionType
ALU = mybir.AluOpType
AX = mybir.AxisListType

import itertools
_UID = itertools.count()
EPOCH = 30000
NEPOCH = 10
D = 1024
LCTX = 256
LLAT = 4096
NT = LCTX + LLAT
NTILE = NT // 128
NCH = 30
NEG = -30000.0
EPS = 1e-6


class Bld:
    CE = ("pe", "act", "dve", "pool")

    def __init__(self, nc, stack):
        self.nc = nc
        self.engs = {"pe": nc.tensor, "act": nc.scalar, "dve": nc.vector, "pool": nc.gpsimd, "sp": nc.sync}
        self.cnt = {e: 0 for e in self.CE}
        self.sems = {e: [stack.enter_context(nc.semaphore(f"s_{e}_{k}")) for k in range(NEPOCH)] for e in self.CE}
        self.dq = {}
        for q, n in (("sp", 16), ("pool", 6)):
            self.dq[q] = {"sems": [stack.enter_context(nc.semaphore(f"d_{q}_{k}")) for k in range(n)],
                          "val": [0] * n, "next": 0}
        self.waited = {}
        self.dwaited = {}
        self.res = {}
        self.pe_rg = {}

    def _wait(self, eng, tok, rg=0):
        E = self.engs[eng]
        if tok[0] == "e":
            _, e2, c2 = tok
            if e2 == eng and eng == "pe" and self.pe_rg.get(c2, 0) == rg:
                return
            if self.waited.get((eng, e2), 0) >= c2:
                return
            ep, v = (c2 - 1) // EPOCH, (c2 - 1) % EPOCH + 1
            E.wait_ge(self.sems[e2][ep], v)
            self.waited[(eng, e2)] = c2
        else:
            _, q, idx, v = tok
            if self.dwaited.get((eng, q, idx), 0) >= v:
                return
            E.wait_ge(self.dq[q]["sems"][idx], v)
            self.dwaited[(eng, q, idx)] = v

    def _sync(self, eng, reads, writes, rg=0):
        toks = []
        for k in reads:
            r = self.res.get(k)
            if r and r["w"] is not None:
                toks.append(r["w"])
        for k in writes:
            r = self.res.get(k)
            if r:
                if r["w"] is not None:
                    toks.append(r["w"])
                toks.extend(r["r"])
        for t in toks:
            self._wait(eng, t, rg)

    def _mark(self, tok, reads, writes):
        for k in reads:
            r = self.res.setdefault(k, {"w": None, "r": []})
            r["r"].append(tok)
            if len(r["r"]) > 48:
                r["r"] = self._prune(r["r"])
        for k in writes:
            self.res[k] = {"w": tok, "r": []}

    @staticmethod
    def _prune(toks):
        best = {}
        for t in toks:
            key = (t[0], t[1]) if t[0] == "e" else (t[0], t[1], t[2])
            v = t[2] if t[0] == "e" else t[3]
            old = best.get(key)
            if old is None or v > (old[2] if old[0] == "e" else old[3]):
                best[key] = t
        return list(best.values())

    def op(self, eng, fn, R=(), W=(), rg=0):
        self._sync(eng, R, W, rg)
        inst = fn(self.engs[eng])
        self.cnt[eng] += 1
        c = self.cnt[eng]
        if eng == "pe" and rg:
            self.pe_rg[c] = rg
        inst.then_inc(self.sems[eng][(c - 1) // EPOCH], 1)
        self._mark(("e", eng, c), R, W)
        return inst

    def dma(self, q, out, in_, R=(), W=(), **kw):
        self._sync(q, R, W)
        d = self.dq[q]
        idx = d["next"]
        d["next"] = (idx + 1) % len(d["sems"])
        if d["val"][idx] > 0:
            self._wait(q, ("d", q, idx, d["val"][idx]))
        inst = self.engs[q].dma_start(out=out, in_=in_, **kw)
        d["val"][idx] += 16
        inst.then_inc(d["sems"][idx], 16)
        tok = ("d", q, idx, d["val"][idx])
        self._mark(tok, R, W)
        return tok

    def barrier(self):
        for eng in ("pe", "act", "dve", "pool", "sp"):
            for e2 in self.CE:
                if self.cnt[e2] > 0:
                    self._wait(eng, ("e", e2, self.cnt[e2]))
            for q, d in self.dq.items():
                for idx, v in enumerate(d["val"]):
                    if v > 0:
                        self._wait(eng, ("d", q, idx, v))
        self.res = {}


def _consts():
    c = {}
    i = np.arange(128)
    c["ident"] = np.eye(128, dtype=np.float32)
    c["ones"] = np.ones((128, 128), np.float32)
    blk = (i[:, None] // 64) == (i[None, :] // 64)
    tri_f = (i[:, None] <= i[None, :])
    tri_b = (i[:, None] >= i[None, :])
    c["tri"] = np.stack([tri_f, tri_b]).astype(np.float32)
    c["tris"] = np.stack([i[:, None] > i[None, :], i[:, None] < i[None, :]]).astype(np.float32)
    c["onesblk"] = blk.astype(np.float32)
    c["triblk"] = np.stack([tri_f & blk, tri_b & blk]).astype(np.float32)
    c["trisblk"] = np.stack([(i[:, None] > i[None, :]) & blk, (i[:, None] < i[None, :]) & blk]).astype(np.float32)
    def m(al):
        return np.where(al, 0.0, NEG).astype(np.float32)
    ssd_m = np.stack([m(i[None, :] >= i[:, None]), m(i[None, :] <= i[:, None])])
    c["ssd_negh"] = np.repeat(ssd_m[:, :, None, :], 6, axis=2).transpose(1, 0, 2, 3).reshape(128, 12, 128).copy()
    g_dt = np.stack([m((i[None, :] >= i[:, None]) & blk), m((i[None, :] <= i[:, None]) & blk)])
    g_d = np.stack([m((i[:, None] > i[None, :]) & blk), m((i[:, None] < i[None, :]) & blk)])
    c["gdn_negdt"] = np.repeat(g_dt[:, :, None, :], 6, axis=2).copy()
    c["gdn_negd"] = np.repeat(g_d[:, :, None, :], 6, axis=2).copy()
    c["gdn_s01"] = (g_d == 0.0).astype(np.float32)
    chs = np.zeros((2, 128, 64), np.float32)
    chs[0, :64] = 1.0
    chs[1, 64:] = 1.0
    c["chsel"] = chs
    am = np.stack([m(i[:, None] >= i[None, :]), m(i[:, None] <= i[None, :])])
    c["att_neg"] = np.repeat(am[:, :, None, :], 2, axis=2).copy()
    t = np.arange(LLAT)
    row = (t // 64).astype(np.float64)
    col = (t % 64).astype(np.float64)
    inv = 10000.0 ** (-np.arange(0, 32, 2, dtype=np.float64) / 32.0)
    cos = np.zeros((64, LLAT)); sin = np.zeros((64, LLAT))
    for d in range(64):
        pos = row if d < 32 else col
        idx = d % 32
        ang = pos * inv[idx % 16]
        cos[d] = np.cos(ang)
        sin[d] = -np.sin(ang) if idx < 16 else np.sin(ang)
    c["rcos"] = np.concatenate([cos, cos]).astype(np.float32)
    c["rsin"] = np.concatenate([sin, sin]).astype(np.float32)
    return c


def _win_perm():
    def partner(d):
        return d + 16 if (d % 32) < 16 else d - 16
    cols = []
    aq = lambda h, d: h * 64 + d
    for hs in ((0, 2), (1, 3)):
        cols += [aq(h, d) for h in hs for d in range(64)]
    for hs in ((0, 2), (1, 3)):
        cols += [aq(h, partner(d)) for h in hs for d in range(64)]
    cols += [256 + kv * 64 + d for kv in range(2) for d in range(64)]
    cols += [256 + kv * 64 + partner(d) for kv in range(2) for d in range(64)]
    cols += list(range(384, 512))
    cols += list(range(512, 1408))
    cols += list(range(1408, 1792))
    cols += list(range(1804, 2956))
    cols += list(range(2956, 3340))
    small = list(range(1792, 1804)) + list(range(3340, 3352)) + list(range(3352, 3364))
    cols += small + [-1] * (128 - len(small))
    assert len(cols) == NCH * 128
    return np.array(cols)


CH_QA, CH_QB, CH_QAP, CH_QBP, CH_K, CH_KP, CH_V = 0, 1, 2, 3, 4, 5, 6
CH_SX, CH_SB, CH_SC, CH_SZ = 7, 10, 12, 14
CH_GQ, CH_GK, CH_GV, CH_GZ = 17, 20, 23, 26
CH_SM = 29


def build(depth=4, dbg=False, skip=()):
    nc = bass.Bass("TRN2", target_bir_lowering=False)
    C = _consts()
    dt_in = lambda name, shape: nc.dram_tensor(name, list(shape), F32, kind="ExternalInput").ap()
    x_d = dt_in("x", (LLAT, D)); ctx_d = dt_in("ctx", (LCTX, D)); cv_d = dt_in("cvec", (2, D))
    wada_d = dt_in("w_ada", (4, D, 6 * D)); bada_d = dt_in("b_ada", (4, 6 * D)); ng_d = dt_in("norm_g", (4, 4, D))
    win_d = dt_in("w_in", (4, D, NCH * 128)); wout_d = dt_in("w_out", (4, D, D))
    sink_d = dt_in("attn_sink", (4, 4))
    scw_d = dt_in("ssd_conv_w", (4, 896, 5)); scb_d = dt_in("ssd_conv_b", (4, 896))
    sal_d = dt_in("ssd_A_log", (4, 12)); sdb_d = dt_in("ssd_dt_bias", (4, 12)); sD_d = dt_in("ssd_D", (4, 6))
    sng_d = dt_in("ssd_norm_g", (4, 384))
    gcw_d = dt_in("dn_conv_w", (4, 1152, 5)); gal_d = dt_in("dn_A_log", (4, 12)); gdb_d = dt_in("dn_dt_bias", (4, 12))
    gng_d = dt_in("dn_norm_g", (4, 64))
    w1_d = dt_in("w_mlp1", (4, D, 4 * D)); w2_d = dt_in("w_mlp2", (4, 4 * D, D))
    cd = {k: dt_in("c_" + k, v.shape) for k, v in C.items()}
    out_d = nc.dram_tensor("out", [LLAT, D], F32, kind="ExternalOutput").ap()
    skind = "ExternalOutput" if dbg else "Internal"
    HT = nc.dram_tensor("HT", [D, NT], F32, kind=skind).ap()
    PT = nc.dram_tensor("PT", [NCH * 128, NT], F32, kind=skind).ap()
    MIXT = nc.dram_tensor("MIXT", [D, NT], F32, kind=skind).ap()
    GDTOK = nc.dram_tensor("GDTOK", [NTILE, 128, 1152], F32, kind="Internal").ap()

    with contextlib.ExitStack() as st:
        b = Bld(nc, st)
        pball = st.enter_context(nc.psum_tensor("pball", [128, 8, 512], F32))
        PB = lambda k: "pb%d" % k
        gT = lambda name, shape, dt=F32: st.enter_context(nc.sbuf_tensor(name + "_%d" % next(_UID), list(shape), dt))
        ident = gT("ident", (128, 128)); ones = gT("ones", (128, 128))
        MV = gT("MV", (128, 6, 8, 2))
        b.dma("sp", ident[:], cd["ident"], W=["ident"])
        b.dma("sp", ones[:], cd["ones"], W=["ones"])

        def act(out, in_, func, R, W, **kw):
            return b.op("act", lambda e: e.activation(out=out, in_=in_, func=func, **kw), R, W)

        def mm(out, lhsT, rhs, start, stop, R, W, rg=0):
            return b.op("pe", lambda e: e.matmul(out, lhsT=lhsT, rhs=rhs, start=start, stop=stop), R, W, rg)

        def tr(out, in_, R, W, idn=None):
            idn = ident[:] if idn is None else idn
            return b.op("pe", lambda e: e.transpose(out=out, in_=in_, identity=idn), list(R) + ["ident"], W)

        def tt(out, in0, in1, op, R, W, eng="dve"):
            return b.op(eng, lambda e: e.tensor_tensor(out=out, in0=in0, in1=in1, op=op), R, W)

        def ts(out, in0, s1, s2, op0, op1, R, W, eng="dve"):
            if op1 is None:
                return b.op(eng, lambda e: e.tensor_scalar(out=out, in0=in0, scalar1=s1, scalar2=None, op0=op0), R, W)
            return b.op(eng, lambda e: e.tensor_scalar(out=out, in0=in0, scalar1=s1, scalar2=s2, op0=op0, op1=op1), R, W)

        def stt(out, in0, scalar, in1, op0, op1, R, W):
            return b.op("dve", lambda e: e.scalar_tensor_tensor(out=out, in0=in0, scalar=scalar, in1=in1, op0=op0, op1=op1), R, W)

        def cp(out, in_, R, W, eng="dve"):
            if eng == "act":
                return act(out, in_, AF.Copy, R, W)
            return b.op(eng, lambda e: e.tensor_copy(out=out, in_=in_), R, W)

        def recip(out, in_, R, W):
            return b.op("dve", lambda e: e.reciprocal(out=out, in_=in_), R, W)

        def memset(ap, val, W, eng="pool"):
            return b.op(eng, lambda e: e.memset(ap, val), (), W)

        def rstd_from(out, ssq, scale, R, W):
            act(out, ssq, AF.Sqrt, R, W, scale=scale, bias=epsc[:, 0:1])
            recip(out, out, W, W)

        epsc = gT("epsc", (128, 1))
        memset(epsc[:], EPS, ["epsc"])

        with contextlib.ExitStack() as ps:
            T = lambda name, shape, dt=F32: ps.enter_context(nc.sbuf_tensor(name + "_%d" % next(_UID), list(shape), dt))
            xin = [T("i_x%d" % k, (128, D)) for k in range(2)]
            xo = [T("i_o%d" % k, (128, 8, 128)) for k in range(2)]
            for t in range(NTILE):
                k = t % 2
                src = ctx_d[t * 128:(t + 1) * 128, :] if t < 2 else x_d[(t - 2) * 128:(t - 1) * 128, :]
                b.dma("sp", xin[k][:], src, W=["i_x%d" % k])
                for c in range(8):
                    bank = (t % 2) * 2 + c // 4
                    tr(pball[:, bank, (c % 4) * 128:(c % 4 + 1) * 128], xin[k][:, c * 128:(c + 1) * 128], ["i_x%d" % k], [PB(bank)])
                for hf in range(2):
                    bank = (t % 2) * 2 + hf
                    cp(xo[k][:, hf * 4:(hf + 1) * 4, :], pball[:, bank, :].rearrange("p (c t) -> p c t", c=4), [], [PB(bank), "i_o%d" % k],
                       eng="act" if hf else "dve")
                b.dma("sp", HT[:, t * 128:(t + 1) * 128].rearrange("(c p) t -> p c t", p=128), xo[k][:], R=["i_o%d" % k], W=["HT"])
            b.barrier()

        blocks512 = [(0, 256)] + [(256 + 512 * i, 512) for i in range(8)]
        blocks256 = [(256 * i, 256) for i in range(17)]

        def norm_mod(ht, TB, sq, rstd, aT, kmul, kshift, s, tag, R, nb=0):
            SQ = [(tag + "sq", c) for c in range(8)]
            act(sq[:, :, :TB], ht[:, :, :TB], AF.Square, R, SQ)
            for c in range(8):
                mm(pball[:, nb, :TB], ones[:], sq[:, c, :TB], c == 0, c == 7, SQ + ["ones"], [PB(nb)])
            rstd_from(rstd[:, :TB], pball[:, nb, :TB], 1.0 / D, [], [PB(nb), tag + "rstd"])
            tt(sq[:, :, :TB], ht[:, :, :TB], rstd[:, :TB].unsqueeze(1).broadcast_to([128, 8, TB]), ALU.mult,
               list(R) + [tag + "rstd"], SQ)
            for c in range(8):
                act(aT[:, c, :TB], sq[:, c, :TB], AF.Identity, [(tag + "sq", c), "MV"], [(tag + "aT", c)],
                    scale=MV[:, kmul, c, s:s + 1], bias=MV[:, kshift, c, s:s + 1])

        def post_norm_add(ht, yT, TB, sq, rstd, kg, s, tag, nb=0):
            SQ = [(tag + "sq", c) for c in range(8)]
            YT = [(tag + "yT", c) for c in range(8)]
            act(sq[:, :, :TB], yT[:, :, :TB], AF.Square, YT, SQ)
            for c in range(8):
                mm(pball[:, nb, :TB], ones[:], sq[:, c, :TB], c == 0, c == 7, SQ + ["ones"], [PB(nb)])
            rstd_from(rstd[:, :TB], pball[:, nb, :TB], 1.0 / D, [], [PB(nb), tag + "rstd"])
            tt(sq[:, :, :TB], yT[:, :, :TB], rstd[:, :TB].unsqueeze(1).broadcast_to([128, 8, TB]), ALU.mult,
               YT + [tag + "rstd"], SQ)
            for c in range(8):
                stt(ht[:, c, :TB], sq[:, c, :TB], MV[:, kg, c, s:s + 1], ht[:, c, :TB], ALU.mult, ALU.add,
                    [(tag + "sq", c), "MV"], [(tag + "ht", c)])

        def conv_seg(xin, acc, rows, t0, Lb, seg_lo, seg_hi, cw, cbias, ch, tag):
            lo = max(t0 - 2, seg_lo); hi = min(t0 + Lb + 2, seg_hi)
            if lo > t0 - 2:
                memset(xin[:, 0:2], 0.0, [tag + "xin"])
            if hi < t0 + Lb + 2:
                memset(xin[:, Lb + 2:Lb + 4], 0.0, [tag + "xin"])
            b.dma("sp", xin[:, lo - (t0 - 2):hi - (t0 - 2)], PT[rows, lo:hi], R=["PT"], W=[tag + "xin"])
            if cbias is None:
                ts(acc[:, :Lb], xin[:, 0:Lb], cw[:, ch, 0:1], None, ALU.mult, None, [tag + "xin", tag + "cw"], [tag + "acc"])
            else:
                ts(acc[:, :Lb], xin[:, 0:Lb], cw[:, ch, 0:1], cbias[:, ch:ch + 1], ALU.mult, ALU.add, [tag + "xin", tag + "cw"], [tag + "acc"])
            for j in range(1, 5):
                stt(acc[:, :Lb], xin[:, j:j + Lb], cw[:, ch, j:j + 1], acc[:, :Lb], ALU.mult, ALU.add, [tag + "xin", tag + "cw"], [tag + "acc"])

        def decay(E_out, Ekey, g1, g2, dirs, lhs_ones, TRI, negh, banks, rhs1, rhs2, tag, R):
            nh = len(dirs)
            d0 = 0
            while d0 < nh:
                d1 = d0
                while d1 < nh and dirs[d1] == dirs[d0]:
                    d1 += 1
                n = d1 - d0
                tt(rhs1[:, d0:d1, :], g1[:, d0:d1].unsqueeze(2).broadcast_to([128, n, 128]),
                   TRI[dirs[d0]].unsqueeze(1).broadcast_to([128, n, 128]), ALU.mult, list(R) + [tag + "c"], [tag + "rhs1"])
                d0 = d1
            cp(rhs2[:, 0:nh, :], g2[:, 0:nh].unsqueeze(2).broadcast_to([128, nh, 128]), R, [tag + "rhs2"], eng="pool")
            for bi, bank in enumerate(banks):
                h0 = 4 * bi; h1 = min(nh, h0 + 4); n = h1 - h0
                if n <= 0:
                    break
                mm(pball[:, bank, 0:n * 128], lhs_ones, rhs1[:, h0:h1, :].rearrange("p h f -> p (h f)"), True, False,
                   [tag + "rhs1", tag + "c"], [PB(bank)])
                a0 = h0
                while a0 < h1:
                    a1 = a0
                    while a1 < h1 and dirs[a1] == dirs[a0]:
                        a1 += 1
                    mm(pball[:, bank, (a0 - h0) * 128:(a1 - h0) * 128], TRI[dirs[a0]], rhs2[:, a0:a1, :].rearrange("p h f -> p (h f)"),
                       False, False, [tag + "rhs2", tag + "c"], [PB(bank)])
                    a0 = a1
                mm(pball[:, bank, 0:n * 128], ident[:], negh[:, h0:h1, :].rearrange("p h f -> p (h f)"), False, True,
                   ["ident", tag + "c"], [PB(bank)])
                act(E_out[:, h0:h1, :].rearrange("p h f -> p (h f)"), pball[:, bank, 0:n * 128], AF.Exp, [], [PB(bank), Ekey])

        def ssd_phase(L, need_ctx):
            with contextlib.ExitStack() as ps:
                T = lambda name, shape, dt=F32: ps.enter_context(nc.sbuf_tensor(name + "_%d" % next(_UID), list(shape), dt))
                xtok = T("s_xtok", (128, NTILE, 384), BF16); Btok = T("s_Btok", (128, NTILE, 256), BF16)
                BT = T("s_BT", (128, 2, NT), BF16); CT = T("s_CT", (128, 2, NT), BF16)
                dttok = T("s_dttok", (128, NTILE, 12)); atok = T("s_atok", (128, NTILE, 12)); natok = T("s_natok", (128, NTILE, 12))
                cw = T("s_cw", (128, 7, 5)); cb = T("s_cb", (128, 7)); Abc = T("s_Abc", (128, 12)); dtb = T("s_dtb", (12, 1))
                Dbc = T("s_Dbc", (128, 6)); sng = T("s_sng", (128, 3))
                with nc.allow_non_contiguous_dma(reason="tiny vectors"):
                    b.dma("sp", cw[:], scw_d[L].rearrange("(c p) j -> p c j", p=128), W=["s_cw"])
                    b.dma("sp", cb[:], scb_d[L].rearrange("(c p) -> p c", p=128), W=["s_cw"])
                    b.dma("sp", dtb[:], sdb_d[L].rearrange("(p o) -> p o", o=1), W=["s_dtb"])
                    b.dma("sp", sng[:], sng_d[L].rearrange("(c p) -> p c", p=128), W=["s_sng"])
                b.dma("sp", Abc[:], sal_d[L].partition_broadcast(128), W=["s_Abc"])
                b.dma("sp", Dbc[:], sD_d[L].partition_broadcast(128), W=["s_Dbc"])
                act(Abc[:], Abc[:], AF.Exp, ["s_Abc"], ["s_Abc"])
                with contextlib.ExitStack() as ps2:
                    T2 = lambda name, shape, dt=F32: ps2.enter_context(nc.sbuf_tensor(name + "_%d" % next(_UID), list(shape), dt))
                    xin = T2("s_xin", (128, LLAT + 4)); acc = T2("s_acc", (128, LLAT)); cvb = T2("s_cv", (128, LLAT))
                    dtT = T2("s_dtT", (12, NT))
                    for ch in range(7):
                        for (t0, Lb) in ((0, LCTX), (LCTX, LLAT)):
                            conv_seg(xin, acc, slice((CH_SX + ch) * 128, (CH_SX + ch + 1) * 128), t0, Lb, t0, t0 + Lb, cw, cb, ch, "s_")
                            if ch < 3 or ch in (3, 4):
                                act(cvb[:, :Lb], acc[:, :Lb], AF.Silu, ["s_acc"], ["s_cv"])
                                if ch in (3, 4):
                                    cp(BT[:, ch - 3, t0:t0 + Lb], cvb[:, :Lb], ["s_cv"], ["s_BT"], eng="pool")
                                for q4 in range(Lb // 512 if Lb >= 512 else 1):
                                    nt4 = min(4, Lb // 128)
                                    bank = 1 + q4 % 4
                                    for j in range(nt4):
                                        tr(pball[:, bank, j * 128:(j + 1) * 128], cvb[:, (q4 * 4 + j) * 128:(q4 * 4 + j + 1) * 128], ["s_cv"], [PB(bank)])
                                    tl0 = t0 // 128 + q4 * 4
                                    dst = xtok[:, tl0:tl0 + nt4, ch * 128:(ch + 1) * 128] if ch < 3 else Btok[:, tl0:tl0 + nt4, (ch - 3) * 128:(ch - 2) * 128]
                                    cp(dst, pball[:, bank, 0:nt4 * 128].rearrange("p (t f) -> p t f", t=nt4), [], [PB(bank), "s_xtok" if ch < 3 else "s_Btok"],
                                       eng="act" if q4 % 2 else "dve")
                            else:
                                act(CT[:, ch - 5, t0:t0 + Lb], acc[:, :Lb], AF.Silu, ["s_acc"], ["s_CT"])
                    b.dma("sp", dtT[:], PT[CH_SM * 128:CH_SM * 128 + 12, :], R=["PT"], W=["s_dtT"])
                    act(dtT[:], dtT[:], AF.Exp, ["s_dtT", "s_dtb"], ["s_dtT"], bias=dtb[:, 0:1])
                    act(dtT[:], dtT[:], AF.Ln, ["s_dtT"], ["s_dtT"], bias=1.0)
                    for t in range(NTILE):
                        tr(pball[:, 7, (t % 32) * 12:(t % 32) * 12 + 12], dtT[:, t * 128:(t + 1) * 128], ["s_dtT"], [PB(7)], idn=ident[0:12, 0:12])
                        if t == 31 or t == NTILE - 1:
                            tl0 = 0 if t == 31 else 32
                            n = t - tl0 + 1
                            cp(dttok[:, tl0:tl0 + n, :], pball[:, 7, 0:n * 12].rearrange("p (t h) -> p t h", h=12), [], [PB(7), "s_dttok"])
                    stt(atok[:], dttok[:], -1.0, Abc[:].unsqueeze(1).broadcast_to([128, NTILE, 12]), ALU.mult, ALU.mult, ["s_dttok", "s_Abc"], ["s_atok"])
                    ts(natok[:], atok[:], -1.0, None, ALU.mult, None, ["s_atok"], ["s_natok"])
                    b.barrier()
                Senter = T("s_Sent", (128, NTILE, 2, 6, 64), BF16); eacum = T("s_eacum", (128, NTILE, 12)); acum = T("s_acum", (128, NTILE, 12))
                tri = T("s_tri", (128, 2, 128)); tris = T("s_tris", (128, 2, 128)); negh = T("s_negh", (128, 12, 128))
                rhs1 = T("s_rhs1", (128, 12, 128)); rhs2 = T("s_rhs2", (128, 12, 128)); E = T("s_E", (128, 12, 128))
                MT = T("s_MT", (128, 12, 128), BF16); xdt = T("s_xdt", (128, 12, 64), BF16); xw = T("s_xw", (128, 6, 64), BF16)
                S = T("s_S", (128, 6, 64)); wexp = T("s_wexp", (128, 6)); elast = T("s_elast", (128, 6)); wv = T("s_w", (128, 6))
                tA = T("s_tA", (128, 6, 64)); tB = T("s_tB", (128, 6, 64)); zT = T("s_zT", (128, 3, 128)); sz = T("s_sz", (128, 384))
                gated = T("s_gated", (128, 384)); junk = T("s_junk", (128, 192)); ssq = T("s_ssq", (128, 2)); ostg = T("s_ostg", (128, 3, 128))
                b.dma("sp", tri[:], cd["tri"].rearrange("d p f -> p d f"), W=["s_c"])
                b.dma("sp", tris[:], cd["tris"].rearrange("d p f -> p d f"), W=["s_c"])
                b.dma("sp", negh[:], cd["ssd_negh"], W=["s_c"])
                TRI = [tri[:, 0, :], tri[:, 1, :]]
                for d in range(2):
                    order = list(range(NTILE)) if d == 0 else [1, 0] + list(range(NTILE - 1, 1, -1))
                    memset(S[:], 0.0, ["s_S"])
                    hs = slice(6 * d, 6 * d + 6)
                    for c in order:
                        mm(pball[:, 7, 0:6], tri[:, d, :], atok[:, c, hs], True, True, ["s_c", "s_atok"], [PB(7)])
                        mm(pball[:, 7, 6:12], tris[:, d, :], atok[:, c, hs], True, True, ["s_c", "s_atok"], [PB(7)])
                        mm(pball[:, 7, 12:18], ones[:], atok[:, c, hs], True, True, ["ones", "s_atok"], [PB(7)])
                        act(eacum[:, c, hs], pball[:, 7, 0:6], AF.Exp, [], [PB(7), "s_eacum"])
                        cp(acum[:, c, hs], pball[:, 7, 0:6], [], [PB(7), "s_acum"])
                        act(wexp[:], pball[:, 7, 6:12], AF.Exp, [], [PB(7), "s_wexp"])
                        act(elast[:], pball[:, 7, 12:18], AF.Exp, [], [PB(7), "s_elast"])
                        tt(wv[:], wexp[:], dttok[:, c, hs], ALU.mult, ["s_wexp", "s_dttok"], ["s_w"])
                        tt(xw[:], xtok[:, c, :].rearrange("p (h e) -> p h e", h=6), wv[:].unsqueeze(2).broadcast_to([128, 6, 64]), ALU.mult,
                           ["s_xtok", "s_w"], ["s_xw"])
                        for g in range(2):
                            mm(pball[:, 6, g * 192:(g + 1) * 192], Btok[:, c, g * 128:(g + 1) * 128],
                               xw[:, 3 * g:3 * g + 3, :].rearrange("p h e -> p (h e)"), True, True, ["s_Btok", "s_xw"], [PB(6)])
                        cp(Senter[:, c, d], S[:], ["s_S"], ["s_Sent"], eng="pool")
                        tt(S[:], S[:], elast[:].unsqueeze(2).broadcast_to([128, 6, 64]), ALU.mult, ["s_elast"], ["s_S"])
                        tt(S[:], S[:], pball[:, 6, 0:384].rearrange("p (h e) -> p h e", h=6), ALU.add, [], [PB(6), "s_S"])
                for c in range(NTILE):
                    if c < 2 and not need_ctx:
                        continue
                    cs = slice(c * 128, (c + 1) * 128)
                    b.dma("sp", zT[:], PT[CH_SZ * 128:(CH_SZ + 3) * 128, cs].rearrange("(c p) t -> p c t", p=128), R=["PT"], W=["s_zT"])
                    cp(rhs1[:], acum[:, c, :].unsqueeze(2).broadcast_to([128, 12, 128]), ["s_acum"], ["s_rhs1"], eng="pool")
                    for hd in range(12):
                        tr(pball[:, hd // 4, (hd % 4) * 128:(hd % 4 + 1) * 128], rhs1[:, hd, :], ["s_rhs1"], [PB(hd // 4)])
                    for hd in range(12):
                        stt(rhs2[:, hd, :], pball[:, hd // 4, (hd % 4) * 128:(hd % 4 + 1) * 128], acum[:, c, hd:hd + 1], negh[:, hd, :], ALU.subtract, ALU.add,
                            ["s_acum", "s_c"], [PB(hd // 4), "s_rhs2"])
                    act(E[:].rearrange("p h f -> p (h f)"), rhs2[:].rearrange("p h f -> p (h f)"), AF.Exp, ["s_rhs2"], ["s_E"])
                    for g in range(2):
                        mm(pball[:, 3, g * 128:(g + 1) * 128], BT[:, g, cs], CT[:, g, cs], True, True, ["s_BT", "s_CT"], [PB(3)])
                    for d in range(2):
                        for g in range(2):
                            h0 = d * 6 + 3 * g
                            tt(MT[:, h0:h0 + 3, :], E[:, h0:h0 + 3, :], pball[:, 3, g * 128:(g + 1) * 128].unsqueeze(1).broadcast_to([128, 3, 128]),
                               ALU.mult, ["s_E"], [PB(3), "s_MT"])
                        tt(xdt[:, 6 * d:6 * d + 6, :], xtok[:, c, :].rearrange("p (h e) -> p h e", h=6),
                           dttok[:, c, 6 * d:6 * d + 6].unsqueeze(2).broadcast_to([128, 6, 64]), ALU.mult, ["s_xtok", "s_dttok"], ["s_xdt"], eng="pool")
                    for h in range(6):
                        for d in range(2):
                            mm(pball[:, 4, h * 64:(h + 1) * 64], MT[:, d * 6 + h, :], xdt[:, d * 6 + h, :], d == 0, d == 1, ["s_MT", "s_xdt"], [PB(4)])
                    for d in range(2):
                        for g in range(2):
                            mm(pball[:, 5 + d, g * 192:(g + 1) * 192], CT[:, g, cs], Senter[:, c, d, 3 * g:3 * g + 3, :].rearrange("p h e -> p (h e)"),
                               True, True, ["s_CT", "s_Sent"], [PB(5 + d)])
                    v6 = lambda bank: pball[:, bank, 0:384].rearrange("p (h e) -> p h e", h=6)
                    tt(tA[:], v6(5), eacum[:, c, 0:6].unsqueeze(2).broadcast_to([128, 6, 64]), ALU.mult, ["s_eacum"], [PB(5), "s_tA"])
                    tt(tB[:], v6(6), eacum[:, c, 6:12].unsqueeze(2).broadcast_to([128, 6, 64]), ALU.mult, ["s_eacum"], [PB(6), "s_tB"])
                    tt(tA[:], tA[:], tB[:], ALU.add, ["s_tB"], ["s_tA"], eng="pool")
                    tt(tA[:], tA[:], v6(4), ALU.add, [], [PB(4), "s_tA"])
                    tt(tB[:], xtok[:, c, :].rearrange("p (h e) -> p h e", h=6), Dbc[:].unsqueeze(2).broadcast_to([128, 6, 64]), ALU.mult,
                       ["s_xtok", "s_Dbc"], ["s_tB"], eng="pool")
                    tt(tA[:], tA[:], tB[:], ALU.add, ["s_tB"], ["s_tA"], eng="pool")
                    for ch in range(3):
                        tr(pball[:, 7, ch * 128:(ch + 1) * 128], zT[:, ch, :], ["s_zT"], [PB(7)])
                    act(sz[:], pball[:, 7, 0:384], AF.Silu, [], [PB(7), "s_sz"])
                    tt(gated[:], tA[:].rearrange("p h e -> p (h e)"), sz[:], ALU.mult, ["s_tA", "s_sz"], ["s_gated"])
                    for g in range(2):
                        act(junk[:], gated[:, g * 192:(g + 1) * 192], AF.Square, ["s_gated"], ["s_junk", "s_ssq"], accum_out=ssq[:, g:g + 1])
                    rstd_from(ssq[:], ssq[:], 1.0 / 192, ["s_ssq"], ["s_ssq"])
                    tt(gated[:].rearrange("p (g f) -> p g f", g=2), gated[:].rearrange("p (g f) -> p g f", g=2),
                       ssq[:].unsqueeze(2).broadcast_to([128, 2, 192]), ALU.mult, ["s_ssq"], ["s_gated"])
                    for ch in range(3):
                        tr(pball[:, 7, ch * 128:(ch + 1) * 128], gated[:, ch * 128:(ch + 1) * 128], ["s_gated"], [PB(7)])
                    for ch in range(3):
                        act(ostg[:, ch, :], pball[:, 7, ch * 128:(ch + 1) * 128], AF.Copy, ["s_sng"], [PB(7), "s_ostg"], scale=sng[:, ch:ch + 1])
                    b.dma("sp", MIXT[256:640, cs].rearrange("(c p) t -> p c t", p=128), ostg[:], R=["s_ostg"], W=["MIXT"])
                b.barrier()

        def gdn_phase(L, need_ctx):
            with contextlib.ExitStack() as ps:
                T = lambda name, shape, dt=F32: ps.enter_context(nc.sbuf_tensor(name + "_%d" % next(_UID), list(shape), dt))
                gtok = T("g_gtok", (128, NTILE, 12)); ngtok = T("g_ngtok", (128, NTILE, 12)); btok = T("g_btok", (128, NTILE, 12)); nbtok = T("g_nbtok", (128, NTILE, 12))
                cw = T("g_cw", (128, 9, 5)); Abc = T("g_Abc", (12, 1)); dtb = T("g_dtb", (12, 1)); gng = T("g_gng", (128, 64))
                with nc.allow_non_contiguous_dma(reason="tiny vectors"):
                    b.dma("sp", cw[:], gcw_d[L].rearrange("(c p) j -> p c j", p=128), W=["g_cw"])
                    b.dma("sp", dtb[:], gdb_d[L].rearrange("(p o) -> p o", o=1), W=["g_dtb"])
                    b.dma("sp", Abc[:], gal_d[L].rearrange("(p o) -> p o", o=1), W=["g_Abc"])
                b.dma("sp", gng[:], gng_d[L].partition_broadcast(128), W=["g_gng"])
                act(Abc[:], Abc[:], AF.Exp, ["g_Abc"], ["g_Abc"])
                with contextlib.ExitStack() as ps2:
                    T2 = lambda name, shape, dt=F32: ps2.enter_context(nc.sbuf_tensor(name + "_%d" % next(_UID), list(shape), dt))
                    xin = T2("g_xin", (128, 1028)); acc = T2("g_acc", (128, 1024)); cv9 = T2("g_cv9", (128, 9, 1024))
                    tokb = [T2("g_tokb%d" % k, (128, 18, 64)) for k in range(2)]; sqs = T2("g_sqs", (128, 12, 64)); ssq = T2("g_ssq", (128, 12))
                    aT_ = T2("g_aT", (12, NT)); bT_ = T2("g_bT", (12, NT))
                    for (t0, Lb, lo, hi) in [(0, 256, 0, 256)] + [(256 + 1024 * i, 1024, 256, NT) for i in range(4)]:
                        for ch in range(9):
                            conv_seg(xin, acc, slice((CH_GQ + ch) * 128, (CH_GQ + ch + 1) * 128), t0, Lb, lo, hi, cw, None, ch, "g_")
                            act(cv9[:, ch, :Lb], acc[:, :Lb], AF.Silu, ["g_acc"], ["g_cv9"])
                        for j in range(Lb // 128):
                            tile = t0 // 128 + j
                            k = tile % 2
                            for ch in range(9):
                                bank = 1 + ch // 3
                                tr(pball[:, bank, (ch % 3) * 128:(ch % 3 + 1) * 128], cv9[:, ch, j * 128:(j + 1) * 128], ["g_cv9"], [PB(bank)])
                            for qk in range(2):
                                act(sqs[:, qk * 6:(qk + 1) * 6, :].rearrange("p h e -> p (h e)"), pball[:, 1 + qk, 0:384], AF.Square, [], [PB(1 + qk), "g_sqs"])
                            b.op("dve", lambda e: e.tensor_reduce(out=ssq[:], in_=sqs[:], axis=AX.X, op=ALU.add), ["g_sqs"], ["g_ssq"])
                            rstd_from(ssq[:], ssq[:], 1.0, ["g_ssq"], ["g_ssq"])
                            ts(ssq[:, 0:6], ssq[:, 0:6], 0.125, None, ALU.mult, None, ["g_ssq"], ["g_ssq"])
                            for qk in range(2):
                                tt(tokb[k][:, qk * 6:(qk + 1) * 6, :], pball[:, 1 + qk, 0:384].rearrange("p (h e) -> p h e", h=6),
                                   ssq[:, qk * 6:(qk + 1) * 6].unsqueeze(2).broadcast_to([128, 6, 64]), ALU.mult, ["g_ssq"], [PB(1 + qk), "g_tokb%d" % k])
                            cp(tokb[k][:, 12:18, :].rearrange("p h e -> p (h e)"), pball[:, 3, 0:384], [], [PB(3), "g_tokb%d" % k], eng="act")
                            b.dma("sp", GDTOK[tile], tokb[k][:].rearrange("p h e -> p (h e)"), R=["g_tokb%d" % k], W=["GDTOK"])
                    b.dma("sp", aT_[:], PT[CH_SM * 128 + 12:CH_SM * 128 + 24, :], R=["PT"], W=["g_aT"])
                    b.dma("sp", bT_[:], PT[CH_SM * 128 + 24:CH_SM * 128 + 36, :], R=["PT"], W=["g_bT"])
                    act(aT_[:], aT_[:], AF.Exp, ["g_dtb"], ["g_aT"], bias=dtb[:, 0:1])
                    act(aT_[:], aT_[:], AF.Ln, [], ["g_aT"], bias=1.0)
                    ts(aT_[:], aT_[:], Abc[:, 0:1], -1.0, ALU.mult, ALU.mult, ["g_Abc"], ["g_aT"])
                    act(bT_[:], bT_[:], AF.Exp, [], ["g_bT"], scale=-1.0)
                    ts(bT_[:], bT_[:], 1.0, None, ALU.add, None, [], ["g_bT"])
                    recip(bT_[:], bT_[:], [], ["g_bT"])
                    for (srcT, dst, key) in ((aT_, gtok, "g_gtok"), (bT_, btok, "g_btok")):
                        for t in range(NTILE):
                            tr(pball[:, 7, (t % 32) * 12:(t % 32) * 12 + 12], srcT[:, t * 128:(t + 1) * 128], ["g_aT", "g_bT"], [PB(7)], idn=ident[0:12, 0:12])
                            if t == 31 or t == NTILE - 1:
                                tl0 = 0 if t == 31 else 32
                                n = t - tl0 + 1
                                cp(dst[:, tl0:tl0 + n, :], pball[:, 7, 0:n * 12].rearrange("p (t h) -> p t h", h=12), [], [PB(7), key])
                    ts(ngtok[:], gtok[:], -1.0, None, ALU.mult, None, ["g_gtok"], ["g_ngtok"])
                    ts(nbtok[:], btok[:], -1.0, None, ALU.mult, None, ["g_btok"], ["g_nbtok"])
                    b.barrier()
                Oacc = T("g_Oacc", (128, NTILE, 384))
                onesblk = T("g_onesblk", (128, 128)); triblk = T("g_triblk", (128, 2, 128)); trisblk = T("g_trisblk", (128, 2, 128))
                negdt = T("g_negdt", (128, 2, 6, 128)); s01 = T("g_s01", (128, 2, 128)); chsel = T("g_chsel", (128, 2, 64))
                zT = T("g_zT", (128, 3, 128)); sz = T("g_sz", (128, 384)); of = T("g_of", (128, 6, 64)); sq6 = T("g_sq6", (128, 6, 64)); ss6 = T("g_ss6", (128, 6))
                ostg = T("g_ostg", (128, 3, 128))
                BUF = []
                for d in range(2):
                    n = lambda s_: "g%d_%s" % (d, s_)
                    BUF.append(dict(
                        tok=T(n("tok"), (128, 18, 64)), rhs1=T(n("rhs1"), (128, 6, 128)), rhs2=T(n("rhs2"), (128, 6, 128)),
                        E1=T(n("E1"), (128, 6, 128)), E2=T(n("E2"), (128, 6, 128)), qkT=T(n("qkT"), (64, 12, 128)), P=T(n("P"), (128, 6, 128)),
                        NTT=T(n("NT"), (128, 6, 2, 128)), attnT=T(n("attnT"), (128, 6, 128), BF16), vb=T(n("vb"), (128, 6, 64), BF16), kbe=T(n("kbe"), (128, 6, 64), BF16),
                        qd=T(n("qd"), (128, 6, 64)), kdec=T(n("kdec"), (128, 6, 64), BF16), u=T(n("u"), (128, 6, 64)), wT=T(n("wT"), (64, 6, 128), BF16),
                        qdT=T(n("qdT"), (64, 6, 128), BF16), vnew=T(n("vnew"), (128, 6, 64), BF16), S=T(n("S"), (64, 6, 64)), egc=T(n("egc"), (128, 6)),
                        TTb=T(n("TTb"), (128, 6, 128), BF16), Sb=T(n("Sb"), (64, 6, 64), BF16),
                        ekd=T(n("ekd"), (128, 6)), elast=T(n("elast"), (64, 2, 6)), be=T(n("be"), (128, 6)), gc6=T(n("gc6"), (128, 6))))
                b.dma("sp", onesblk[:], cd["onesblk"], W=["g_c"])
                b.dma("sp", triblk[:], cd["triblk"].rearrange("d p f -> p d f"), W=["g_c"])
                b.dma("sp", trisblk[:], cd["trisblk"].rearrange("d p f -> p d f"), W=["g_c"])
                b.dma("sp", negdt[:], cd["gdn_negdt"].rearrange("d p h f -> p d h f"), W=["g_c"])
                b.dma("sp", s01[:], cd["gdn_s01"].rearrange("d p f -> p d f"), W=["g_c"])
                b.dma("sp", chsel[:], cd["chsel"].rearrange("c p f -> p c f"), W=["g_c"])
                b.barrier()
                TRIB = [triblk[:, 0, :], triblk[:, 1, :]]
                step_of = [{}, {}]
                orders = [list(range(NTILE)), [1, 0] + list(range(NTILE - 1, 1, -1))]
                for d in range(2):
                    for i_, t_ in enumerate(orders[d]):
                        step_of[d][t_] = i_

                def tile_gen(d, tile):
                    Bf = BUF[d]
                    K_ = lambda s_: "g%d_%s" % (d, s_)
                    bk = [4 * d + j for j in range(4)]
                    tk = Bf["tok"]; tkk = K_("tok")
                    rhs1, rhs2, E1, E2, qkT, P, NTT, attnT = (Bf[x] for x in ("rhs1", "rhs2", "E1", "E2", "qkT", "P", "NTT", "attnT"))
                    vb, kbe, qd, kdec, u, wT, qdT, vnew, S, egc, ekd, elast, be = (Bf[x] for x in ("vb", "kbe", "qd", "kdec", "u", "wT", "qdT", "vnew", "S", "egc", "ekd", "elast", "be"))
                    hs = slice(6 * d, 6 * d + 6)
                    cs = slice(tile * 128, (tile + 1) * 128)
                    v6 = lambda bank: pball[:, bank, 0:384].rearrange("p (h e) -> p h e", h=6)
                    b.dma("sp", tk[:].rearrange("p h e -> p (h e)"), GDTOK[tile], R=["GDTOK"], W=[tkk])
                    g6 = gtok[:, tile, hs]; ng6 = ngtok[:, tile, hs]; b6 = btok[:, tile, hs]; nb6 = nbtok[:, tile, hs]
                    sb = bk[3]
                    mm(pball[:, sb, 256:262], triblk[:, d, :], g6, True, True, [], [PB(sb)])
                    mm(pball[:, sb, 262:268], trisblk[:, d, :], g6, True, True, [], [PB(sb)])
                    for c in range(2):
                        mm(pball[0:64, sb, 268 + 6 * c:274 + 6 * c], chsel[:, c, :], g6, True, True, [], [PB(sb)])
                    cp(Bf["gc6"][:], pball[:, sb, 256:262], [], [PB(sb), K_("gc6")])
                    act(egc[:], pball[:, sb, 256:262], AF.Exp, [], [PB(sb), K_("egc")])
                    act(ekd[:], pball[:, sb, 262:268], AF.Exp, [], [PB(sb), K_("ekd")])
                    act(elast[:].rearrange("p c h -> p (c h)"), pball[0:64, sb, 268:280], AF.Exp, [], [PB(sb), K_("elast")])
                    cp(rhs1[:], Bf["gc6"][:].unsqueeze(2).broadcast_to([128, 6, 128]), [K_("gc6")], [K_("rhs1")], eng="pool")
                    yield
                    for h in range(6):
                        bank = bk[h // 4]
                        tr(pball[:, bank, (h % 4) * 128:(h % 4 + 1) * 128], rhs1[:, h, :], [K_("rhs1")], [PB(bank)])
                    for h in range(6):
                        bank = bk[h // 4]
                        stt(rhs2[:, h, :], pball[:, bank, (h % 4) * 128:(h % 4 + 1) * 128], Bf["gc6"][:, h:h + 1], negdt[:, d, h, :], ALU.subtract, ALU.add,
                            [K_("gc6")], [PB(bank), K_("rhs2")])
                    act(E1[:].rearrange("p h f -> p (h f)"), rhs2[:].rearrange("p h f -> p (h f)"), AF.Exp, [K_("rhs2")], [K_("E1")])
                    yield
                    for h in range(6):
                        bank = bk[2 + h // 4]
                        tr(pball[:, bank, (h % 4) * 128:(h % 4 + 1) * 128], E1[:, h, :], [K_("E1")], [PB(bank)])
                    tt(E2[:, 0:4, :], pball[:, bk[2], :].rearrange("p (h f) -> p h f", h=4), s01[:, d, :].unsqueeze(1).broadcast_to([128, 4, 128]), ALU.mult,
                       [], [PB(bk[2]), K_("E2")])
                    tt(E2[:, 4:6, :], pball[:, bk[3], 0:256].rearrange("p (h f) -> p h f", h=2), s01[:, d, :].unsqueeze(1).broadcast_to([128, 2, 128]), ALU.mult,
                       [], [PB(bk[3]), K_("E2")])
                    yield
                    for x in range(12):
                        bank = bk[x // 4]
                        tr(pball[0:64, bank, (x % 4) * 128:(x % 4 + 1) * 128], tk[:, x, :], [tkk], [PB(bank)])
                    for q3 in range(3):
                        cp(qkT[:, q3 * 4:(q3 + 1) * 4, :].rearrange("p x t -> p (x t)"), pball[0:64, bk[q3], :], [], [PB(bk[q3]), K_("qkT")],
                           eng="act" if q3 == 1 else "dve")
                    yield
                    for h in range(6):
                        bank = bk[h // 2]; off = (h % 2) * 256
                        mm(pball[:, bank, off:off + 128], qkT[:, 6 + h, :], qkT[:, 6 + h, :], True, True, [K_("qkT")], [PB(bank)])
                        mm(pball[:, bank, off + 128:off + 256], qkT[:, 6 + h, :], qkT[:, h, :], True, True, [K_("qkT")], [PB(bank)])
                    for h in range(6):
                        bank = bk[h // 2]; off = (h % 2) * 256
                        stt(P[:, h, :], pball[:, bank, off:off + 128], nb6[:, h:h + 1], E2[:, h, :], ALU.mult, ALU.mult, [K_("E2")], [PB(bank), K_("P%d" % (h // 3))])
                        tt(attnT[:, h, :], pball[:, bank, off + 128:off + 256], E1[:, h, :], ALU.mult, [K_("E1")], [PB(bank), K_("attnT")])
                    yield
                    for h in range(6):
                        bank = bk[3 if h < 3 else 0]
                        tr(pball[:, bank, (h % 3) * 128:(h % 3 + 1) * 128], P[:, h, :], [K_("P%d" % (h // 3))], [PB(bank)])
                    for grp in range(2):
                        bank = bk[3 if grp == 0 else 0]
                        cp(NTT[:, 3 * grp:3 * grp + 3, 0, :], pball[:, bank, 0:384].rearrange("p (h f) -> p h f", h=3), [], [PB(bank), K_("N%d" % grp)], eng="act")
                    tt(NTT[:, :, 1, :], NTT[:, :, 0, :], ident[:].unsqueeze(1).broadcast_to([128, 6, 128]), ALU.add, [K_("N0"), K_("N1")], [K_("T0"), K_("T1")])
                    tt(vb[:], tk[:, 12:18, :], b6.unsqueeze(2).broadcast_to([128, 6, 64]), ALU.mult, [tkk], [K_("vb")], eng="pool")
                    tt(be[:], b6, egc[:], ALU.mult, [K_("egc")], [K_("be")])
                    tt(kbe[:], tk[:, 6:12, :], be[:].unsqueeze(2).broadcast_to([128, 6, 64]), ALU.mult, [tkk, K_("be")], [K_("kbe")])
                    tt(qd[:], tk[:, 0:6, :], egc[:].unsqueeze(2).broadcast_to([128, 6, 64]), ALU.mult, [tkk, K_("egc")], [K_("qd")], eng="pool")
                    tt(kdec[:], tk[:, 6:12, :], ekd[:].unsqueeze(2).broadcast_to([128, 6, 64]), ALU.mult, [tkk, K_("ekd")], [K_("kdec")])
                    yield
                    for m in range(6):
                        for grp in range(2):
                            hsl = slice(3 * grp, 3 * grp + 3)
                            for hh in range(3):
                                h = 3 * grp + hh
                                bank = bk[hh]
                                if m == 0:
                                    mm(pball[:, bank, 0:128], P[:, h, :], NTT[:, h, 0, :], True, True, [K_("P%d" % grp), K_("N%d" % grp)], [PB(bank)])
                                    mm(pball[:, bank, 256:384], NTT[:, h, 0, :], P[:, h, :], True, True, [K_("P%d" % grp), K_("N%d" % grp)], [PB(bank)])
                                elif m < 5:
                                    mm(pball[:, bank, 0:256], P[:, h, :], NTT[:, h, :, :].rearrange("p s f -> p (s f)"), True, True, [K_("P%d" % grp), K_("N%d" % grp), K_("T%d" % grp)], [PB(bank)])
                                    mm(pball[:, bank, 256:384], NTT[:, h, 0, :], P[:, h, :], True, True, [K_("P%d" % grp), K_("N%d" % grp)], [PB(bank)])
                                else:
                                    mm(pball[:, bank, 128:256], P[:, h, :], NTT[:, h, 1, :], True, True, [K_("P%d" % grp), K_("T%d" % grp)], [PB(bank)])
                            PBS = [PB(bk[0]), PB(bk[1]), PB(bk[2])]
                            b0 = bk[0]
                            if m < 5:
                                cp(NTT[:, hsl, 0, :], pball[:, b0:b0 + 3, 0:128], [], PBS + [K_("N%d" % grp)], eng="act")
                                cp(P[:, hsl, :], pball[:, b0:b0 + 3, 256:384], [], PBS + [K_("P%d" % grp)], eng="act" if grp else "dve")
                            if m > 0:
                                tt(NTT[:, hsl, 1, :], NTT[:, hsl, 1, :], pball[:, b0:b0 + 3, 128:256], ALU.add, [], PBS + [K_("T%d" % grp)])
                            yield
                    cp(Bf["TTb"][:], NTT[:, :, 1, :], [K_("T0"), K_("T1")], [K_("TTb")])
                    for h in range(6):
                        mm(pball[:, bk[3], h * 64:(h + 1) * 64], Bf["TTb"][:, h, :], vb[:, h, :], True, True, [K_("TTb"), K_("vb")], [PB(bk[3])])
                    cp(u[:], v6(bk[3]), [], [PB(bk[3]), K_("u")])
                    for h in range(6):
                        bank = bk[h // 3]
                        mm(pball[0:64, bank, (h % 3) * 128:(h % 3 + 1) * 128], kbe[:, h, :], Bf["TTb"][:, h, :], True, True, [K_("kbe"), K_("TTb")], [PB(bank)])
                    for grp in range(2):
                        cp(wT[:, 3 * grp:3 * grp + 3, :].rearrange("p h t -> p (h t)"), pball[0:64, bk[grp], 0:384], [], [PB(bk[grp]), K_("wT")], eng="act")
                    yield
                    for h in range(6):
                        bank = bk[2 + h // 3]
                        tr(pball[0:64, bank, (h % 3) * 128:(h % 3 + 1) * 128], qd[:, h, :], [K_("qd")], [PB(bank)])
                    for grp in range(2):
                        cp(qdT[:, 3 * grp:3 * grp + 3, :].rearrange("p h t -> p (h t)"), pball[0:64, bk[2 + grp], 0:384], [], [PB(bk[2 + grp]), K_("qdT")])
                    yield
                    for c in ((0, 1) if d == 0 else (1, 0)):
                        pc = slice(64 * c, 64 * c + 64)
                        rgc = 64 * c
                        for h in range(6):
                            mm(pball[pc, bk[0], h * 64:(h + 1) * 64], wT[:, h, pc], Bf["Sb"][:, h, :], True, True, [K_("wT"), K_("Sb")], [PB(bk[0])])
                        tt(vnew[pc], u[pc], pball[pc, bk[0], 0:384].rearrange("p (h e) -> p h e", h=6), ALU.subtract, [K_("u")], [PB(bk[0]), K_("vnew")])
                        yield
                        for h in range(6):
                            mm(pball[pc, bk[1], h * 64:(h + 1) * 64], qdT[:, h, pc], Bf["Sb"][:, h, :], True, False, [K_("qdT"), K_("Sb")], [PB(bk[1])])
                            mm(pball[pc, bk[1], h * 64:(h + 1) * 64], attnT[pc, h, pc], vnew[pc, h, :], False, True, [K_("attnT"), K_("vnew")], [PB(bk[1])], rg=rgc)
                        for h in range(6):
                            mm(pball[0:64, bk[2], h * 64:(h + 1) * 64], kdec[pc, h, :], vnew[pc, h, :], True, True, [K_("kdec"), K_("vnew")], [PB(bk[2])], rg=rgc)
                        tt(S[:], S[:], elast[:, c, :].unsqueeze(2).broadcast_to([64, 6, 64]), ALU.mult, [K_("elast")], [K_("S")])
                        tt(S[:], S[:], pball[0:64, bk[2], 0:384].rearrange("p (h e) -> p h e", h=6), ALU.add, [], [PB(bk[2]), K_("S")])
                        cp(Bf["Sb"][:], S[:], [K_("S")], [K_("Sb")], eng="act")
                        yield
                    first = step_of[d][tile] < step_of[1 - d][tile]
                    if first:
                        cp(Oacc[:, tile, :], pball[:, bk[1], 0:384], [], [PB(bk[1]), ("g_Oacc", tile)], eng="act")
                        return
                    if tile < 2 and not need_ctx:
                        return
                    b.dma("sp", zT[:], PT[CH_GZ * 128:(CH_GZ + 3) * 128, cs].rearrange("(c p) t -> p c t", p=128), R=["PT"], W=["g_zT"])
                    tt(of[:], Oacc[:, tile, :].rearrange("p (h e) -> p h e", h=6), v6(bk[1]), ALU.add, [("g_Oacc", tile)], [PB(bk[1]), "g_of"])
                    act(sq6[:], of[:], AF.Square, ["g_of"], ["g_sq6"])
                    b.op("dve", lambda e: e.tensor_reduce(out=ss6[:], in_=sq6[:], axis=AX.X, op=ALU.add), ["g_sq6"], ["g_ss6"])
                    rstd_from(ss6[:], ss6[:], 1.0 / 64, ["g_ss6"], ["g_ss6"])
                    tt(of[:], of[:], ss6[:].unsqueeze(2).broadcast_to([128, 6, 64]), ALU.mult, ["g_ss6"], ["g_of"])
                    tt(of[:], of[:], gng[:].unsqueeze(1).broadcast_to([128, 6, 64]), ALU.mult, ["g_gng"], ["g_of"], eng="pool")
                    for ch in range(3):
                        tr(pball[:, bk[3], ch * 128:(ch + 1) * 128], zT[:, ch, :], ["g_zT"], [PB(bk[3])])
                    act(sz[:], pball[:, bk[3], 0:384], AF.Silu, [], [PB(bk[3]), "g_sz"])
                    tt(sz[:], sz[:], of[:].rearrange("p h e -> p (h e)"), ALU.mult, ["g_of"], ["g_sz"])
                    for ch in range(3):
                        tr(pball[:, bk[3], ch * 128:(ch + 1) * 128], sz[:, ch * 128:(ch + 1) * 128], ["g_sz"], [PB(bk[3])])
                    cp(ostg[:].rearrange("p c t -> p (c t)"), pball[:, bk[3], 0:384], [], [PB(bk[3]), "g_ostg"], eng="act")
                    b.dma("sp", MIXT[640:1024, cs].rearrange("(c p) t -> p c t", p=128), ostg[:], R=["g_ostg"], W=["MIXT"])

                def dir_gen(d):
                    memset(BUF[d]["S"][:], 0.0, ["g%d_S" % d])
                    memset(BUF[d]["Sb"][:], 0.0, ["g%d_Sb" % d])
                    for tile in orders[d]:
                        yield from tile_gen(d, tile)

                gens = [dir_gen(0), dir_gen(1)]
                while gens:
                    for g_ in list(gens):
                        try:
                            next(g_)
                        except StopIteration:
                            gens.remove(g_)
                b.barrier()

        def mlp_phase(L, need_ctx):
            with contextlib.ExitStack() as ps:
                T = lambda name, shape, dt=F32: ps.enter_context(nc.sbuf_tensor(name + "_%d" % next(_UID), list(shape), dt))
                Wout = T("m5_wout", (128, 8, D), BF16); W1 = T("m5_w1", (128, 8, 4 * D), BF16); W2 = T("m5_w2", (128, 32, D), BF16)
                ht = T("m5_htb", (128, 8, 256)); mxb = T("m5_mxb", (128, 8, 256), BF16); yT = T("m5_yTb", (128, 8, 256)); sq = T("m5_sqb", (128, 8, 256))
                rstd = T("m5_rstdb", (128, 256)); mT = T("m5_aTb", (128, 8, 256), BF16); h1T = T("m5_h1T", (128, 32, 256), BF16)
                rtmp = [T("m5_r%d" % k, (128, 256)) for k in range(2)]
                for c in range(8):
                    b.dma("pool", Wout[:, c, :], wout_d[L, c * 128:(c + 1) * 128, :], W=["m5_wout"])
                    for hf in range(2):
                        b.dma("pool", W1[:, c, hf * 2048:(hf + 1) * 2048], w1_d[L, c * 128:(c + 1) * 128, hf * 2048:(hf + 1) * 2048], W=["m5_w1"])
                for fc in range(32):
                    b.dma("pool", W2[:, fc, :], w2_d[L, fc * 128:(fc + 1) * 128, :], W=["m5_w2"])
                for (t0, TB) in blocks256:
                    if t0 < LCTX and not need_ctx:
                        continue
                    s = 1 if t0 < LCTX else 0
                    b.dma("sp", ht[:], HT[:, t0:t0 + TB].rearrange("(c p) t -> p c t", p=128), R=["HT"], W=[("m5_ht", c) for c in range(8)])
                    b.dma("pool", mxb[:], MIXT[:, t0:t0 + TB].rearrange("(c p) t -> p c t", p=128), R=["MIXT"], W=["m5_mxb"])
                    for oc in range(8):
                        bank = 1 + oc % 4
                        for c in range(8):
                            mm(pball[:, bank, :TB], Wout[:, c, oc * 128:(oc + 1) * 128], mxb[:, c, :], c == 0, c == 7, ["m5_wout", "m5_mxb"], [PB(bank)])
                        cp(yT[:, oc, :], pball[:, bank, :TB], [], [PB(bank), ("m5_yT", oc)], eng="act" if oc % 2 else "dve")
                    post_norm_add(ht, yT, TB, sq, rstd, 2, s, "m5_")
                    norm_mod(ht, TB, sq, rstd, mT, 3, 4, s, "m5_", [("m5_ht", c) for c in range(8)])
                    for fc in range(32):
                        bank = 1 + fc % 4
                        for c in range(8):
                            mm(pball[:, bank, :TB], W1[:, c, fc * 128:(fc + 1) * 128], mT[:, c, :], c == 0, c == 7, ["m5_w1", ("m5_aT", c)], [PB(bank)])
                        k = fc % 2
                        act(rtmp[k][:], pball[:, bank, :TB], AF.Relu, [], [PB(bank), "m5_r%d" % k])
                        tt(h1T[:, fc, :], rtmp[k][:], rtmp[k][:], ALU.mult, ["m5_r%d" % k], [("m5_h1T", fc)], eng="pool" if fc % 4 == 3 else "dve")
                    for oc in range(8):
                        bank = 5 + oc % 3
                        for fc in range(32):
                            mm(pball[:, bank, :TB], W2[:, fc, oc * 128:(oc + 1) * 128], h1T[:, fc, :], fc == 0, fc == 31, ["m5_w2", ("m5_h1T", fc)], [PB(bank)])
                        cp(yT[:, oc, :], pball[:, bank, :TB], [], [PB(bank), ("m5_yT", oc)], eng="act" if oc % 2 else "dve")
                    post_norm_add(ht, yT, TB, sq, rstd, 5, s, "m5_")
                    b.dma("sp", HT[:, t0:t0 + TB].rearrange("(c p) t -> p c t", p=128), ht[:], R=[("m5_ht", c) for c in range(8)], W=["HT"])
                b.barrier()

        for L in range(depth):
            need_ctx = L < depth - 1
            with contextlib.ExitStack() as ps:
                T = lambda name, shape, dt=F32: ps.enter_context(nc.sbuf_tensor(name + "_%d" % next(_UID), list(shape), dt))
                wa = [T("m_wa%d" % k, (128, 8, 1024)) for k in range(2)]
                sc = T("m_sc", (128, 8, 2)); modv = T("m_mod", (128, 48, 2)); bad = T("m_bad", (128, 48)); ng = T("m_ng", (128, 4, 8))
                with nc.allow_non_contiguous_dma(reason="tiny vectors"):
                    for s_ in range(2):
                        b.dma("sp", sc[:, :, s_], cv_d[s_].rearrange("(c p) -> p c", p=128), W=["m_sc"])
                    b.dma("sp", bad[:], bada_d[L].rearrange("(c p) -> p c", p=128), W=["m_bad"])
                    for k_ in range(4):
                        b.dma("sp", ng[:, k_, :], ng_d[L, k_].rearrange("(c p) -> p c", p=128), W=["m_ng"])
                act(sc[:], sc[:], AF.Silu, ["m_sc"], ["m_sc"])
                for blk in range(6):
                    k = blk % 2
                    for c in range(8):
                        b.dma("sp", wa[k][:, c, :], wada_d[L, c * 128:(c + 1) * 128, blk * 1024:(blk + 1) * 1024], W=[("m_wa", k, c)])
                    for fcl in range(8):
                        fc = blk * 8 + fcl
                        for c in range(8):
                            mm(pball[:, 1, fc * 2:fc * 2 + 2], wa[k][:, c, fcl * 128:(fcl + 1) * 128], sc[:, c, :], c == 0, c == 7,
                               [("m_wa", k, c), "m_sc"], [PB(1)])
                    for c in range(8):
                        b.res.setdefault(("m_wa", k, c), {"w": None, "r": []})
                tt(modv[:], pball[:, 1, 0:96].rearrange("p (f s) -> p f s", s=2), bad[:].unsqueeze(2).broadcast_to([128, 48, 2]),
                   ALU.add, ["m_bad"], [PB(1), "m_mod"])
                g_bc = lambda k: ng[:, k, :].unsqueeze(2).broadcast_to([128, 8, 2])
                stt(MV[:, 0], modv[:, 8:16, :], 1.0, g_bc(0), ALU.add, ALU.mult, ["m_mod", "m_ng"], ["MV"])
                cp(MV[:, 1], modv[:, 0:8, :], ["m_mod"], ["MV"])
                tt(MV[:, 2], modv[:, 16:24, :], g_bc(1), ALU.mult, ["m_mod", "m_ng"], ["MV"])
                stt(MV[:, 3], modv[:, 32:40, :], 1.0, g_bc(2), ALU.add, ALU.mult, ["m_mod", "m_ng"], ["MV"])
                cp(MV[:, 4], modv[:, 24:32, :], ["m_mod"], ["MV"])
                tt(MV[:, 5], modv[:, 40:48, :], g_bc(3), ALU.mult, ["m_mod", "m_ng"], ["MV"])
                b.barrier()

            with contextlib.ExitStack() as ps:
                T = lambda name, shape, dt=F32: ps.enter_context(nc.sbuf_tensor(name + "_%d" % next(_UID), list(shape), dt))
                Win = T("p1_win", (128, 8, NCH * 128), BF16)
                ht = [T("p1_ht%d" % k, (128, 8, 512)) for k in range(2)]
                sq = [T("p1_sq%d" % k, (128, 8, 512)) for k in range(2)]; rstd = [T("p1_rstd%d" % k, (128, 512)) for k in range(2)]
                aT = [T("p1_aT%d" % k, (128, 8, 512), BF16) for k in range(2)]
                stg = [T("p1_st%d" % k, (128, 512)) for k in range(4)]
                for c in range(8):
                    for hf in range(2):
                        b.dma("pool", Win[:, c, hf * 1920:(hf + 1) * 1920], win_d[L, c * 128:(c + 1) * 128, hf * 1920:(hf + 1) * 1920], W=["p1_win"])
                def p1_norm(bi):
                    t0, TB = blocks512[bi]
                    k = bi % 2
                    s = 1 if t0 < LCTX else 0
                    b.dma("sp", ht[k][:, :, :TB], HT[:, t0:t0 + TB].rearrange("(c p) t -> p c t", p=128), R=["HT"], W=["p1_ht%d" % k])
                    norm_mod(ht[k], TB, sq[k], rstd[k], aT[k], 0, 1, s, "p1_%d" % k, ["p1_ht%d" % k])

                p1_norm(0)
                for bi, (t0, TB) in enumerate(blocks512):
                    k = bi % 2
                    if bi + 1 < len(blocks512):
                        p1_norm(bi + 1)
                    for fc in range(NCH):
                        bank = 1 + fc % 6
                        for c in range(8):
                            mm(pball[:, bank, :TB], Win[:, c, fc * 128:(fc + 1) * 128], aT[k][:, c, :TB], c == 0, c == 7,
                               ["p1_win", ("p1_%daT" % k, c)], [PB(bank)])
                        sk = fc % 4
                        cp(stg[sk][:, :TB], pball[:, bank, :TB], [], [PB(bank), "p1_st%d" % sk], eng="act" if fc % 2 else "dve")
                        b.dma("sp", PT[fc * 128:(fc + 1) * 128, t0:t0 + TB], stg[sk][:, :TB], R=["p1_st%d" % sk], W=["PT"])
                b.barrier()

            with contextlib.ExitStack() as ps:
                T = lambda name, shape, dt=F32: ps.enter_context(nc.sbuf_tensor(name + "_%d" % next(_UID), list(shape), dt))
                rcos = T("a_cos", (128, LLAT)); rsin = T("a_sin", (128, LLAT))
                qrT = T("a_qrT", (128, 2, NT), BF16); krT = T("a_krT", (128, NT), BF16)
                vtok = T("a_vtok", (128, NTILE, 2, 65), BF16)
                pin = T("a_pin", (128, 7, 512)); t1 = T("a_t1", (128, 512)); t2 = T("a_t2", (128, 512))
                identb = T("a_idb", (128, 128), BF16); negm = T("a_negm", (128, 2, 2, 128)); negmb = T("a_negmb", (128, 2, 2, 128), BF16)
                esink = T("a_esink", (128, 4))
                pT = T("a_pT", (128, 2, 5, 2, 128), BF16)
                o4 = T("a_o4", (128, 4, 64)); den = T("a_den", (128, 4)); ostg = T("a_ostg", (128, 2, 128))
                b.dma("sp", rcos[:], cd["rcos"], W=["a_cos"]); b.dma("sp", rsin[:], cd["rsin"], W=["a_sin"])
                b.dma("sp", negm[:], cd["att_neg"].rearrange("m p h q -> p m h q"), W=["a_negm"])
                cp(negmb[:], negm[:], ["a_negm"], ["a_negmb"]); cp(identb[:], ident[:], ["ident"], ["a_idb"])
                b.dma("sp", esink[:], sink_d[L].partition_broadcast(128), W=["a_esink"])
                act(esink[:], esink[:], AF.Exp, ["a_esink"], ["a_esink"])
                memset(vtok[:, :, :, 64:65], 1.0, ["a_vtok"])
                for (t0, TB) in blocks512:
                    b.dma("sp", pin[:, :, :TB], PT[0:7 * 128, t0:t0 + TB].rearrange("(c p) t -> p c t", p=128), R=["PT"], W=["a_pin"])
                    if t0 < LCTX:
                        for hh in range(2):
                            cp(qrT[:, hh, t0:t0 + TB], pin[:, CH_QA + hh, :TB], ["a_pin"], ["a_qrT"])
                        cp(krT[:, t0:t0 + TB], pin[:, CH_K, :TB], ["a_pin"], ["a_krT"], eng="act")
                    else:
                        l0 = t0 - LCTX
                        for (src, srcp, dst, key) in ((CH_QA, CH_QAP, qrT[:, 0, t0:t0 + TB], "a_qrT"), (CH_QB, CH_QBP, qrT[:, 1, t0:t0 + TB], "a_qrT"),
                                                      (CH_K, CH_KP, krT[:, t0:t0 + TB], "a_krT")):
                            tt(t1[:, :TB], pin[:, src, :TB], rcos[:, l0:l0 + TB], ALU.mult, ["a_pin", "a_cos"], ["a_t1"])
                            tt(t2[:, :TB], pin[:, srcp, :TB], rsin[:, l0:l0 + TB], ALU.mult, ["a_pin", "a_sin"], ["a_t2"], eng="pool")
                            tt(dst, t1[:, :TB], t2[:, :TB], ALU.add, ["a_t1", "a_t2"], [key])
                    for j in range(TB // 128):
                        tile = t0 // 128 + j
                        tr(pball[:, 7, 0:128], pin[:, CH_V, j * 128:(j + 1) * 128], ["a_pin"], [PB(7)])
                        cp(vtok[:, tile, :, 0:64], pball[:, 7, 0:128].rearrange("p (k d) -> p k d", k=2), [], [PB(7), "a_vtok"])
                for qt in range(NTILE):
                    if qt < 2 and not need_ctx:
                        continue
                    if qt < 2:
                        keys = [(0, None), (1, None)]
                    else:
                        keys = [(0, None), (1, None), (qt, None)]
                        if qt > 2:
                            keys.append((qt - 1, 0))
                        if qt < NTILE - 1:
                            keys.append((qt + 1, 1))
                    nk = len(keys)
                    qs = slice(qt * 128, (qt + 1) * 128)
                    for kv in range(2):
                        rg = 64 * kv
                        pr = slice(64 * kv, 64 * kv + 64)
                        for s_, (kt, mk) in enumerate(keys):
                            bank = kv * 3 + s_ // 2
                            off = (s_ % 2) * 256
                            o_ap = pball[:, bank, off:off + 256]
                            mm(o_ap, krT[pr, kt * 128:(kt + 1) * 128], qrT[pr, :, qs], True, mk is None, ["a_krT", "a_qrT"], [PB(bank)], rg=rg)
                            if mk is not None:
                                mm(o_ap, identb[:], negmb[:, mk].rearrange("p h q -> p (h q)"), False, True, ["a_idb", "a_negmb"], [PB(bank)])
                        for bk in range((nk + 1) // 2):
                            ns = min(2, nk - 2 * bk)
                            act(pT[:, kv, 2 * bk:2 * bk + ns].rearrange("p s h q -> p (s h q)"), pball[:, kv * 3 + bk, 0:ns * 256], AF.Exp,
                                [], [PB(kv * 3 + bk), ("a_pT", kv)], scale=0.125)
                        for hh in range(2):
                            h = 2 * kv + hh
                            for s_, (kt, mk) in enumerate(keys):
                                mm(pball[:, 6, h * 65:(h + 1) * 65], pT[:, kv, s_, hh, :], vtok[:, kt, kv, :], s_ == 0, s_ == nk - 1,
                                   [("a_pT", kv), "a_vtok"], [PB(6)])
                    o65 = pball[:, 6, 0:260].rearrange("p (h e) -> p h e", h=4)
                    tt(den[:], o65[:, :, 64], esink[:], ALU.add, ["a_esink"], [PB(6), "a_den"])
                    recip(den[:], den[:], ["a_den"], ["a_den"])
                    tt(o4[:], o65[:, :, 0:64], den[:].unsqueeze(2).broadcast_to([128, 4, 64]), ALU.mult, ["a_den"], [PB(6), "a_o4"])
                    o4f = o4[:].rearrange("p h d -> p (h d)")
                    for cc in range(2):
                        tr(pball[:, 7, cc * 128:(cc + 1) * 128], o4f[:, cc * 128:(cc + 1) * 128], ["a_o4"], [PB(7)])
                    cp(ostg[:], pball[:, 7, 0:256].rearrange("p (c t) -> p c t", c=2), [], [PB(7), "a_ostg"], eng="act")
                    b.dma("sp", MIXT[0:256, qs].rearrange("(c p) t -> p c t", p=128), ostg[:], R=["a_ostg"], W=["MIXT"])
                b.barrier()
            if "ssd" not in skip:
                ssd_phase(L, need_ctx)
            if "gdn" not in skip:
                gdn_phase(L, need_ctx)
            if "mlp" not in skip:
                mlp_phase(L, need_ctx)

        with contextlib.ExitStack() as ps:
            T = lambda name, shape, dt=F32: ps.enter_context(nc.sbuf_tensor(name + "_%d" % next(_UID), list(shape), dt))
            hin = [T("f_h%d" % k, (128, 8, 128)) for k in range(2)]
            xo = [T("f_o%d" % k, (128, D)) for k in range(2)]
            for t in range(2, NTILE):
                k = t % 2
                b.dma("sp", hin[k][:], HT[:, t * 128:(t + 1) * 128].rearrange("(c p) t -> p c t", p=128), R=["HT"], W=["f_h%d" % k])
                for c in range(8):
                    bank = (t % 2) * 2 + c // 4
                    tr(pball[:, bank, (c % 4) * 128:(c % 4 + 1) * 128], hin[k][:, c, :], ["f_h%d" % k], [PB(bank)])
                for hf in range(2):
                    bank = (t % 2) * 2 + hf
                    cp(xo[k][:, hf * 512:(hf + 1) * 512], pball[:, bank, :], [], [PB(bank), "f_o%d" % k], eng="act" if hf else "dve")
                b.dma("sp", out_d[(t - 2) * 128:(t - 1) * 128, :], xo[k][:], R=["f_o%d" % k], W=["out"])
            b.barrier()
    return nc


_CACHE = {}


def kernel(**inputs):
    inp = {k: np.asarray(v) for k, v in inputs.items()}
    depth = 4
    if "nc" not in _CACHE:
        _CACHE["nc"] = build(depth)
    nc = _CACHE["nc"]
    perm = _win_perm()
    w_in = inp["w_in"]
    w_in_p = np.zeros((w_in.shape[0], D, NCH * 128), np.float32)
    ok = perm >= 0
    w_in_p[:, :, ok] = w_in[:, :, perm[ok]]
    shared = {
        "w_ada": inp["w_ada"], "b_ada": inp["b_ada"], "norm_g": inp["norm_g"], "w_in": w_in_p, "w_out": inp["w_out"],
        "attn_sink": inp["attn_sink"], "ssd_conv_w": inp["ssd_conv_w"], "ssd_conv_b": inp["ssd_conv_b"],
        "ssd_A_log": inp["ssd_A_log"].reshape(4, 12), "ssd_dt_bias": inp["ssd_dt_bias"].reshape(4, 12), "ssd_D": inp["ssd_D"],
        "ssd_norm_g": inp["ssd_norm_g"], "dn_conv_w": inp["dn_conv_w"], "dn_A_log": inp["dn_A_log"].reshape(4, 12),
        "dn_dt_bias": inp["dn_dt_bias"].reshape(4, 12), "dn_norm_g": inp["dn_norm_g"], "w_mlp1": inp["w_mlp1"], "w_mlp2": inp["w_mlp2"],
    }
    shared = {k: np.ascontiguousarray(v, dtype=np.float32) for k, v in shared.items()}
    for k, v in _consts().items():
        shared["c_" + k] = np.ascontiguousarray(v, dtype=np.float32)
    in_maps = []
    for c in range(8):
        m = dict(shared)
        m["x"] = np.ascontiguousarray(inp["x"][c], dtype=np.float32)
        m["ctx"] = np.ascontiguousarray(inp["ctx"][c], dtype=np.float32)
        m["cvec"] = np.ascontiguousarray(np.stack([inp["c"][c], inp["c_ctx"]]), dtype=np.float32)
        in_maps.append(m)
    res = run_bass_kernel_spmd(nc, in_maps, core_ids=list(range(8)))
    return np.stack([np.asarray(r["out"], dtype=np.float32) for r in res.results], axis=0)
```
